# Optimizing a Trainium2 kernel written in Bass

```python
import math
import jax, jax.numpy as jnp
from jax import lax
import numpy as np

D_MODEL = 1024
BATCH = 16
SEQ = 2048
DEPTH = 2
DEC_BATCH = 8
DEC_SEQ = 8192
PAST_LEN = 128

SSD_EXPAND = 2
SSD_D_INNER = SSD_EXPAND * D_MODEL
SSD_HEAD_DIM = 64
SSD_N_HEADS = SSD_D_INNER // SSD_HEAD_DIM
SSD_N_GROUPS = 8
SSD_D_STATE = 128
SSD_CONV_W = 7
SSD_CHUNK = 128
SSD_GN = SSD_N_GROUPS * SSD_D_STATE
SSD_CONV_DIM = SSD_D_INNER + 2 * SSD_GN
SSD_IN_DIM = SSD_D_INNER + SSD_CONV_DIM + 2 * SSD_N_HEADS

GLA_N_HEADS = 4
GLA_KEY_DIM = D_MODEL // 2
GLA_VALUE_DIM = D_MODEL
GLA_HEAD_K = GLA_KEY_DIM // GLA_N_HEADS
GLA_HEAD_V = GLA_VALUE_DIM // GLA_N_HEADS
GLA_GATE_RANK = 16
GLA_GATE_NORMALIZER = 16.0
GLA_CHUNK = 64
GLA_IN_DIM = 2 * GLA_KEY_DIM + 2 * GLA_VALUE_DIM + 2 * GLA_GATE_RANK

D_FF = 4 * D_MODEL

N_SSD_LAYERS = (DEPTH + 1) // 2
N_GLA_LAYERS = DEPTH // 2
EPS = 1e-5

kernel_name = "hybrid_bidir_ssd_gla_encoder"


def rmsnorm(x, g):
    xf = x.astype(jnp.float32)
    y = xf * lax.rsqrt(jnp.mean(xf * xf, axis=-1, keepdims=True) + EPS)
    return (y * g.astype(jnp.float32)).astype(x.dtype)


def flip_seq(t):
    return jnp.flip(t, axis=1)


def centred_depthwise_conv(u, w, b):
    pad = (SSD_CONV_W - 1) // 2
    y = lax.conv_general_dilated(
        u, w[:, None, :].astype(u.dtype), window_strides=(1,), padding=[(pad, pad)],
        dimension_numbers=("NWC", "WIO", "NWC"), feature_group_count=u.shape[-1])
    return y + b.astype(u.dtype)


def ssd_scan(x, dt, a, bm, cm):
    bsz, l, h, p = x.shape
    g, n = bm.shape[-2:]
    r = h // g
    c = l // SSD_CHUNK
    q = SSD_CHUNK
    xc = (x * dt[..., None]).reshape(bsz, c, q, g, r, p)
    log_a = (dt * a).reshape(bsz, c, q, g, r)
    a_cs = jnp.cumsum(log_a, axis=2)
    bc = bm.reshape(bsz, c, q, g, n)
    cc = cm.reshape(bsz, c, q, g, n)
    seg = a_cs[:, :, :, None] - a_cs[:, :, None, :]
    mask = jnp.tril(jnp.ones((q, q), dtype=bool))[None, None, :, :, None, None]
    decay = jnp.exp(jnp.where(mask, seg, -jnp.inf))
    scores = jnp.einsum("bclgn,bcsgn->bclsg", cc, bc)
    y_diag = jnp.einsum("bclsg,bclsgr,bcsgrp->bclgrp", scores, decay, xc)
    decay_to_end = jnp.exp(a_cs[:, :, -1:] - a_cs)
    states = jnp.einsum("bcsgn,bcsgr,bcsgrp->bcgrpn", bc, decay_to_end, xc)
    chunk_decay = jnp.exp(a_cs[:, :, -1])

    def step(carry, inp):
        s, d = inp
        return carry * d[..., None, None] + s, carry

    init = jnp.zeros((bsz, g, r, p, n), jnp.float32)
    _, prev = lax.scan(step, init, (jnp.moveaxis(states, 1, 0), jnp.moveaxis(chunk_decay, 1, 0)))
    prev = jnp.moveaxis(prev, 0, 1)
    y_off = jnp.einsum("bclgn,bcgrpn,bclgr->bclgrp", cc, prev, jnp.exp(a_cs))
    return (y_diag + y_off).reshape(bsz, l, h, p)


def ssd_mixer(u, w_in, conv_w, conv_b, dt_bias_f, dt_bias_b, a_log_f, a_log_b, d_skip, norm_g, w_out):
    bsz, l, _ = u.shape
    f32 = jnp.float32
    proj = u @ w_in
    z = proj[..., :SSD_D_INNER]
    xbc = proj[..., SSD_D_INNER:SSD_D_INNER + SSD_CONV_DIM]
    dt_raw = proj[..., SSD_D_INNER + SSD_CONV_DIM:].astype(f32)
    xbc = jax.nn.silu(centred_depthwise_conv(xbc, conv_w, conv_b)).astype(f32)
    xs = xbc[..., :SSD_D_INNER].reshape(bsz, l, SSD_N_HEADS, SSD_HEAD_DIM)
    bm = xbc[..., SSD_D_INNER:SSD_D_INNER + SSD_GN].reshape(bsz, l, SSD_N_GROUPS, SSD_D_STATE)
    cm = xbc[..., SSD_D_INNER + SSD_GN:].reshape(bsz, l, SSD_N_GROUPS, SSD_D_STATE)
    dt_f = jax.nn.softplus(dt_raw[..., :SSD_N_HEADS] + dt_bias_f.astype(f32))
    dt_b = jax.nn.softplus(dt_raw[..., SSD_N_HEADS:] + dt_bias_b.astype(f32))
    a_f = -jnp.exp(a_log_f.astype(f32))
    a_b = -jnp.exp(a_log_b.astype(f32))
    y_f = ssd_scan(xs, dt_f, a_f, bm, cm)
    y_b = flip_seq(ssd_scan(flip_seq(xs), flip_seq(dt_b), a_b, flip_seq(bm), flip_seq(cm)))
    y = y_f + y_b + d_skip.astype(f32)[:, None] * xs
    y = y.reshape(bsz, l, SSD_D_INNER) * jax.nn.silu(z.astype(f32))
    y = rmsnorm(y, norm_g).astype(u.dtype)
    return y @ w_out


def gla_scan(q, k, v, lg):
    bsz, l, h, dk = q.shape
    dv = v.shape[-1]
    c = l // GLA_CHUNK
    Q = GLA_CHUNK
    q = q.reshape(bsz, c, Q, h, dk)
    k = k.reshape(bsz, c, Q, h, dk)
    v = v.reshape(bsz, c, Q, h, dv)
    bcs = jnp.cumsum(lg.reshape(bsz, c, Q, h, dk), axis=2)
    ref = bcs[:, :, Q // 2:Q // 2 + 1]
    q_in = q * jnp.exp(bcs - ref)
    k_in = k * jnp.exp(ref - bcs)
    attn = jnp.einsum("bclhd,bcshd->bchls", q_in, k_in)
    mask = jnp.tril(jnp.ones((Q, Q), dtype=bool))
    attn = jnp.where(mask, attn, 0.0)
    o_intra = jnp.einsum("bchls,bcshv->bclhv", attn, v)
    b_last = bcs[:, :, -1:]
    k_end = k * jnp.exp(b_last - bcs)
    chunk_states = jnp.einsum("bcshd,bcshv->bchdv", k_end, v)
    chunk_decay = jnp.exp(b_last[:, :, 0])

    def step(carry, inp):
        s, d = inp
        return carry * d[..., None] + s, carry

    init = jnp.zeros((bsz, h, dk, dv), jnp.float32)
    _, prev = lax.scan(step, init, (jnp.moveaxis(chunk_states, 1, 0), jnp.moveaxis(chunk_decay, 1, 0)))
    prev = jnp.moveaxis(prev, 0, 1)
    o_inter = jnp.einsum("bclhd,bchdv->bclhv", q * jnp.exp(bcs), prev)
    return (o_intra + o_inter).reshape(bsz, l, h, dv)


def gla_mixer(u, w_in, gate_up_f, gate_bias_f, gate_up_b, gate_bias_b, norm_g, w_out):
    bsz, l, _ = u.shape
    f32 = jnp.float32
    proj = u @ w_in
    o1 = GLA_KEY_DIM
    o2 = o1 + GLA_KEY_DIM
    o3 = o2 + GLA_VALUE_DIM
    o4 = o3 + GLA_VALUE_DIM
    o5 = o4 + GLA_GATE_RANK
    q = proj[..., :o1].astype(f32).reshape(bsz, l, GLA_N_HEADS, GLA_HEAD_K) * (GLA_HEAD_K ** -0.5)
    k = proj[..., o1:o2].astype(f32).reshape(bsz, l, GLA_N_HEADS, GLA_HEAD_K)
    v = proj[..., o2:o3].astype(f32).reshape(bsz, l, GLA_N_HEADS, GLA_HEAD_V)
    g_out = proj[..., o3:o4].astype(f32)
    lr_f = proj[..., o4:o5]
    lr_b = proj[..., o5:]
    lg_f = (jax.nn.log_sigmoid((lr_f @ gate_up_f + gate_bias_f).astype(f32)) / GLA_GATE_NORMALIZER
            ).reshape(bsz, l, GLA_N_HEADS, GLA_HEAD_K)
    lg_b = (jax.nn.log_sigmoid((lr_b @ gate_up_b + gate_bias_b).astype(f32)) / GLA_GATE_NORMALIZER
            ).reshape(bsz, l, GLA_N_HEADS, GLA_HEAD_K)
    o_f = gla_scan(q, k, v, lg_f)
    o_b = flip_seq(gla_scan(flip_seq(q), flip_seq(k), flip_seq(v), flip_seq(lg_b)))
    o = rmsnorm(o_f + o_b, norm_g)
    o = o.reshape(bsz, l, GLA_VALUE_DIM) * jax.nn.silu(g_out)
    return o.astype(u.dtype) @ w_out


def sqrelu_mlp(u, w_up, w_down):
    h = jnp.square(jax.nn.relu(u @ w_up))
    return h @ w_down


def trunk(x, norm_mix_g, norm_mlp_g, norm_final_g,
          ssd_w_in, ssd_conv_w, ssd_conv_b, ssd_dt_bias_f, ssd_dt_bias_b,
          ssd_a_log_f, ssd_a_log_b, ssd_d, ssd_norm_g, ssd_w_out,
          gla_w_in, gla_gate_up_f, gla_gate_bias_f, gla_gate_up_b, gla_gate_bias_b,
          gla_norm_g, gla_w_out, mlp_w_up, mlp_w_down):
    for i in range(DEPTH):
        u = rmsnorm(x, norm_mix_g[i])
        j = i // 2
        if i % 2 == 0:
            x = x + ssd_mixer(u, ssd_w_in[j], ssd_conv_w[j], ssd_conv_b[j], ssd_dt_bias_f[j],
                              ssd_dt_bias_b[j], ssd_a_log_f[j], ssd_a_log_b[j], ssd_d[j],
                              ssd_norm_g[j], ssd_w_out[j])
        else:
            x = x + gla_mixer(u, gla_w_in[j], gla_gate_up_f[j], gla_gate_bias_f[j],
                              gla_gate_up_b[j], gla_gate_bias_b[j], gla_norm_g[j], gla_w_out[j])
        x = x + sqrelu_mlp(rmsnorm(x, norm_mlp_g[i]), mlp_w_up[i], mlp_w_down[i])
    return rmsnorm(x, norm_final_g)


def _dt_bias(key, shape):
    lo, hi = math.log(1e-3), math.log(1e-1)
    dt = jnp.exp(jax.random.uniform(key, shape, jnp.float32) * (hi - lo) + lo)
    dt = jnp.maximum(dt, 1e-4)
    return dt + jnp.log(-jnp.expm1(-dt))


def setup_inputs(seed: int = 0) -> dict:
    key = jax.random.key(seed)
    ks = jax.random.split(key, 32)
    f32 = jnp.float32
    nrm = lambda k, s, scale: jax.random.normal(k, s, f32) * scale
    NS, NG = N_SSD_LAYERS, N_GLA_LAYERS
    return {
        "x_prompt": nrm(ks[0], (BATCH, SEQ, D_MODEL), 1.0),
        "x_sample": nrm(ks[1], (DEC_BATCH, DEC_SEQ, D_MODEL), 1.0),
        "norm_mix_g": 1.0 + nrm(ks[2], (DEPTH, D_MODEL), 0.02),
        "norm_mlp_g": 1.0 + nrm(ks[3], (DEPTH, D_MODEL), 0.02),
        "norm_final_g": 1.0 + nrm(ks[4], (D_MODEL,), 0.02),
        "ssd_w_in": nrm(ks[5], (NS, D_MODEL, SSD_IN_DIM), D_MODEL ** -0.5),
        "ssd_conv_w": nrm(ks[6], (NS, SSD_CONV_W, SSD_CONV_DIM), SSD_CONV_W ** -0.5),
        "ssd_conv_b": nrm(ks[7], (NS, SSD_CONV_DIM), 0.02),
        "ssd_dt_bias_f": _dt_bias(ks[8], (NS, SSD_N_HEADS)),
        "ssd_dt_bias_b": _dt_bias(ks[9], (NS, SSD_N_HEADS)),
        "ssd_a_log_f": jnp.log(jax.random.uniform(ks[10], (NS, SSD_N_HEADS), f32, 1.0, 16.0)),
        "ssd_a_log_b": jnp.log(jax.random.uniform(ks[11], (NS, SSD_N_HEADS), f32, 1.0, 16.0)),
        "ssd_d": 1.0 + nrm(ks[12], (NS, SSD_N_HEADS), 0.1),
        "ssd_norm_g": 1.0 + nrm(ks[13], (NS, SSD_D_INNER), 0.02),
        "ssd_w_out": nrm(ks[14], (NS, SSD_D_INNER, D_MODEL), SSD_D_INNER ** -0.5),
        "gla_w_in": nrm(ks[15], (NG, D_MODEL, GLA_IN_DIM), D_MODEL ** -0.5),
        "gla_gate_up_f": nrm(ks[16], (NG, GLA_GATE_RANK, GLA_KEY_DIM), GLA_GATE_RANK ** -0.5),
        "gla_gate_bias_f": nrm(ks[17], (NG, GLA_KEY_DIM), 0.1),
        "gla_gate_up_b": nrm(ks[18], (NG, GLA_GATE_RANK, GLA_KEY_DIM), GLA_GATE_RANK ** -0.5),
        "gla_gate_bias_b": nrm(ks[19], (NG, GLA_KEY_DIM), 0.1),
        "gla_norm_g": 1.0 + nrm(ks[20], (NG, GLA_HEAD_V), 0.02),
        "gla_w_out": nrm(ks[21], (NG, GLA_VALUE_DIM, D_MODEL), GLA_VALUE_DIM ** -0.5),
        "mlp_w_up": nrm(ks[22], (DEPTH, D_MODEL, D_FF), D_MODEL ** -0.5),
        "mlp_w_down": nrm(ks[23], (DEPTH, D_FF, D_MODEL), D_FF ** -0.5),
    }


def reference(x_prompt, x_sample, norm_mix_g, norm_mlp_g, norm_final_g,
              ssd_w_in, ssd_conv_w, ssd_conv_b, ssd_dt_bias_f, ssd_dt_bias_b,
              ssd_a_log_f, ssd_a_log_b, ssd_d, ssd_norm_g, ssd_w_out,
              gla_w_in, gla_gate_up_f, gla_gate_bias_f, gla_gate_up_b, gla_gate_bias_b,
              gla_norm_g, gla_w_out, mlp_w_up, mlp_w_down):
    y_prompt = trunk(x_prompt, norm_mix_g, norm_mlp_g, norm_final_g,
                     ssd_w_in, ssd_conv_w, ssd_conv_b, ssd_dt_bias_f, ssd_dt_bias_b,
                     ssd_a_log_f, ssd_a_log_b, ssd_d, ssd_norm_g, ssd_w_out,
                     gla_w_in, gla_gate_up_f, gla_gate_bias_f, gla_gate_up_b, gla_gate_bias_b,
                     gla_norm_g, gla_w_out, mlp_w_up, mlp_w_down)
    y_sample = trunk(x_sample, norm_mix_g, norm_mlp_g, norm_final_g,
                     ssd_w_in, ssd_conv_w, ssd_conv_b, ssd_dt_bias_f, ssd_dt_bias_b,
                     ssd_a_log_f, ssd_a_log_b, ssd_d, ssd_norm_g, ssd_w_out,
                     gla_w_in, gla_gate_up_f, gla_gate_bias_f, gla_gate_up_b, gla_gate_bias_b,
                     gla_norm_g, gla_w_out, mlp_w_up, mlp_w_down)
    return (y_prompt, y_sample)
```

```python
import numpy as np
import concourse.bass as bass
import concourse.mybir as mybir
from concourse.bass_utils import run_bass_kernel_spmd

F32 = mybir.dt.float32
BF16 = mybir.dt.bfloat16
AF = mybir.ActivationFunctionType
ALU = mybir.AluOpType
AX = mybir.AxisListType

DM = 1024
DI = 2048
NH = 32
NGRP = 8
SSD_IN = 6208
GLA_IN = 3104
DFF = 4096
EPS = 1e-5
SEQ_LENS = (2048, 2048, 8192)


class T:
    def __init__(self, h, name):
        self.h = h
        self.name = name
        self.w = None
        self.r = []

    def __getitem__(self, k):
        return self.h[k]


class Sched:
    ENG = ["sync", "tensor", "vector", "scalar", "gpsimd"]

    def __init__(self, nc):
        self.nc = nc
        self.prog = {e: [] for e in self.ENG}
        self.cnt = {}
        self.sems = {}
        self.waited = {e: {} for e in self.ENG}
        self.sem_stack = []
        self.tile_stack = []
        self.ninstr = 0
        self.uid = 0

    def sem(self, key):
        if key not in self.sems:
            cm = self.nc.semaphore("s_" + key)
            self.sems[key] = cm.__enter__()
            self.sem_stack.append(cm)
            self.cnt[key] = 0
        return self.sems[key]

    def tile(self, shape, dt, name, psum=False):
        self.uid += 1
        nm = f"{name}_{self.uid}"
        if psum:
            cm = self.nc.psum_tensor(nm, shape, dt)
        else:
            cm = self.nc.sbuf_tensor(nm, shape, dt)
        h = cm.__enter__()
        self.tile_stack.append(cm)
        return T(h, name)

    def mark(self):
        return len(self.tile_stack)

    def release(self, mark):
        while len(self.tile_stack) > mark:
            self.tile_stack.pop().__exit__(None, None, None)

    def op(self, eng, fn, reads=(), writes=(), dma=False, semkey=None):
        waits = {}

        def need(dep):
            if dep is None:
                return
            k, v = dep
            if k == "tensor" and eng == "tensor":
                return
            if self.waited[eng].get(k, 0) >= v:
                return
            waits[k] = max(waits.get(k, 0), v)

        for t in reads:
            need(t.w)
        for t in writes:
            need(t.w)
            for d in t.r:
                need(d)
        for k, v in waits.items():
            self.waited[eng][k] = v
        if dma:
            key, inc = semkey, 16
        else:
            key, inc = eng, 1
        self.sem(key)
        self.cnt[key] += inc
        me = (key, self.cnt[key])
        for t in reads:
            t.r.append(me)
            if len(t.r) > 64:
                mx = {}
                for k, v in t.r:
                    mx[k] = max(mx.get(k, 0), v)
                t.r = list(mx.items())
        for t in writes:
            t.w = me
            t.r = []
        self.prog[eng].append((fn, list(waits.items()), key, inc))
        self.ninstr += 1
        return me

    def wait_all(self, eng, keys=None):
        waits = []
        for k in (keys if keys is not None else list(self.cnt.keys())):
            v = self.cnt.get(k, 0)
            if v > 0 and self.waited[eng].get(k, 0) < v:
                waits.append((k, v))
                self.waited[eng][k] = v
        self.prog[eng].append((None, waits, None, 0))

    def barrier(self):
        for e in self.ENG:
            self.wait_all(e)

    def emit(self):
        nc = self.nc
        with nc.Block() as block:
            def run(eng_name):
                def body(e):
                    for fn, waits, key, inc in self.prog[eng_name]:
                        for k, v in waits:
                            e.wait_ge(self.sems[k], v)
                        if fn is not None:
                            fn(e).then_inc(self.sems[key], inc)
                return body
            block.sync(run("sync"))
            block.tensor(run("tensor"))
            block.vector(run("vector"))
            block.scalar(run("scalar"))
            block.gpsimd(run("gpsimd"))

    def close(self):
        self.release(0)
        while self.sem_stack:
            self.sem_stack.pop().__exit__(None, None, None)


class K:
    pass


def build(seq_lens=SEQ_LENS, stop_after=99, dbg=False):
    nc = bass.Bass("TRN2", target_bir_lowering=False)
    NT = sum(seq_lens)
    NCH = NT // 128
    seq_chunks = []
    c0 = 0
    for L in seq_lens:
        seq_chunks.append((c0, L // 128))
        c0 += L // 128

    def din(name, shape, dt=F32):
        return nc.dram_tensor(name, list(shape), dt, kind="ExternalInput").ap()

    x_in = din("x_in", [NT, DM])
    norm_mix_g = din("norm_mix_g", [2, DM])
    norm_mlp_g = din("norm_mlp_g", [2, DM])
    norm_final_g = din("norm_final_g", [1, DM])
    ssd_w_in = din("ssd_w_in", [DM, SSD_IN])
    ssd_cw = din("ssd_cw", [128, 32, 7])
    ssd_cb = din("ssd_cb", [128, 32])
    ssd_hp = din("ssd_hp", [1, 5 * 32])
    ssd_norm_g = din("ssd_norm_g", [1, DI])
    ssd_w_out = din("ssd_w_out", [DI, DM])
    gla_w_in = din("gla_w_in", [DM, GLA_IN])
    gla_gate_up = din("gla_gate_up", [16, 1024])
    gla_gate_bias = din("gla_gate_bias", [1, 1024])
    gla_norm_g = din("gla_norm_g", [1, 256])
    gla_w_out = din("gla_w_out", [DM, DM])
    mlp_w_up = din("mlp_w_up", [2, DM, DFF])
    mlp_w_down = din("mlp_w_down", [2, DFF, DM])
    norm_mix_gc = din("norm_mix_gc", [2, 128, 8])
    norm_mlp_gc = din("norm_mlp_gc", [2, 128, 8])
    ssd_norm_gc = din("ssd_norm_gc", [128, 16])
    y_out = nc.dram_tensor("y_out", [NT, DM], F32, kind="ExternalOutput").ap()
    ndbg = 24
    if dbg:
        dbg32 = nc.dram_tensor("dbg32", [ndbg, 128, 2048], F32, kind="ExternalOutput").ap()
        dbg16 = nc.dram_tensor("dbg16", [ndbg, 128, 2048], BF16, kind="ExternalOutput").ap()

    def dscr(name, shape, dt):
        return nc.dram_tensor(name, list(shape), dt).ap()

    X1 = dscr("X1", [NT, DM], F32)
    X2 = dscr("X2", [NT, DM], F32)
    X3 = dscr("X3", [NT, DM], F32)
    s_xtok = dscr("s_xtok", [NT, DI], BF16)
    s_btok = dscr("s_btok", [NT, 1024], BF16)
    s_bct = dscr("s_bct", [NCH, 128, 2048], BF16)
    s_small = dscr("s_small", [NT, 192], F32)
    s_prev = dscr("s_prev", [NCH, 128, 2048], BF16)
    g_prev = dscr("g_prev", [NCH, 128, 1024], BF16)
    s_acs = dscr("s_acs", [NCH, 128, 128], BF16)

    S = Sched(nc)
    for e in Sched.ENG:
        S.sem(e)

    def V(fn, r=(), w=()):
        S.op("vector", fn, reads=r, writes=w)

    def A(fn, r=(), w=()):
        S.op("scalar", fn, reads=r, writes=w)

    def G(fn, r=(), w=()):
        S.op("gpsimd", fn, reads=r, writes=w)

    def P(fn, r=(), w=()):
        S.op("tensor", fn, reads=r, writes=w)

    def LD(out_t, out_ap, in_ap, eng="sync"):
        S.op(eng, lambda e: e.dma_start(out=out_ap, in_=in_ap), writes=[out_t], dma=True,
             semkey="ld_" + out_t.name)

    def ST(in_t, out_ap, in_ap, eng="gpsimd"):
        S.op(eng, lambda e: e.dma_start(out=out_ap, in_=in_ap), reads=[in_t], dma=True,
             semkey="st_" + in_t.name)

    PS = [S.tile([128, 512], F32, f"ps{i}", psum=True) for i in range(6)]
    PSB = [S.tile([128, 1024], BF16, f"psb{i}", psum=True) for i in range(2)]
    st_ps = {"i": 0, "b": 0}

    def nps():
        st_ps["i"] += 1
        return PS[st_ps["i"] % 6]

    def npsb():
        st_ps["b"] += 1
        return PSB[st_ps["b"] % 2]

    identf = S.tile([128, 128], F32, "identf")
    identb = S.tile([128, 128], BF16, "identb")
    Mle = S.tile([128, 128], F32, "Mle")
    Mge = S.tile([128, 128], F32, "Mge")
    Mgt = S.tile([128, 128], F32, "Mgt")
    Mlt = S.tile([128, 128], F32, "Mlt")
    onesf = S.tile([128, 128], F32, "onesf")
    MleB = S.tile([128, 128], BF16, "MleB")
    MgeB = S.tile([128, 128], BF16, "MgeB")

    def tri(t, pat, cm, cmp):
        G(lambda e: e.memset(t[:], 1.0), w=[t])
        G(lambda e: e.affine_select(out=t[:], in_=t[:], pattern=[[pat, 128]], compare_op=cmp, fill=0.0,
                                    base=0, channel_multiplier=cm), r=[t], w=[t])

    tri(identf, -1, 1, ALU.is_equal)
    tri(Mle, 1, -1, ALU.is_ge)
    tri(Mge, -1, 1, ALU.is_ge)
    tri(Mgt, -1, 1, ALU.is_gt)
    tri(Mlt, 1, -1, ALU.is_gt)
    G(lambda e: e.memset(onesf[:], 1.0), w=[onesf])
    V(lambda e: e.tensor_copy(out=identb[:], in_=identf[:]), r=[identf], w=[identb])
    V(lambda e: e.tensor_copy(out=MleB[:], in_=Mle[:]), r=[Mle], w=[MleB])
    V(lambda e: e.tensor_copy(out=MgeB[:], in_=Mge[:]), r=[Mge], w=[MgeB])
    MgtB = S.tile([128, 128], BF16, "MgtB")
    MltB = S.tile([128, 128], BF16, "MltB")
    onesb = S.tile([128, 128], BF16, "onesb")
    V(lambda e: e.tensor_copy(out=MgtB[:], in_=Mgt[:]), r=[Mgt], w=[MgtB])
    V(lambda e: e.tensor_copy(out=MltB[:], in_=Mlt[:]), r=[Mlt], w=[MltB])
    V(lambda e: e.tensor_copy(out=onesb[:], in_=onesf[:]), r=[onesf], w=[onesb])

    dbg_i = {"i": 0}
    dbg_names = {}

    def DBG(name, t, ap, ncols, parts=128, dt=F32):
        if not dbg:
            return
        i = dbg_i["i"]
        dbg_i["i"] += 1
        assert i < ndbg
        dbg_names[name] = (i, dt == F32)
        dst = dbg32 if dt == F32 else dbg16
        ST(t, dst[i][0:parts, 0:ncols], ap)

    def load_w(dst, dcol0, src, scol0, ncols, K, stg, rowscale=None):
        srcv = src.rearrange("(k p) c -> p k c", p=128)
        engs = ["vector", "scalar", "gpsimd", "vector", "scalar"]
        i = 0
        c = 0
        while c < ncols:
            cc = min(512, ncols - c)
            kk_max = max(1, 2048 // cc)
            k0 = 0
            while k0 < K:
                kk = min(kk_max, K - k0)
                st = stg[i % len(stg)]
                stv = st[:, 0:kk * cc].rearrange("p (k c) -> p k c", k=kk)
                LD(st, stv, srcv[:, k0:k0 + kk, scol0 + c:scol0 + c + cc])
                eng = engs[i % 5]
                if rowscale is None:
                    groups = [(stv, dst[:, k0:k0 + kk, dcol0 + c:dcol0 + c + cc], None)]
                else:
                    groups = [(stv[:, q, :], dst[:, k0 + q, dcol0 + c:dcol0 + c + cc], rowscale[:, k0 + q:k0 + q + 1]) for q in range(kk)]
                for (sv, dv, rs) in groups:
                    rd = [st] if rs is None else [st, rowscale]
                    if eng == "scalar":
                        if rs is None:
                            A(lambda e, dv=dv, sv=sv: e.activation(out=dv, in_=sv, func=AF.Copy), r=rd, w=[])
                        else:
                            A(lambda e, dv=dv, sv=sv, rs=rs: e.activation(out=dv, in_=sv, func=AF.Copy, scale=rs), r=rd, w=[])
                    else:
                        fnE = V if eng == "vector" else G
                        if rs is None:
                            fnE(lambda e, dv=dv, sv=sv: e.tensor_copy(out=dv, in_=sv), r=rd, w=[])
                        else:
                            fnE(lambda e, dv=dv, sv=sv, rs=rs: e.tensor_scalar(out=dv, in0=sv, scalar1=rs, scalar2=None, op0=ALU.mult), r=rd, w=[])
                i += 1
                k0 += kk
            c += cc

    ssr = [S.tile([128, 8], F32, f"ssr{i}") for i in range(4)]
    ssr_i = {"i": 0}
    junk = S.tile([128, 2048], BF16, "junk")

    def rmsnorm(xt, xap, g_t, gap, ut, uap, Dn):
        ssr_i["i"] += 1
        ss = ssr[ssr_i["i"] % 4]
        V(lambda e: e.memset(ss[:, 0:2], 0.0), w=[ss])
        A(lambda e: e.activation(out=junk[:, 0:Dn], in_=xap, func=AF.Square, accum_out=ss[:, 0:1]),
          r=[xt, ss], w=[junk, ss])
        V(lambda e: e.tensor_scalar(out=ss[:, 1:2], in0=ss[:, 0:1], scalar1=1.0 / Dn, scalar2=EPS,
                                    op0=ALU.mult, op1=ALU.add), r=[ss], w=[ss])
        A(lambda e: e.activation(out=ss[:, 1:2], in_=ss[:, 1:2], func=AF.Ln), r=[ss], w=[ss])
        A(lambda e: e.activation(out=ss[:, 1:2], in_=ss[:, 1:2], func=AF.Exp, scale=-0.5), r=[ss], w=[ss])
        if g_t is None:
            V(lambda e: e.tensor_scalar(out=uap, in0=xap, scalar1=ss[:, 1:2], scalar2=None, op0=ALU.mult), r=[xt, ss], w=[ut])
        else:
            V(lambda e: e.scalar_tensor_tensor(out=uap, in0=xap, scalar=ss[:, 1:2], in1=gap,
                                               op0=ALU.mult, op1=ALU.mult), r=[xt, ss, g_t], w=[ut])

    def transpose_to(src_t, src_ap_fn, ntiles, dst_t, dst_ap_fn):
        i = 0
        while i < ntiles:
            n = min(8, ntiles - i)
            pb = npsb()
            for j in range(n):
                P(lambda e, pb=pb, j=j, i=i: e.transpose(pb[:, j * 128:(j + 1) * 128], src_ap_fn(i + j), identb[:]),
                  r=[src_t, identb], w=[pb])
            V(lambda e, pb=pb, n=n, i=i: e.tensor_copy(out=dst_ap_fn(i, n), in_=pb[:, 0:n * 128].rearrange("p (a b) -> p a b", a=n)), r=[pb], w=[dst_t])
            i += n

    class Locks:
        def __init__(self):
            self.front = False
            self.back = False

        def release_front(self):
            self.front = False

        def back_free(self):
            return not self.back

        def acquire_back(self):
            self.back = True

        def release_back(self):
            self.back = False

    def run_pipeline(items, body, depth=2):
        L = Locks()
        active = []
        idx = 0
        while active or idx < len(items):
            if idx < len(items) and not L.front and len(active) < depth:
                L.front = True
                active.append(body(items[idx], L))
                idx += 1
            for g in list(active):
                try:
                    next(g)
                except StopIteration:
                    active.remove(g)

    def bcast_load(t, ap_row, n):
        LD(t, t[:, 0:n], ap_row.partition_broadcast(128))

    def phase_mlp(layer, Xsrc, Xdst, final):
        m0 = S.mark()
        Wup = S.tile([128, 8, DFF], BF16, "Wup")
        Wdn = S.tile([128, 32, DM], BF16, "Wdn")
        gcm = S.tile([128, 8], F32, "gcm")
        LD(gcm, gcm[:], norm_mlp_gc[layer])
        if final:
            gf_bc = S.tile([128, DM], F32, "gf_bc")
            bcast_load(gf_bc, norm_final_g[0:1, :], DM)
        m1 = S.mark()
        stg = [S.tile([128, 2048], F32, f"stg{i}") for i in range(3)]
        load_w(Wup, 0, mlp_w_up[layer], 0, DFF, 8, stg, rowscale=gcm)
        load_w(Wdn, 0, mlp_w_down[layer], 0, DM, 32, stg)
        S.barrier()
        S.release(m1)
        TT = 256
        xts = [S.tile([128, 2, DM], F32, f"mx{i}") for i in range(2)]
        xos = [S.tile([128, 2, DM], F32, "mo0")] * 2
        hTs = [S.tile([128, 32, TT], BF16, f"mhT{i}") for i in range(2)]
        u = S.tile([128, DM], BF16, "mu")
        uT = S.tile([128, 8, TT], BF16, "muT")
        rl = [S.tile([128, 512], F32, f"mrl{i}") for i in range(2)]

        def body(t, L):
            xt, xo, hT = xts[t % 2], xos[t % 2], hTs[t % 2]
            LD(xt, xt[:], Xsrc[t * TT:(t + 1) * TT, :].rearrange("(j p) d -> p j d", p=128))
            yield
            for j in range(2):
                rmsnorm(xt, xt[:, j, :], None, None, u, u[:], DM)
                yield
                yield
                transpose_to(u, lambda i: u[:, i * 128:(i + 1) * 128], 8, uT,
                             lambda i0, n, j=j: uT[:, i0:i0 + n, j * 128:(j + 1) * 128])
                yield
            for fp in range(16):
                ps = nps()
                for f2 in range(2):
                    f = fp * 2 + f2
                    for k in range(8):
                        P(lambda e, ps=ps, f=f, f2=f2, k=k: e.matmul(ps[:, f2 * TT:(f2 + 1) * TT],
                                                                     lhsT=Wup[:, k, f * 128:(f + 1) * 128],
                                                                     rhs=uT[:, k, :], start=(k == 0), stop=(k == 7)),
                          r=[Wup, uT], w=[ps])
                r_ = rl[fp % 2]
                A(lambda e, ps=ps, r_=r_: e.activation(out=r_[:], in_=ps[:], func=AF.Relu), r=[ps], w=[r_])
                V(lambda e, r_=r_, fp=fp, hT=hT: e.tensor_tensor(out=hT[:, fp * 2:fp * 2 + 2, :],
                                                                 in0=r_[:].rearrange("p (a b) -> p a b", a=2),
                                                                 in1=r_[:].rearrange("p (a b) -> p a b", a=2), op=ALU.mult),
                  r=[r_], w=[hT])
                yield
            L.release_front()
            while not L.back_free():
                yield
            L.acquire_back()
            for j in range(2):
                for cb in range(2):
                    ps = nps()
                    for f in range(32):
                        P(lambda e, ps=ps, f=f, j=j, cb=cb, hT=hT: e.matmul(ps[:], lhsT=hT[:, f, j * 128:(j + 1) * 128],
                                                                            rhs=Wdn[:, f, cb * 512:(cb + 1) * 512],
                                                                            start=(f == 0), stop=(f == 31)),
                          r=[hT, Wdn], w=[ps])
                        if f % 8 == 7 and f != 31:
                            yield
                    V(lambda e, ps=ps, j=j, cb=cb, xo=xo, xt=xt: e.tensor_tensor(
                        out=xo[:, j, cb * 512:(cb + 1) * 512], in0=ps[:], in1=xt[:, j, cb * 512:(cb + 1) * 512],
                        op=ALU.add), r=[ps, xt], w=[xo])
                    yield
                if final:
                    rmsnorm(xo, xo[:, j, :], gf_bc, gf_bc[:], xo, xo[:, j, :], DM)
            ST(xo, Xdst[t * TT:(t + 1) * TT, :].rearrange("(j p) d -> p j d", p=128), xo[:])
            L.release_back()

        run_pipeline(list(range(NT // TT)), body)
        S.barrier()
        S.release(m0)

    def ssd_small_consts():
        hp = S.tile([128, 160], F32, "hp")
        bcast_load(hp, ssd_hp[0:1, :], 160)
        A(lambda e: e.activation(out=hp[:, 64:128], in_=hp[:, 64:128], func=AF.Exp), r=[hp], w=[hp])
        V(lambda e: e.tensor_scalar(out=hp[:, 64:128], in0=hp[:, 64:128], scalar1=-1.0, scalar2=None, op0=ALU.mult),
          r=[hp], w=[hp])
        return hp

    def ssd_phase_a():
        m0 = S.mark()
        Wx = S.tile([128, 8, 4160], BF16, "Wx")
        diag = S.tile([128, 32, 7, 128], BF16, "diag")
        cb = S.tile([128, 32], F32, "cb")
        gcx = S.tile([128, 8], F32, "gcx")
        hp = ssd_small_consts()
        LD(cb, cb[:], ssd_cb)
        LD(gcx, gcx[:], norm_mix_gc[0])
        m1 = S.mark()
        cw = S.tile([128, 32, 7], F32, "cw")
        LD(cw, cw[:], ssd_cw)
        stg = [S.tile([128, 2048], F32, f"stg{i}") for i in range(2)]
        load_w(Wx, 0, ssd_w_in, DI, 4096, 8, stg, rowscale=gcx)
        load_w(Wx, 4096, ssd_w_in, DI + 4096, 64, 8, stg, rowscale=gcx)
        di = 0
        for m in range(32):
            for j in range(7):
                di += 1
                if di % 5 in (0, 2):
                    V(lambda e, m=m, j=j: e.tensor_scalar(out=diag[:, m, j, :], in0=identf[:], scalar1=cw[:, m, j:j + 1],
                                                          scalar2=None, op0=ALU.mult), r=[identf, cw], w=[])
                elif di % 5 in (1, 3):
                    A(lambda e, m=m, j=j: e.activation(out=diag[:, m, j, :], in_=identf[:], func=AF.Copy, scale=cw[:, m, j:j + 1]),
                      r=[identf, cw], w=[])
                else:
                    G(lambda e, m=m, j=j: e.tensor_scalar(out=diag[:, m, j, :], in0=identf[:], scalar1=cw[:, m, j:j + 1],
                                                          scalar2=None, op0=ALU.mult), r=[identf, cw], w=[])
        S.barrier()
        S.release(m1)
        xt = S.tile([128, DM], F32, "ax")
        u = S.tile([128, DM], BF16, "au")
        uT = S.tile([128, 8, 256], BF16, "auT")
        pre = [S.tile([128, 32, 262], BF16, f"pre{i}") for i in range(2)]
        lh = S.tile([128, 32, 3], BF16, "lh")
        xbcT = S.tile([128, 16, 256], BF16, "xbcT")
        xtok = S.tile([128, DI], BF16, "xtok")
        btoks = [S.tile([128, 1024], BF16, f"btok{i}") for i in range(2)]
        sms = [S.tile([128, 192], F32, f"sm{i}") for i in range(6)]
        ex = S.tile([128, 64], F32, "ex")
        wf = S.tile([128, 32], F32, "wf")
        Sf = S.tile([128, DI], F32, "Sf")
        prevb = S.tile([128, DI], BF16, "prevb")
        acs = S.tile([128, 64], F32, "acs")
        acshl = S.tile([128, 128], BF16, "acshl")
        acsT = S.tile([128, 128], BF16, "acsT")

        for (cs, C) in seq_chunks:
            NSC = C // 2
            V(lambda e: e.memset(Sf[:], 0.0), w=[Sf])

            def stage1_a1(sc, jj):
                gc = cs + 2 * sc + jj
                LD(xt, xt[:], x_in[gc * 128:(gc + 1) * 128, :])
                rmsnorm(xt, xt[:], None, None, u, u[:], DM)

            def stage1(sc, hoisted):
                sl = pre[sc % 2]
                for jj in range(2):
                    gc = cs + 2 * sc + jj
                    sm = sms[gc % 6]
                    if not (jj == 0 and hoisted):
                        stage1_a1(sc, jj)
                    transpose_to(u, lambda i: u[:, i * 128:(i + 1) * 128], 8, uT,
                                 lambda i0, n, jj=jj: uT[:, i0:i0 + n, jj * 128:(jj + 1) * 128])
                    ps = nps()
                    for k in range(8):
                        P(lambda e, ps=ps, k=k, jj=jj: e.matmul(ps[:, 0:64], lhsT=uT[:, k, jj * 128:(jj + 1) * 128], rhs=Wx[:, k, 4096:4160],
                                                                start=(k == 0), stop=(k == 7)), r=[Wx, uT], w=[ps])
                    V(lambda e, ps=ps, sm=sm: e.tensor_tensor(out=sm[:, 0:64], in0=ps[:, 0:64], in1=hp[:, 0:64], op=ALU.add),
                      r=[ps, hp], w=[sm])
                    A(lambda e, sm=sm: e.activation(out=sm[:, 0:64], in_=sm[:, 0:64], func=AF.Exp), r=[sm], w=[sm])
                    A(lambda e, sm=sm: e.activation(out=sm[:, 0:64], in_=sm[:, 0:64], func=AF.Ln, bias=1.0), r=[sm], w=[sm])
                    A(lambda e, sm=sm: e.activation(out=sm[:, 64:128], in_=sm[:, 0:64], func=AF.Ln), r=[sm], w=[sm])
                    V(lambda e, sm=sm: e.tensor_tensor(out=sm[:, 128:192], in0=sm[:, 0:64], in1=hp[:, 64:128], op=ALU.mult),
                      r=[sm, hp], w=[sm])
                    yield
                if sc == 0:
                    G(lambda e, sl=sl: e.memset(sl[:, :, 0:3], 0.0), w=[sl])
                else:
                    G(lambda e, sl=sl: e.tensor_copy(out=sl[:, :, 0:3], in_=lh[:]), r=[lh], w=[sl])
                for mb in range(16):
                    ps = nps()
                    for m2 in range(2):
                        m = mb * 2 + m2
                        for k in range(8):
                            P(lambda e, ps=ps, m=m, m2=m2, k=k: e.matmul(ps[:, m2 * 256:(m2 + 1) * 256],
                                                                         lhsT=Wx[:, k, m * 128:(m + 1) * 128],
                                                                         rhs=uT[:, k, :], start=(k == 0), stop=(k == 7)),
                              r=[Wx, uT], w=[ps])
                    A(lambda e, ps=ps, mb=mb, sl=sl: e.activation(out=sl[:, mb * 2:mb * 2 + 2, 3:259],
                                                                  in_=ps[:].rearrange("p (a b) -> p a b", a=2),
                                                                  func=AF.Copy), r=[ps], w=[sl])
                    yield
                G(lambda e, sl=sl: e.tensor_copy(out=lh[:], in_=sl[:, :, 256:259]), r=[sl], w=[lh])
                if sc > 0:
                    slp = pre[(sc - 1) % 2]
                    G(lambda e, sl=sl, slp=slp: e.tensor_copy(out=slp[:, :, 259:262], in_=sl[:, :, 3:6]), r=[sl], w=[slp])
                if sc == NSC - 1:
                    G(lambda e, sl=sl: e.memset(sl[:, :, 259:262], 0.0), w=[sl])

            def conv_half(sc, half):
                sl = pre[sc % 2]
                for mb in range(8):
                    ps = nps()
                    for m2 in range(2):
                        m = half * 16 + mb * 2 + m2
                        for j in range(7):
                            P(lambda e, ps=ps, m=m, m2=m2, j=j, sl=sl: e.matmul(ps[:, m2 * 256:(m2 + 1) * 256],
                                                                                lhsT=diag[:, m, j, :], rhs=sl[:, m, j:j + 256],
                                                                                start=(j == 0), stop=(j == 6)),
                              r=[diag, sl], w=[ps])
                    for m2 in range(2):
                        m = half * 16 + mb * 2 + m2
                        A(lambda e, ps=ps, m=m, m2=m2, mb=mb: e.activation(out=xbcT[:, mb * 2 + m2, :], in_=ps[:, m2 * 256:(m2 + 1) * 256],
                                                                           func=AF.Silu, bias=cb[:, m:m + 1]),
                          r=[ps, cb], w=[xbcT])

            def conv_part(sc):
                conv_half(sc, 1)
                for jj in range(2):
                    gc = cs + 2 * sc + jj
                    btok = btoks[jj]
                    transpose_to(xbcT, lambda i, jj=jj: xbcT[:, i, jj * 128:(jj + 1) * 128], 8, btok,
                                 lambda i0, n, btok=btok: btok[:, i0 * 128:(i0 + n) * 128].rearrange("p (a b) -> p a b", a=n))
                    ST(btok, s_btok[gc * 128:(gc + 1) * 128, :], btok[:])
                    ST(xbcT, s_bct[gc].rearrange("p (a b) -> p a b", a=16), xbcT[:, :, jj * 128:(jj + 1) * 128])
                conv_half(sc, 0)

            def tails(sc):
                for jj in range(2):
                    gc = cs + 2 * sc + jj
                    sm = sms[gc % 6]
                    btok = btoks[jj]
                    transpose_to(xbcT, lambda i, jj=jj: xbcT[:, i, jj * 128:(jj + 1) * 128], 16, xtok,
                                 lambda i0, n: xtok[:, i0 * 128:(i0 + n) * 128].rearrange("p (a b) -> p a b", a=n))
                    ST(xtok, s_xtok[gc * 128:(gc + 1) * 128, :], xtok[:])
                    yield
                    ps = nps()
                    P(lambda e, ps=ps, sm=sm: e.matmul(ps[:, 0:32], lhsT=Mle[:], rhs=sm[:, 128:160], start=True, stop=True), r=[Mle, sm], w=[ps])
                    P(lambda e, ps=ps, sm=sm: e.matmul(ps[:, 32:64], lhsT=Mge[:], rhs=sm[:, 160:192], start=True, stop=True), r=[Mge, sm], w=[ps])
                    V(lambda e, ps=ps: e.tensor_copy(out=acs[:], in_=ps[:, 0:64]), r=[ps], w=[acs])
                    V(lambda e, sm=sm: e.tensor_tensor(out=sm[:, 64:128], in0=sm[:, 64:128], in1=acs[:], op=ALU.subtract), r=[sm, acs], w=[sm])
                    ST(sm, s_small[gc * 128:(gc + 1) * 128, :], sm[:])
                    yield
                    V(lambda e: e.tensor_copy(out=acshl[:, 0:64], in_=acs[:]), r=[acs], w=[acshl])
                    V(lambda e: e.tensor_tensor(out=acs[:], in0=acs[:], in1=acshl[:, 0:64], op=ALU.subtract), r=[acs, acshl], w=[acs])
                    V(lambda e: e.tensor_copy(out=acshl[:, 64:128], in_=acs[:]), r=[acs], w=[acshl])
                    pb = npsb()
                    P(lambda e, pb=pb: e.transpose(pb[:, 0:128], acshl[:], identb[:]), r=[acshl, identb], w=[pb])
                    V(lambda e, pb=pb: e.tensor_copy(out=acsT[:], in_=pb[:, 0:128]), r=[pb], w=[acsT])
                    ST(acsT, s_acs[gc], acsT[:])
                    yield
                    ps = nps()
                    P(lambda e, ps=ps, sm=sm: e.matmul(ps[:, 0:32], lhsT=Mgt[:], rhs=sm[:, 128:160], start=True, stop=True),
                      r=[Mgt, sm], w=[ps])
                    P(lambda e, ps=ps, sm=sm: e.matmul(ps[:, 32:64], lhsT=onesf[:], rhs=sm[:, 128:160], start=True, stop=True),
                      r=[onesf, sm], w=[ps])
                    A(lambda e, ps=ps: e.activation(out=ex[:], in_=ps[:, 0:64], func=AF.Exp), r=[ps], w=[ex])
                    V(lambda e, sm=sm: e.tensor_tensor(out=wf[:], in0=sm[:, 0:32], in1=ex[:, 0:32], op=ALU.mult), r=[sm, ex], w=[wf])
                    V(lambda e: e.tensor_tensor(out=xtok[:].rearrange("p (h d) -> p h d", h=32),
                                                in0=xtok[:].rearrange("p (h d) -> p h d", h=32),
                                                in1=wf[:].unsqueeze(2).to_broadcast([128, 32, 64]), op=ALU.mult),
                      r=[xtok, wf], w=[xtok])
                    yield
                    V(lambda e: e.tensor_copy(out=prevb[:], in_=Sf[:]), r=[Sf], w=[prevb])
                    ST(prevb, s_prev[gc], prevb[:])
                    V(lambda e: e.tensor_tensor(out=Sf[:].rearrange("p (h d) -> p h d", h=32),
                                                in0=Sf[:].rearrange("p (h d) -> p h d", h=32),
                                                in1=ex[:, 32:64].unsqueeze(2).to_broadcast([128, 32, 64]), op=ALU.mult),
                      r=[Sf, ex], w=[Sf])
                    for g2 in range(4):
                        ps = nps()
                        for gg in range(2):
                            g = g2 * 2 + gg
                            P(lambda e, ps=ps, g=g, gg=gg, btok=btok: e.matmul(ps[:, gg * 256:(gg + 1) * 256],
                                                                              lhsT=btok[:, g * 128:(g + 1) * 128],
                                                                              rhs=xtok[:, g * 256:(g + 1) * 256], start=True, stop=True),
                              r=[btok, xtok], w=[ps])
                        V(lambda e, ps=ps, g2=g2: e.tensor_tensor(out=Sf[:, g2 * 512:(g2 + 1) * 512],
                                                                  in0=Sf[:, g2 * 512:(g2 + 1) * 512], in1=ps[:], op=ALU.add),
                          r=[Sf, ps], w=[Sf])
                        yield

            def interleave(ga, gb):
                gens = [g for g in (ga, gb) if g is not None]
                while gens:
                    for g in list(gens):
                        try:
                            next(g)
                        except StopIteration:
                            gens.remove(g)

            interleave(stage1(0, False), None)
            if NSC > 1:
                stage1_a1(1, 0)
            for sc in range(NSC):
                ga = stage1(sc + 1, True) if sc + 1 < NSC else None
                gb = tails(sc - 1) if sc >= 1 else None
                interleave(ga, gb)
                if sc + 2 < NSC:
                    stage1_a1(sc + 2, 0)
                conv_part(sc)
            interleave(tails(NSC - 1), None)
        S.barrier()
        S.release(m0)

    def ssd_phase_b(Xdst):
        m0 = S.mark()
        Wz = S.tile([128, 8, DI], BF16, "Wz")
        Wo = S.tile([128, 16, DM], BF16, "Wo")
        gcz = S.tile([128, 8], F32, "gcz")
        gco = S.tile([128, 16], F32, "gco")
        hp = ssd_small_consts()
        LD(gcz, gcz[:], norm_mix_gc[0])
        LD(gco, gco[:], ssd_norm_gc)
        m1 = S.mark()
        stg = [S.tile([128, 2048], F32, f"stg{i}") for i in range(3)]
        load_w(Wz, 0, ssd_w_in, 0, DI, 8, stg, rowscale=gcz)
        load_w(Wo, 0, ssd_w_out, 0, DM, 16, stg, rowscale=gco)
        S.barrier()
        S.release(m1)
        xts = [S.tile([128, DM], F32, f"bx{i}") for i in range(2)]
        xtoks = [S.tile([128, DI], BF16, f"bxtok{i}") for i in range(2)]
        btoks = [S.tile([128, 1024], BF16, f"bbtok{i}") for i in range(2)]
        bcts = [S.tile([128, 16, 128], BF16, f"bbct{i}") for i in range(2)]
        smsb = [S.tile([128, 192], F32, f"bsm{i}") for i in range(2)]
        prevfs = [S.tile([128, DI], BF16, f"bprevf{i}") for i in range(2)]
        zss = [S.tile([128, DI], BF16, f"zs{i}") for i in range(2)]
        evs = [S.tile([128, 128], F32, f"ev{i}") for i in range(2)]
        yaccs = [S.tile([128, DI], F32, f"yacc{i}") for i in range(2)]
        xos = [S.tile([128, DM], F32, f"bxo{i}") for i in range(2)]
        RB = S.tile([128, 3072], BF16, "RB")
        u = S.tile([128, DM], BF16, "bu")
        uT = S.tile([128, 8, 128], BF16, "buT")
        sc = S.tile([128, 8, 128], BF16, "sc")
        Wr = [S.tile([128, 4, 128], BF16, f"Wr{i}") for i in range(8)]
        mneg = [S.tile([128, 4, 128], BF16, f"mneg{i}") for i in range(2)]
        for d, Mm in enumerate([Mle, Mge]):
            V(lambda e, d=d, Mm=Mm: e.tensor_scalar(out=mneg[d][:], in0=Mm[:].unsqueeze(1).to_broadcast([128, 4, 128]), scalar1=-1.0, scalar2=30000.0,
                                                    op0=ALU.add, op1=ALU.mult), r=[Mm], w=[mneg[d]])
        t1r = [S.tile([128, 512], F32, f"t1r{i}") for i in range(2)]
        wb = S.tile([128, 32], F32, "wb")
        xw = S.tile([128, DI], BF16, "bxw")
        Sb = S.tile([128, DI], F32, "Sb")
        prevb = S.tile([128, DI], BF16, "bprevb")
        yn = S.tile([128, DI], BF16, "yn")
        ynT = S.tile([128, 16, 128], BF16, "ynT")
        st = {"ri": 0, "ti": 0}

        def body(item, L):
            gc, first = item
            i2 = gc % 2
            xt, xtok, btok, bct, sm, prevf, zs, ev, yacc, xo = (xts[i2], xtoks[i2], btoks[i2], bcts[i2], smsb[i2], prevfs[i2],
                                                                zss[i2], evs[i2], yaccs[i2], xos[i2])
            LD(xt, xt[:], x_in[gc * 128:(gc + 1) * 128, :])
            LD(sm, sm[:], s_small[gc * 128:(gc + 1) * 128, :])
            LD(bct, bct[:].rearrange("p a b -> p (a b)"), s_bct[gc])
            acv = s_acs[gc].rearrange("(hl u h4) l -> hl u (h4 l)", hl=2, u=16)
            for b in range(3):
                n = len(range(b, 16, 3))
                for hl in range(2):
                    LD(RB, RB[32 * b + hl:32 * b + hl + 1, 0:n * 512].rearrange("p (s x) -> p s x", s=n), acv[hl:hl + 1, b::3, :])
            LD(xtok, xtok[:], s_xtok[gc * 128:(gc + 1) * 128, :])
            LD(btok, btok[:], s_btok[gc * 128:(gc + 1) * 128, :])
            LD(prevf, prevf[:], s_prev[gc])
            yield
            rmsnorm(xt, xt[:], None, None, u, u[:], DM)
            yield
            yield
            yield
            transpose_to(u, lambda i: u[:, i * 128:(i + 1) * 128], 8, uT, lambda i0, n: uT[:, i0:i0 + n, :])
            yield
            for cbk in range(4):
                ps = nps()
                for k in range(8):
                    P(lambda e, ps=ps, k=k, cbk=cbk: e.matmul(ps[:], lhsT=uT[:, k, :], rhs=Wz[:, k, cbk * 512:(cbk + 1) * 512],
                                                              start=(k == 0), stop=(k == 7)), r=[uT, Wz], w=[ps])
                A(lambda e, ps=ps, cbk=cbk, zs=zs: e.activation(out=zs[:, cbk * 512:(cbk + 1) * 512], in_=ps[:], func=AF.Silu),
                  r=[ps], w=[zs])
                yield
            ps = nps()
            P(lambda e, ps=ps, sm=sm: e.matmul(ps[:, 0:32], lhsT=Mle[:], rhs=sm[:, 128:160], start=True, stop=True), r=[Mle, sm], w=[ps])
            P(lambda e, ps=ps, sm=sm: e.matmul(ps[:, 32:64], lhsT=Mge[:], rhs=sm[:, 160:192], start=True, stop=True), r=[Mge, sm], w=[ps])
            P(lambda e, ps=ps, sm=sm: e.matmul(ps[:, 64:96], lhsT=Mlt[:], rhs=sm[:, 160:192], start=True, stop=True), r=[Mlt, sm], w=[ps])
            P(lambda e, ps=ps, sm=sm: e.matmul(ps[:, 96:128], lhsT=onesf[:], rhs=sm[:, 160:192], start=True, stop=True), r=[onesf, sm], w=[ps])
            A(lambda e, ps=ps, ev=ev: e.activation(out=ev[:], in_=ps[:, 0:128], func=AF.Exp), r=[ps], w=[ev])
            for g2 in range(2):
                ps = nps()
                for g4 in range(4):
                    g = g2 * 4 + g4
                    P(lambda e, ps=ps, g=g, g4=g4, bct=bct: e.matmul(ps[:, g4 * 128:(g4 + 1) * 128], lhsT=bct[:, g, :],
                                                                     rhs=bct[:, 8 + g, :], start=True, stop=True), r=[bct], w=[ps])
                A(lambda e, ps=ps, g2=g2: e.activation(out=sc[:, g2 * 4:g2 * 4 + 4, :], in_=ps[:].rearrange("p (a b) -> p a b", a=4),
                                                       func=AF.Copy), r=[ps], w=[sc])
            yield
            Wts = {}

            def P1(g):
                for d in range(2):
                    Wt = Wr[st["ri"] % 8]
                    st["ri"] += 1
                    uu = d * 8 + g
                    b, slot = uu % 3, uu // 3
                    ps = nps()
                    P(lambda e, ps=ps, b=b, slot=slot: e.matmul(ps[:], lhsT=onesb[32 * b:32 * b + 2, :], rhs=RB[32 * b:32 * b + 2, slot * 512:(slot + 1) * 512],
                                                                start=True, stop=False), r=[onesb, RB], w=[ps])
                    P(lambda e, ps=ps, d=d: e.matmul(ps[:], lhsT=identb[:], rhs=mneg[d][:].rearrange("p a b -> p (a b)"),
                                                     start=False, stop=True), r=[identb, mneg[d]], w=[ps])
                    for h4 in range(4):
                        h = g * 4 + h4
                        A(lambda e, ps=ps, Wt=Wt, h4=h4, h=h, d=d, sm=sm: e.activation(
                            out=Wt[:, h4, :], in_=ps[:, h4 * 128:(h4 + 1) * 128], func=AF.Exp,
                            bias=sm[:, 64 + d * 32 + h:64 + d * 32 + h + 1]), r=[ps, sm], w=[Wt])
                    V(lambda e, Wt=Wt, g=g: e.tensor_tensor(out=Wt[:], in0=Wt[:],
                                                            in1=sc[:, g:g + 1, :].to_broadcast([128, 4, 128]), op=ALU.mult),
                      r=[Wt, sc], w=[Wt])
                    Wts[(g, d)] = Wt

            def P2(g):
                psy = nps()
                for h4 in range(4):
                    h = g * 4 + h4
                    for d in range(2):
                        Wt = Wts[(g, d)]
                        P(lambda e, psy=psy, Wt=Wt, h4=h4, h=h, d=d, xtok=xtok: e.matmul(
                            psy[:, h4 * 64:(h4 + 1) * 64], lhsT=Wt[:, h4, :], rhs=xtok[:, h * 64:(h + 1) * 64],
                            start=(d == 0), stop=(d == 1)), r=[Wt, xtok], w=[psy])
                V(lambda e, psy=psy, g=g, yacc=yacc: e.tensor_copy(out=yacc[:, g * 256:(g + 1) * 256], in_=psy[:, 0:256]), r=[psy], w=[yacc])

            P1(0)
            yield
            P1(1)
            yield
            for g in range(8):
                if g + 2 < 8:
                    P1(g + 2)
                P2(g)
                yield
            L.release_front()
            while not L.back_free():
                yield
            L.acquire_back()
            if first:
                V(lambda e: e.memset(Sb[:], 0.0), w=[Sb])
            V(lambda e: e.tensor_copy(out=prevb[:], in_=Sb[:]), r=[Sb], w=[prevb])
            for d in range(2):
                pv = prevf if d == 0 else prevb
                for g2 in range(4):
                    ps = nps()
                    t1 = t1r[st["ti"] % 2]
                    st["ti"] += 1
                    for gg in range(2):
                        g = g2 * 2 + gg
                        P(lambda e, ps=ps, g=g, gg=gg, pv=pv, bct=bct: e.matmul(ps[:, gg * 256:(gg + 1) * 256], lhsT=bct[:, 8 + g, :],
                                                                                rhs=pv[:, g * 256:(g + 1) * 256], start=True, stop=True),
                          r=[bct, pv], w=[ps])
                    V(lambda e, ps=ps, g2=g2, d=d, t1=t1, ev=ev: e.tensor_tensor(
                        out=t1[:].rearrange("p (h x) -> p h x", h=8), in0=ps[:].rearrange("p (h x) -> p h x", h=8),
                        in1=ev[:, d * 32 + g2 * 8:d * 32 + g2 * 8 + 8].unsqueeze(2).to_broadcast([128, 8, 64]), op=ALU.mult),
                      r=[ps, ev], w=[t1])
                    G(lambda e, g2=g2, t1=t1, yacc=yacc: e.tensor_tensor(out=yacc[:, g2 * 512:(g2 + 1) * 512], in0=yacc[:, g2 * 512:(g2 + 1) * 512],
                                                                         in1=t1[:], op=ALU.add), r=[yacc, t1], w=[yacc])
                    yield
            for g2 in range(4):
                t1 = t1r[st["ti"] % 2]
                st["ti"] += 1
                G(lambda e, xtok=xtok, t1=t1, g2=g2: e.tensor_tensor(out=t1[:].rearrange("p (h x) -> p h x", h=8),
                                                                     in0=xtok[:, g2 * 512:(g2 + 1) * 512].rearrange("p (h x) -> p h x", h=8),
                                                                     in1=hp[:, 128 + g2 * 8:128 + g2 * 8 + 8].unsqueeze(2).to_broadcast([128, 8, 64]), op=ALU.mult),
                  r=[xtok, hp], w=[t1])
                V(lambda e, t1=t1, g2=g2, yacc=yacc: e.tensor_tensor(out=yacc[:, g2 * 512:(g2 + 1) * 512], in0=yacc[:, g2 * 512:(g2 + 1) * 512],
                                                                     in1=t1[:], op=ALU.add), r=[yacc, t1], w=[yacc])
            yield
            V(lambda e, sm=sm, ev=ev: e.tensor_tensor(out=wb[:], in0=sm[:, 32:64], in1=ev[:, 64:96], op=ALU.mult), r=[sm, ev], w=[wb])
            V(lambda e, xtok=xtok: e.tensor_tensor(out=xw[:].rearrange("p (h d) -> p h d", h=32),
                                                   in0=xtok[:].rearrange("p (h d) -> p h d", h=32),
                                                   in1=wb[:].unsqueeze(2).to_broadcast([128, 32, 64]), op=ALU.mult),
              r=[xtok, wb], w=[xw])
            G(lambda e, ev=ev: e.tensor_tensor(out=Sb[:].rearrange("p (h d) -> p h d", h=32),
                                               in0=Sb[:].rearrange("p (h d) -> p h d", h=32),
                                               in1=ev[:, 96:128].unsqueeze(2).to_broadcast([128, 32, 64]), op=ALU.mult),
              r=[Sb, ev], w=[Sb])
            for g2 in range(4):
                ps = nps()
                for gg in range(2):
                    g = g2 * 2 + gg
                    P(lambda e, ps=ps, g=g, gg=gg, btok=btok: e.matmul(ps[:, gg * 256:(gg + 1) * 256], lhsT=btok[:, g * 128:(g + 1) * 128],
                                                                      rhs=xw[:, g * 256:(g + 1) * 256], start=True, stop=True),
                      r=[btok, xw], w=[ps])
                V(lambda e, ps=ps, g2=g2: e.tensor_tensor(out=Sb[:, g2 * 512:(g2 + 1) * 512], in0=Sb[:, g2 * 512:(g2 + 1) * 512],
                                                          in1=ps[:], op=ALU.add), r=[Sb, ps], w=[Sb])
            yield
            V(lambda e, yacc=yacc, zs=zs: e.tensor_tensor(out=yacc[:], in0=yacc[:], in1=zs[:], op=ALU.mult), r=[yacc, zs], w=[yacc])
            rmsnorm(yacc, yacc[:], None, None, yn, yn[:], DI)
            yield
            yield
            yield
            yield
            transpose_to(yn, lambda i: yn[:, i * 128:(i + 1) * 128], 16, ynT, lambda i0, n: ynT[:, i0:i0 + n, :])
            yield
            for cbk in range(2):
                ps = nps()
                for k in range(16):
                    P(lambda e, ps=ps, k=k, cbk=cbk: e.matmul(ps[:], lhsT=ynT[:, k, :], rhs=Wo[:, k, cbk * 512:(cbk + 1) * 512],
                                                              start=(k == 0), stop=(k == 15)), r=[ynT, Wo], w=[ps])
                V(lambda e, ps=ps, cbk=cbk, xo=xo, xt=xt: e.tensor_tensor(out=xo[:, cbk * 512:(cbk + 1) * 512], in0=ps[:],
                                                                          in1=xt[:, cbk * 512:(cbk + 1) * 512], op=ALU.add),
                  r=[ps, xt], w=[xo])
                yield
            ST(xo, Xdst[gc * 128:(gc + 1) * 128, :], xo[:])
            L.release_back()

        items = []
        for (cs, C) in seq_chunks:
            for c in range(C - 1, -1, -1):
                items.append((cs + c, c == C - 1))
        run_pipeline(items, body)
        S.barrier()
        S.release(m0)

    def gla_common_tiles():
        k = K()
        k.Win = S.tile([128, 8, GLA_IN], BF16, "gWin")
        k.g_bc = S.tile([128, DM], F32, "gg_bc")
        k.gup = S.tile([16, 1024], F32, "gup")
        k.gupb = S.tile([16, 1024], BF16, "gupb")
        k.gbias = S.tile([128, 1024], F32, "gbias")
        bcast_load(k.g_bc, norm_mix_g[1:2, :], DM)
        bcast_load(k.gbias, gla_gate_bias[0:1, :], 1024)
        LD(k.gup, k.gup[:], gla_gate_up)
        V(lambda e: e.tensor_copy(out=k.gupb[:], in_=k.gup[:]), r=[k.gup], w=[k.gupb])
        return k

    def gla_front(k, xt, need_b, skip_norm=False):
        if not skip_norm:
            rmsnorm(xt, xt[:], k.g_bc, k.g_bc[:], k.u, k.u[:], DM)
        transpose_to(k.u, lambda i: k.u[:, i * 128:(i + 1) * 128], 8, k.uT, lambda i0, n: k.uT[:, i0:i0 + n, :])
        for d in range(2 if need_b else 1):
            ps = nps()
            for kk in range(8):
                P(lambda e, ps=ps, kk=kk, d=d: e.matmul(ps[0:16, 0:128], lhsT=k.Win[:, kk, 3072 + d * 16:3072 + d * 16 + 16],
                                                        rhs=k.uT[:, kk, :], start=(kk == 0), stop=(kk == 7)), r=[k.Win, k.uT], w=[ps])
            V(lambda e, ps=ps: e.tensor_copy(out=k.lrT[:], in_=ps[0:16, 0:128]), r=[ps], w=[k.lrT])
            ps2 = nps()
            P(lambda e, ps2=ps2, d=d: e.matmul(ps2[:], lhsT=k.lrT[:], rhs=k.gupb[:, d * 512:(d + 1) * 512], start=True, stop=True),
              r=[k.lrT, k.gupb], w=[ps2])
            lg = k.lg[d]
            V(lambda e, ps2=ps2, d=d, lg=lg: e.tensor_tensor(out=lg[:], in0=ps2[:], in1=k.gbias[:, d * 512:(d + 1) * 512], op=ALU.add),
              r=[ps2, k.gbias], w=[lg])
            A(lambda e, lg=lg: e.activation(out=lg[:], in_=lg[:], func=AF.Exp, scale=-1.0), r=[lg], w=[lg])
            A(lambda e, lg=lg: e.activation(out=lg[:], in_=lg[:], func=AF.Ln, bias=1.0), r=[lg], w=[lg])
            V(lambda e, lg=lg: e.tensor_scalar(out=lg[:], in0=lg[:], scalar1=-1.0 / 16.0, scalar2=None, op0=ALU.mult), r=[lg], w=[lg])
            lgh, lgl = k.lgh[d], k.lgl[d]
            V(lambda e, lg=lg, lgh=lgh: e.tensor_copy(out=lgh[:], in_=lg[:]), r=[lg], w=[lgh])
            G(lambda e, lg=lg, lgh=lgh: e.tensor_tensor(out=k.lgt[:], in0=lg[:], in1=lgh[:], op=ALU.subtract), r=[lg, lgh], w=[k.lgt])
            G(lambda e, lgl=lgl: e.tensor_copy(out=lgl[:], in_=k.lgt[:]), r=[k.lgt], w=[lgl])

    def gla_state_update(k, d, lT, Sst, ktok_ps_fn):
        lgh, lgl = k.lgh[d], k.lgl[d]
        ps = nps()
        P(lambda e, ps=ps: e.matmul(ps[:], lhsT=lT[:], rhs=lgh[:], start=True, stop=False), r=[lT, lgh], w=[ps])
        P(lambda e, ps=ps: e.matmul(ps[:], lhsT=lT[:], rhs=lgl[:], start=False, stop=True), r=[lT, lgl], w=[ps])
        A(lambda e, ps=ps: e.activation(out=k.kex[:], in_=ps[:], func=AF.Exp), r=[ps], w=[k.kex])
        psk = ktok_ps_fn()
        V(lambda e, psk=psk: e.tensor_tensor(out=k.kend[:], in0=psk[:], in1=k.kex[:], op=ALU.mult), r=[psk, k.kex], w=[k.kend])
        ps = nps()
        for h in range(4):
            P(lambda e, ps=ps, h=h: e.matmul(ps[:, h:h + 1], lhsT=lgh[:, h * 128:(h + 1) * 128], rhs=onesb[:, 0:1], start=True, stop=False),
              r=[lgh, onesb], w=[ps])
            P(lambda e, ps=ps, h=h: e.matmul(ps[:, h:h + 1], lhsT=lgl[:, h * 128:(h + 1) * 128], rhs=onesb[:, 0:1], start=False, stop=True),
              r=[lgl, onesb], w=[ps])
        A(lambda e, ps=ps: e.activation(out=k.cd[:], in_=ps[:, 0:4], func=AF.Exp), r=[ps], w=[k.cd])
        for h2 in range(2):
            ps = nps()
            for hh in range(2):
                h = h2 * 2 + hh
                P(lambda e, ps=ps, h=h, hh=hh: e.matmul(ps[:, hh * 256:(hh + 1) * 256], lhsT=k.kend[:, h * 128:(h + 1) * 128],
                                                        rhs=k.vtok[:, h * 256:(h + 1) * 256], start=True, stop=True),
                  r=[k.kend, k.vtok], w=[ps])
            for hh in range(2):
                h = h2 * 2 + hh
                V(lambda e, ps=ps, h=h, hh=hh: e.scalar_tensor_tensor(out=Sst[:, h * 256:(h + 1) * 256], in0=Sst[:, h * 256:(h + 1) * 256],
                                                                      scalar=k.cd[:, h:h + 1], in1=ps[:, hh * 256:(hh + 1) * 256],
                                                                      op0=ALU.mult, op1=ALU.add), r=[Sst, k.cd, ps], w=[Sst])

    def gla_state_front(k, d, lT, ktok_ps_fn, kend, cd):
        lgh, lgl = k.lgh[d], k.lgl[d]
        ps = nps()
        P(lambda e, ps=ps: e.matmul(ps[:], lhsT=lT[:], rhs=lgh[:], start=True, stop=False), r=[lT, lgh], w=[ps])
        P(lambda e, ps=ps: e.matmul(ps[:], lhsT=lT[:], rhs=lgl[:], start=False, stop=True), r=[lT, lgl], w=[ps])
        A(lambda e, ps=ps: e.activation(out=k.kex[:], in_=ps[:], func=AF.Exp), r=[ps], w=[k.kex])
        psk = ktok_ps_fn()
        V(lambda e, psk=psk: e.tensor_tensor(out=kend[:], in0=psk[:], in1=k.kex[:], op=ALU.mult), r=[psk, k.kex], w=[kend])
        ps = nps()
        for h in range(4):
            P(lambda e, ps=ps, h=h: e.matmul(ps[:, h:h + 1], lhsT=lgh[:, h * 128:(h + 1) * 128], rhs=onesb[:, 0:1], start=True, stop=False),
              r=[lgh, onesb], w=[ps])
            P(lambda e, ps=ps, h=h: e.matmul(ps[:, h:h + 1], lhsT=lgl[:, h * 128:(h + 1) * 128], rhs=onesb[:, 0:1], start=False, stop=True),
              r=[lgl, onesb], w=[ps])
        A(lambda e, ps=ps: e.activation(out=cd[:], in_=ps[:, 0:4], func=AF.Exp), r=[ps], w=[cd])

    def gla_state_back(kend, vtok, cd, Sst):
        for h2 in range(2):
            ps = nps()
            for hh in range(2):
                h = h2 * 2 + hh
                P(lambda e, ps=ps, h=h, hh=hh: e.matmul(ps[:, hh * 256:(hh + 1) * 256], lhsT=kend[:, h * 128:(h + 1) * 128],
                                                        rhs=vtok[:, h * 256:(h + 1) * 256], start=True, stop=True),
                  r=[kend, vtok], w=[ps])
            for hh in range(2):
                h = h2 * 2 + hh
                V(lambda e, ps=ps, h=h, hh=hh: e.scalar_tensor_tensor(out=Sst[:, h * 256:(h + 1) * 256], in0=Sst[:, h * 256:(h + 1) * 256],
                                                                      scalar=cd[:, h:h + 1], in1=ps[:, hh * 256:(hh + 1) * 256],
                                                                      op0=ALU.mult, op1=ALU.add), r=[Sst, cd, ps], w=[Sst])

    def gla_ktok(k):
        ps = nps()
        for kk in range(8):
            P(lambda e, ps=ps, kk=kk: e.matmul(ps[:], lhsT=k.uT[:, kk, :], rhs=k.Win[:, kk, 512:1024], start=(kk == 0), stop=(kk == 7)),
              r=[k.uT, k.Win], w=[ps])
        return ps

    def gla_vtok(k, vtok=None):
        if vtok is None:
            vtok = k.vtok
        for cbk in range(2):
            ps = nps()
            for kk in range(8):
                P(lambda e, ps=ps, kk=kk, cbk=cbk: e.matmul(ps[:], lhsT=k.uT[:, kk, :], rhs=k.Win[:, kk, 1024 + cbk * 512:1024 + (cbk + 1) * 512],
                                                            start=(kk == 0), stop=(kk == 7)), r=[k.uT, k.Win], w=[ps])
            A(lambda e, ps=ps, cbk=cbk, vtok=vtok: e.activation(out=vtok[:, cbk * 512:(cbk + 1) * 512], in_=ps[:], func=AF.Copy), r=[ps], w=[vtok])

    def gla_alloc_work(k):
        k.u = S.tile([128, DM], BF16, "gu")
        k.uT = S.tile([128, 8, 128], BF16, "guT")
        k.lrT = S.tile([16, 128], BF16, "glrT")
        k.lg = [S.tile([128, 512], F32, f"glg{i}") for i in range(2)]
        k.lgh = [S.tile([128, 512], BF16, f"glgh{i}") for i in range(2)]
        k.lgl = [S.tile([128, 512], BF16, f"glgl{i}") for i in range(2)]
        k.lgt = S.tile([128, 512], F32, "glgt")
        k.kex = S.tile([128, 512], F32, "gkex")
        k.kend = S.tile([128, 512], BF16, "gkend")
        k.cd = S.tile([128, 4], F32, "gcd")
        k.vtok = S.tile([128, 1024], BF16, "gvtok")

    def gla_phase_a(Xsrc):
        m0 = S.mark()
        k = gla_common_tiles()
        m1 = S.mark()
        stg = [S.tile([128, 2048], F32, f"stg{i}") for i in range(3)]
        load_w(k.Win, 0, gla_w_in, 0, GLA_IN, 8, stg)
        S.barrier()
        S.release(m1)
        ks = []
        for i in range(2):
            kq = K()
            kq.Win, kq.g_bc, kq.gupb, kq.gbias = k.Win, k.g_bc, k.gupb, k.gbias
            gla_alloc_work(kq)
            ks.append(kq)
        xts = [S.tile([128, DM], F32, f"gax{i}") for i in range(3)]
        Sf = S.tile([128, 1024], F32, "gSf")
        prevbs = [S.tile([128, 1024], BF16, f"gprevb{i}") for i in range(3)]
        vtoks = [S.tile([128, 1024], BF16, f"gavt{i}") for i in range(3)]
        kends = [S.tile([128, 512], BF16, f"gake{i}") for i in range(3)]
        cds = [S.tile([128, 4], F32, f"gacd{i}") for i in range(3)]
        LK = {"front": 0, "back": False, "next_back": 0, "front_done": set()}

        def body(item, my):
            gc, first, last = item
            i3 = my % 3
            kq = ks[my % 2]
            xt, prevb, vtok, kend, cd = xts[i3], prevbs[i3], vtoks[i3], kends[i3], cds[i3]
            if not last:
                LD(xt, xt[:], Xsrc[gc * 128:(gc + 1) * 128, :])
                yield
                rmsnorm(xt, xt[:], kq.g_bc, kq.g_bc[:], kq.u, kq.u[:], DM)
                yield
                yield
                gla_front(kq, xt, False, skip_norm=True)
                yield
                gla_vtok(kq, vtok)
                yield
                gla_state_front(kq, 0, MgtB, lambda: gla_ktok(kq), kend, cd)
                yield
            LK["front"] -= 1
            LK["front_done"].add(my)
            while LK["back"] or LK["next_back"] != my:
                yield
            LK["back"] = True
            if first:
                V(lambda e: e.memset(Sf[:], 0.0), w=[Sf])
            V(lambda e, prevb=prevb: e.tensor_copy(out=prevb[:], in_=Sf[:]), r=[Sf], w=[prevb])
            ST(prevb, g_prev[gc], prevb[:])
            yield
            if not last:
                gla_state_back(kend, vtok, cd, Sf)
            LK["back"] = False
            LK["next_back"] += 1

        items = []
        for (cs, C) in seq_chunks:
            for c in range(C):
                items.append((cs + c, c == 0, c == C - 1))
        active = []
        idx = 0
        while active or idx < len(items):
            if (idx < len(items) and LK["front"] < 2 and len(active) < 3
                    and (idx < 2 or (idx - 2) in LK["front_done"])):
                LK["front"] += 1
                active.append(body(items[idx], idx))
                idx += 1
            for g in list(active):
                try:
                    next(g)
                except StopIteration:
                    active.remove(g)
        S.barrier()
        S.release(m0)

    def gla_phase_b(Xsrc, Xdst):
        m0 = S.mark()
        k = gla_common_tiles()
        Wo = S.tile([128, 8, DM], BF16, "gWo")
        ng = S.tile([128, 256], F32, "gng")
        bcast_load(ng, gla_norm_g[0:1, :], 256)
        TF = S.tile([128, 128], F32, "TF")
        TB = S.tile([128, 128], F32, "TB")
        V(lambda e: e.tensor_tensor(out=TF[:], in0=Mle[:], in1=Mle[:, 64:65].to_broadcast([128, 128]), op=ALU.subtract), r=[Mle], w=[TF])
        V(lambda e: e.tensor_tensor(out=TB[:], in0=Mge[:], in1=Mge[:, 64:65].to_broadcast([128, 128]), op=ALU.subtract), r=[Mge], w=[TB])
        TFb = S.tile([128, 128], BF16, "TFb")
        TBb = S.tile([128, 128], BF16, "TBb")
        V(lambda e: e.tensor_copy(out=TFb[:], in_=TF[:]), r=[TF], w=[TFb])
        V(lambda e: e.tensor_copy(out=TBb[:], in_=TB[:]), r=[TB], w=[TBb])
        m1 = S.mark()
        stg = [S.tile([128, 2048], F32, f"stg{i}") for i in range(3)]
        load_w(k.Win, 0, gla_w_in, 0, GLA_IN, 8, stg)
        load_w(Wo, 0, gla_w_out, 0, DM, 8, stg)
        S.barrier()
        S.release(m1)
        gla_alloc_work(k)
        xts = [S.tile([128, DM], F32, f"gbx{i}") for i in range(2)]
        prevfs = [S.tile([128, 1024], BF16, f"gprevf{i}") for i in range(2)]
        qgs = [[S.tile([128, 4, 128], BF16, f"gqg{i}{d}") for d in range(2)] for i in range(2)]
        ATs = [[S.tile([128, 4, 128], BF16, f"gAT{i}{d}") for d in range(2)] for i in range(2)]
        vtoks = [S.tile([128, 1024], BF16, f"gvt{i}") for i in range(2)]
        gss = [S.tile([128, 1024], BF16, f"ggs{i}") for i in range(2)]
        kends = [S.tile([128, 512], BF16, f"gke{i}") for i in range(2)]
        cds = [S.tile([128, 4], F32, f"gcd{i}") for i in range(2)]
        xos = [S.tile([128, DM], F32, f"gxo{i}") for i in range(2)]
        qk = S.tile([128, 8, 128], F32, "gqk")
        EX = [S.tile([128, 4, 128], F32, f"gEX{i}") for i in range(2)]
        opT = [S.tile([128, 4, 128], BF16, f"gopT{i}") for i in range(4)]
        Sb = S.tile([128, 1024], F32, "gSb")
        prevb = S.tile([128, 1024], BF16, "gprevb")
        o = S.tile([128, 1024], F32, "go")
        osq = S.tile([128, 1024], F32, "gosq")
        rs = S.tile([128, 8], F32, "grs")
        on = S.tile([128, 1024], BF16, "gon")
        onT = S.tile([128, 8, 128], BF16, "gonT")

        def body(item, L):
            gc, first, last = item
            i2 = gc % 2
            xt, prevf, xo, qg, AT, vtok, gs, kend, cd = xts[i2], prevfs[i2], xos[i2], qgs[i2], ATs[i2], vtoks[i2], gss[i2], kends[i2], cds[i2]
            LD(xt, xt[:], Xsrc[gc * 128:(gc + 1) * 128, :])
            LD(prevf, prevf[:], g_prev[gc])
            yield
            rmsnorm(xt, xt[:], k.g_bc, k.g_bc[:], k.u, k.u[:], DM)
            yield
            yield
            yield
            gla_front(k, xt, True, skip_norm=True)
            yield
            gla_vtok(k, vtok)
            yield
            for cbk in range(2):
                ps = nps()
                for kk in range(8):
                    P(lambda e, ps=ps, kk=kk, cbk=cbk: e.matmul(ps[:], lhsT=k.uT[:, kk, :], rhs=k.Win[:, kk, 2048 + cbk * 512:2048 + (cbk + 1) * 512],
                                                                start=(kk == 0), stop=(kk == 7)), r=[k.uT, k.Win], w=[ps])
                A(lambda e, ps=ps, cbk=cbk, gs=gs: e.activation(out=gs[:, cbk * 512:(cbk + 1) * 512], in_=ps[:], func=AF.Silu), r=[ps], w=[gs])
                yield
            for qkk in range(2):
                ps = nps()
                for h in range(4):
                    m = qkk * 4 + h
                    for kk in range(8):
                        P(lambda e, ps=ps, h=h, m=m, kk=kk: e.matmul(ps[:, h * 128:(h + 1) * 128], lhsT=k.Win[:, kk, m * 128:(m + 1) * 128],
                                                                     rhs=k.uT[:, kk, :], start=(kk == 0), stop=(kk == 7)), r=[k.Win, k.uT], w=[ps])
                A(lambda e, ps=ps, qkk=qkk: e.activation(out=qk[:, qkk * 4:qkk * 4 + 4, :], in_=ps[:].rearrange("p (a b) -> p a b", a=4),
                                                         func=AF.Copy, scale=(128.0 ** -0.5 if qkk == 0 else 1.0)), r=[ps], w=[qk])
                yield
            outs = [opT[0], opT[1], opT[2], opT[3], qg[0], qg[1]]
            specs = [(0, TFb, 1.0, 0, 0), (0, TFb, -1.0, 1, 1), (1, TBb, 1.0, 0, 2), (1, TBb, -1.0, 1, 3),
                     (0, MleB, 1.0, 0, 4), (1, MgeB, 1.0, 0, 5)]
            last_key = None
            pse = None
            for si, (d, Tm, sgn, qi, oi) in enumerate(specs):
                if last_key != (d, id(Tm)):
                    ps = nps()
                    for h in range(4):
                        P(lambda e, ps=ps, h=h, d=d, Tm=Tm: e.matmul(ps[:, h * 128:(h + 1) * 128], lhsT=k.lgh[d][:, h * 128:(h + 1) * 128],
                                                                     rhs=Tm[:], start=True, stop=False), r=[k.lgh[d], Tm], w=[ps])
                        P(lambda e, ps=ps, h=h, d=d, Tm=Tm: e.matmul(ps[:, h * 128:(h + 1) * 128], lhsT=k.lgl[d][:, h * 128:(h + 1) * 128],
                                                                     rhs=Tm[:], start=False, stop=True), r=[k.lgl[d], Tm], w=[ps])
                    last_key = (d, id(Tm))
                    pse = ps
                ex = EX[si % 2]
                A(lambda e, pse=pse, ex=ex, sgn=sgn: e.activation(out=ex[:], in_=pse[:].rearrange("p (a b) -> p a b", a=4), func=AF.Exp, scale=sgn),
                  r=[pse], w=[ex])
                ot = outs[oi]
                V(lambda e, ex=ex, ot=ot, qi=qi: e.tensor_tensor(out=ot[:], in0=qk[:, qi * 4:qi * 4 + 4, :], in1=ex[:], op=ALU.mult),
                  r=[qk, ex], w=[ot])
                yield
            for d in range(2):
                ps = nps()
                for h in range(4):
                    P(lambda e, ps=ps, h=h, d=d: e.matmul(ps[:, h * 128:(h + 1) * 128], lhsT=opT[2 * d + 1][:, h, :], rhs=opT[2 * d][:, h, :],
                                                          start=True, stop=True), r=[opT[2 * d + 1], opT[2 * d]], w=[ps])
                msk = Mle if d == 0 else Mge
                V(lambda e, ps=ps, d=d, msk=msk, AT=AT: e.tensor_tensor(out=AT[d][:], in0=ps[:].rearrange("p (a b) -> p a b", a=4),
                                                                       in1=msk[:].unsqueeze(1).to_broadcast([128, 4, 128]), op=ALU.mult),
                  r=[ps, msk], w=[AT[d]])
                yield
            if not last:
                gla_state_front(k, 1, MltB, lambda: gla_ktok(k), kend, cd)
            yield
            L.release_front()
            while not L.back_free():
                yield
            L.acquire_back()
            if first:
                V(lambda e: e.memset(Sb[:], 0.0), w=[Sb])
            V(lambda e: e.tensor_copy(out=prevb[:], in_=Sb[:]), r=[Sb], w=[prevb])
            for h2 in range(2):
                ps = nps()
                for hh in range(2):
                    h = h2 * 2 + hh
                    oslc = ps[:, hh * 256:(hh + 1) * 256]
                    P(lambda e, oslc=oslc, h=h, AT=AT, vtok=vtok: e.matmul(oslc, lhsT=AT[0][:, h, :], rhs=vtok[:, h * 256:(h + 1) * 256], start=True, stop=False),
                      r=[AT[0], vtok], w=[ps])
                    P(lambda e, oslc=oslc, h=h, AT=AT, vtok=vtok: e.matmul(oslc, lhsT=AT[1][:, h, :], rhs=vtok[:, h * 256:(h + 1) * 256], start=False, stop=False),
                      r=[AT[1], vtok], w=[ps])
                    P(lambda e, oslc=oslc, h=h, prevf=prevf, qg=qg: e.matmul(oslc, lhsT=qg[0][:, h, :], rhs=prevf[:, h * 256:(h + 1) * 256], start=False, stop=False),
                      r=[qg[0], prevf], w=[ps])
                    P(lambda e, oslc=oslc, h=h, qg=qg: e.matmul(oslc, lhsT=qg[1][:, h, :], rhs=prevb[:, h * 256:(h + 1) * 256], start=False, stop=True),
                      r=[qg[1], prevb], w=[ps])
                V(lambda e, ps=ps, h2=h2: e.tensor_copy(out=o[:, h2 * 512:(h2 + 1) * 512], in_=ps[:]), r=[ps], w=[o])
                yield
            if not last:
                gla_state_back(kend, vtok, cd, Sb)
            yield
            G(lambda e: e.tensor_tensor(out=osq[:], in0=o[:], in1=o[:], op=ALU.mult), r=[o], w=[osq])
            V(lambda e: e.tensor_reduce(out=rs[:, 0:4], in_=osq[:].rearrange("p (h x) -> p h x", h=4), axis=AX.X, op=ALU.add), r=[osq], w=[rs])
            V(lambda e: e.tensor_scalar(out=rs[:, 4:8], in0=rs[:, 0:4], scalar1=1.0 / 256, scalar2=EPS, op0=ALU.mult, op1=ALU.add), r=[rs], w=[rs])
            A(lambda e: e.activation(out=rs[:, 4:8], in_=rs[:, 4:8], func=AF.Ln), r=[rs], w=[rs])
            A(lambda e: e.activation(out=rs[:, 4:8], in_=rs[:, 4:8], func=AF.Exp, scale=-0.5), r=[rs], w=[rs])
            V(lambda e: e.tensor_tensor(out=o[:].rearrange("p (h x) -> p h x", h=4), in0=o[:].rearrange("p (h x) -> p h x", h=4),
                                        in1=rs[:, 4:8].unsqueeze(2).to_broadcast([128, 4, 256]), op=ALU.mult), r=[o, rs], w=[o])
            G(lambda e: e.tensor_tensor(out=o[:].rearrange("p (h x) -> p h x", h=4), in0=o[:].rearrange("p (h x) -> p h x", h=4),
                                        in1=ng[:].unsqueeze(1).to_broadcast([128, 4, 256]), op=ALU.mult), r=[o, ng], w=[o])
            V(lambda e, gs=gs: e.tensor_tensor(out=on[:], in0=o[:], in1=gs[:], op=ALU.mult), r=[o, gs], w=[on])
            yield
            yield
            yield
            yield
            transpose_to(on, lambda i: on[:, i * 128:(i + 1) * 128], 8, onT, lambda i0, n: onT[:, i0:i0 + n, :])
            yield
            for cbk in range(2):
                ps = nps()
                for kk in range(8):
                    P(lambda e, ps=ps, kk=kk, cbk=cbk: e.matmul(ps[:], lhsT=onT[:, kk, :], rhs=Wo[:, kk, cbk * 512:(cbk + 1) * 512],
                                                                start=(kk == 0), stop=(kk == 7)), r=[onT, Wo], w=[ps])
                V(lambda e, ps=ps, cbk=cbk, xo=xo, xt=xt: e.tensor_tensor(out=xo[:, cbk * 512:(cbk + 1) * 512], in0=ps[:],
                                                                          in1=xt[:, cbk * 512:(cbk + 1) * 512], op=ALU.add), r=[ps, xt], w=[xo])
                yield
            ST(xo, Xdst[gc * 128:(gc + 1) * 128, :], xo[:])
            L.release_back()

        items = []
        for (cs, C) in seq_chunks:
            for c in range(C - 1, -1, -1):
                items.append((cs + c, c == C - 1, c == 0))
        run_pipeline(items, body)
        S.barrier()
        S.release(m0)

    S.barrier()
    phases = [
        lambda: ssd_phase_a(),
        lambda: ssd_phase_b(X1 if stop_after > 2 else y_out),
        lambda: phase_mlp(0, X1, X2 if stop_after > 3 else y_out, False),
        lambda: gla_phase_a(X2),
        lambda: gla_phase_b(X2, X3 if stop_after > 5 else y_out),
        lambda: phase_mlp(1, X3, y_out, True),
    ]
    for i, ph in enumerate(phases):
        if i + 1 > stop_after:
            break
        ph()
    S.barrier()
    S.emit()
    S.close()
    S.dbg_names = dbg_names
    return nc, S


def shard_inputs(inp, seq_per_core):
    f = lambda a: np.ascontiguousarray(np.asarray(a, dtype=np.float32))
    cw = f(inp["ssd_conv_w"])[0]
    cwl = np.ascontiguousarray(cw.T.reshape(32, 128, 7).transpose(1, 0, 2))
    cbl = np.ascontiguousarray(f(inp["ssd_conv_b"])[0].reshape(32, 128).T)
    hp = np.concatenate([f(inp["ssd_dt_bias_f"])[0], f(inp["ssd_dt_bias_b"])[0], f(inp["ssd_a_log_f"])[0],
                         f(inp["ssd_a_log_b"])[0], f(inp["ssd_d"])[0]])[None, :]
    common = {
        "norm_mix_g": f(inp["norm_mix_g"]), "norm_mlp_g": f(inp["norm_mlp_g"]),
        "norm_final_g": f(inp["norm_final_g"])[None, :],
        "norm_mix_gc": np.ascontiguousarray(f(inp["norm_mix_g"]).reshape(2, 8, 128).transpose(0, 2, 1)),
        "norm_mlp_gc": np.ascontiguousarray(f(inp["norm_mlp_g"]).reshape(2, 8, 128).transpose(0, 2, 1)),
        "ssd_norm_gc": np.ascontiguousarray(f(inp["ssd_norm_g"])[0].reshape(16, 128).T),
        "ssd_w_in": f(inp["ssd_w_in"])[0], "ssd_cw": cwl, "ssd_cb": cbl, "ssd_hp": np.ascontiguousarray(hp),
        "ssd_norm_g": f(inp["ssd_norm_g"]), "ssd_w_out": f(inp["ssd_w_out"])[0],
        "gla_w_in": f(inp["gla_w_in"])[0],
        "gla_gate_up": np.ascontiguousarray(np.concatenate([f(inp["gla_gate_up_f"])[0], f(inp["gla_gate_up_b"])[0]], axis=1)),
        "gla_gate_bias": np.ascontiguousarray(np.concatenate([f(inp["gla_gate_bias_f"])[0], f(inp["gla_gate_bias_b"])[0]])[None, :]),
        "gla_norm_g": f(inp["gla_norm_g"]), "gla_w_out": f(inp["gla_w_out"])[0],
        "mlp_w_up": f(inp["mlp_w_up"]), "mlp_w_down": f(inp["mlp_w_down"]),
    }
    maps = []
    for seqs in seq_per_core:
        m = dict(common)
        m["x_in"] = np.ascontiguousarray(np.concatenate([s.reshape(-1, DM) for s in seqs], axis=0))
        maps.append(m)
    return maps


def kernel(**inp):
    xp = np.asarray(inp["x_prompt"], dtype=np.float32)
    xs = np.asarray(inp["x_sample"], dtype=np.float32)
    n = 8
    seq_per_core = [[xp[2 * c], xp[2 * c + 1], xs[c]] for c in range(n)]
    maps = shard_inputs(inp, seq_per_core)
    nc, _ = build(SEQ_LENS)
    res = run_bass_kernel_spmd(nc, maps, core_ids=list(range(n)))
    yp = np.empty_like(xp)
    ys = np.empty_like(xs)
    for c in range(n):
        y = res.results[c]["y_out"]
        yp[2 * c] = y[0:2048]
        yp[2 * c + 1] = y[2048:4096]
        ys[c] = y[4096:12288]
    return (yp, ys)
```

```python
import numpy as np
import concourse.bass as bass
import concourse.mybir as mybir
from concourse.bass_utils import run_bass_kernel_spmd

F32 = mybir.dt.float32
BF16 = mybir.dt.bfloat16
AF = mybir.ActivationFunctionType
ALU = mybir.AluOpType
AX = mybir.AxisListType

DM = 1024
DI = 2048
NH = 32
NGRP = 8
SSD_IN = 6208
GLA_IN = 3104
DFF = 4096
EPS = 1e-5
SEQ_LENS = (2048, 2048, 8192)


class T:
    def __init__(self, h, name):
        self.h = h
        self.name = name
        self.w = None
        self.r = []

    def __getitem__(self, k):
        return self.h[k]


class Sched:
    ENG = ["sync", "tensor", "vector", "scalar", "gpsimd"]

    def __init__(self, nc):
        self.nc = nc
        self.prog = {e: [] for e in self.ENG}
        self.cnt = {}
        self.sems = {}
        self.waited = {e: {} for e in self.ENG}
        self.sem_stack = []
        self.tile_stack = []
        self.ninstr = 0
        self.uid = 0

    def sem(self, key):
        if key not in self.sems:
            cm = self.nc.semaphore("s_" + key)
            self.sems[key] = cm.__enter__()
            self.sem_stack.append(cm)
            self.cnt[key] = 0
        return self.sems[key]

    def tile(self, shape, dt, name, psum=False):
        self.uid += 1
        nm = f"{name}_{self.uid}"
        if psum:
            cm = self.nc.psum_tensor(nm, shape, dt)
        else:
            cm = self.nc.sbuf_tensor(nm, shape, dt)
        h = cm.__enter__()
        self.tile_stack.append(cm)
        return T(h, name)

    def mark(self):
        return len(self.tile_stack)

    def release(self, mark):
        while len(self.tile_stack) > mark:
            self.tile_stack.pop().__exit__(None, None, None)

    def op(self, eng, fn, reads=(), writes=(), dma=False, semkey=None):
        waits = {}

        def need(dep):
            if dep is None:
                return
            k, v = dep
            if k == "tensor" and eng == "tensor":
                return
            if self.waited[eng].get(k, 0) >= v:
                return
            waits[k] = max(waits.get(k, 0), v)

        for t in reads:
            need(t.w)
        for t in writes:
            need(t.w)
            for d in t.r:
                need(d)
        for k, v in waits.items():
            self.waited[eng][k] = v
        if dma:
            key, inc = semkey, 16
        else:
            key, inc = eng, 1
        self.sem(key)
        self.cnt[key] += inc
        me = (key, self.cnt[key])
        for t in reads:
            t.r.append(me)
            if len(t.r) > 64:
                mx = {}
                for k, v in t.r:
                    mx[k] = max(mx.get(k, 0), v)
                t.r = list(mx.items())
        for t in writes:
            t.w = me
            t.r = []
        self.prog[eng].append((fn, list(waits.items()), key, inc))
        self.ninstr += 1
        return me

    def wait_all(self, eng, keys=None):
        waits = []
        for k in (keys if keys is not None else list(self.cnt.keys())):
            v = self.cnt.get(k, 0)
            if v > 0 and self.waited[eng].get(k, 0) < v:
                waits.append((k, v))
                self.waited[eng][k] = v
        self.prog[eng].append((None, waits, None, 0))

    def barrier(self):
        for e in self.ENG:
            self.wait_all(e)

    def emit(self):
        nc = self.nc
        with nc.Block() as block:
            def run(eng_name):
                def body(e):
                    for fn, waits, key, inc in self.prog[eng_name]:
                        for k, v in waits:
                            e.wait_ge(self.sems[k], v)
                        if fn is not None:
                            fn(e).then_inc(self.sems[key], inc)
                return body
            block.sync(run("sync"))
            block.tensor(run("tensor"))
            block.vector(run("vector"))
            block.scalar(run("scalar"))
            block.gpsimd(run("gpsimd"))

    def close(self):
        self.release(0)
        while self.sem_stack:
            self.sem_stack.pop().__exit__(None, None, None)


class K:
    pass


def build(seq_lens=SEQ_LENS, stop_after=99, dbg=False):
    nc = bass.Bass("TRN2", target_bir_lowering=False)
    NT = sum(seq_lens)
    NCH = NT // 128
    seq_chunks = []
    c0 = 0
    for L in seq_lens:
        seq_chunks.append((c0, L // 128))
        c0 += L // 128

    def din(name, shape, dt=F32):
        return nc.dram_tensor(name, list(shape), dt, kind="ExternalInput").ap()

    x_in = din("x_in", [NT, DM])
    norm_mix_g = din("norm_mix_g", [2, DM])
    norm_mlp_g = din("norm_mlp_g", [2, DM])
    norm_final_g = din("norm_final_g", [1, DM])
    ssd_w_in = din("ssd_w_in", [DM, SSD_IN])
    ssd_cw = din("ssd_cw", [128, 32, 7])
    ssd_cb = din("ssd_cb", [128, 32])
    ssd_hp = din("ssd_hp", [1, 5 * 32])
    ssd_norm_g = din("ssd_norm_g", [1, DI])
    ssd_w_out = din("ssd_w_out", [DI, DM])
    gla_w_in = din("gla_w_in", [DM, GLA_IN])
    gla_gate_up = din("gla_gate_up", [16, 1024])
    gla_gate_bias = din("gla_gate_bias", [1, 1024])
    gla_norm_g = din("gla_norm_g", [1, 256])
    gla_w_out = din("gla_w_out", [DM, DM])
    mlp_w_up = din("mlp_w_up", [2, DM, DFF])
    mlp_w_down = din("mlp_w_down", [2, DFF, DM])
    norm_mix_gc = din("norm_mix_gc", [2, 128, 8])
    norm_mlp_gc = din("norm_mlp_gc", [2, 128, 8])
    ssd_norm_gc = din("ssd_norm_gc", [128, 16])
    y_out = nc.dram_tensor("y_out", [NT, DM], F32, kind="ExternalOutput").ap()
    ndbg = 24
    if dbg:
        dbg32 = nc.dram_tensor("dbg32", [ndbg, 128, 2048], F32, kind="ExternalOutput").ap()
        dbg16 = nc.dram_tensor("dbg16", [ndbg, 128, 2048], BF16, kind="ExternalOutput").ap()

    def dscr(name, shape, dt):
        return nc.dram_tensor(name, list(shape), dt).ap()

    X1 = dscr("X1", [NT, DM], F32)
    X2 = dscr("X2", [NT, DM], F32)
    X3 = dscr("X3", [NT, DM], F32)
    s_xtok = dscr("s_xtok", [NT, DI], BF16)
    s_btok = dscr("s_btok", [NT, 1024], BF16)
    s_bct = dscr("s_bct", [NCH, 128, 2048], BF16)
    s_small = dscr("s_small", [NT, 192], F32)
    s_prev = dscr("s_prev", [NCH, 128, 2048], BF16)
    g_prev = dscr("g_prev", [NCH, 128, 1024], BF16)
    s_acs = dscr("s_acs", [NCH, 128, 128], BF16)

    S = Sched(nc)
    for e in Sched.ENG:
        S.sem(e)

    def V(fn, r=(), w=()):
        S.op("vector", fn, reads=r, writes=w)

    def A(fn, r=(), w=()):
        S.op("scalar", fn, reads=r, writes=w)

    def G(fn, r=(), w=()):
        S.op("gpsimd", fn, reads=r, writes=w)

    def P(fn, r=(), w=()):
        S.op("tensor", fn, reads=r, writes=w)

    def LD(out_t, out_ap, in_ap, eng="sync"):
        S.op(eng, lambda e: e.dma_start(out=out_ap, in_=in_ap), writes=[out_t], dma=True,
             semkey="ld_" + out_t.name)

    def ST(in_t, out_ap, in_ap, eng="gpsimd"):
        S.op(eng, lambda e: e.dma_start(out=out_ap, in_=in_ap), reads=[in_t], dma=True,
             semkey="st_" + in_t.name)

    PS = [S.tile([128, 512], F32, f"ps{i}", psum=True) for i in range(6)]
    PSB = [S.tile([128, 1024], BF16, f"psb{i}", psum=True) for i in range(2)]
    st_ps = {"i": 0, "b": 0}

    def nps():
        st_ps["i"] += 1
        return PS[st_ps["i"] % 6]

    def npsb():
        st_ps["b"] += 1
        return PSB[st_ps["b"] % 2]

    identf = S.tile([128, 128], F32, "identf")
    identb = S.tile([128, 128], BF16, "identb")
    Mle = S.tile([128, 128], F32, "Mle")
    Mge = S.tile([128, 128], F32, "Mge")
    Mgt = S.tile([128, 128], F32, "Mgt")
    Mlt = S.tile([128, 128], F32, "Mlt")
    onesf = S.tile([128, 128], F32, "onesf")
    MleB = S.tile([128, 128], BF16, "MleB")
    MgeB = S.tile([128, 128], BF16, "MgeB")

    def tri(t, pat, cm, cmp):
        G(lambda e: e.memset(t[:], 1.0), w=[t])
        G(lambda e: e.affine_select(out=t[:], in_=t[:], pattern=[[pat, 128]], compare_op=cmp, fill=0.0,
                                    base=0, channel_multiplier=cm), r=[t], w=[t])

    tri(identf, -1, 1, ALU.is_equal)
    tri(Mle, 1, -1, ALU.is_ge)
    tri(Mge, -1, 1, ALU.is_ge)
    tri(Mgt, -1, 1, ALU.is_gt)
    tri(Mlt, 1, -1, ALU.is_gt)
    G(lambda e: e.memset(onesf[:], 1.0), w=[onesf])
    V(lambda e: e.tensor_copy(out=identb[:], in_=identf[:]), r=[identf], w=[identb])
    V(lambda e: e.tensor_copy(out=MleB[:], in_=Mle[:]), r=[Mle], w=[MleB])
    V(lambda e: e.tensor_copy(out=MgeB[:], in_=Mge[:]), r=[Mge], w=[MgeB])
    MgtB = S.tile([128, 128], BF16, "MgtB")
    MltB = S.tile([128, 128], BF16, "MltB")
    onesb = S.tile([128, 128], BF16, "onesb")
    V(lambda e: e.tensor_copy(out=MgtB[:], in_=Mgt[:]), r=[Mgt], w=[MgtB])
    V(lambda e: e.tensor_copy(out=MltB[:], in_=Mlt[:]), r=[Mlt], w=[MltB])
    V(lambda e: e.tensor_copy(out=onesb[:], in_=onesf[:]), r=[onesf], w=[onesb])

    dbg_i = {"i": 0}
    dbg_names = {}

    def DBG(name, t, ap, ncols, parts=128, dt=F32):
        if not dbg:
            return
        i = dbg_i["i"]
        dbg_i["i"] += 1
        assert i < ndbg
        dbg_names[name] = (i, dt == F32)
        dst = dbg32 if dt == F32 else dbg16
        ST(t, dst[i][0:parts, 0:ncols], ap)

    def load_w(dst, dcol0, src, scol0, ncols, K, stg, rowscale=None):
        srcv = src.rearrange("(k p) c -> p k c", p=128)
        engs = ["vector", "scalar", "gpsimd", "vector", "scalar"]
        i = 0
        c = 0
        while c < ncols:
            cc = min(512, ncols - c)
            kk_max = max(1, 2048 // cc)
            k0 = 0
            while k0 < K:
                kk = min(kk_max, K - k0)
                st = stg[i % len(stg)]
                stv = st[:, 0:kk * cc].rearrange("p (k c) -> p k c", k=kk)
                LD(st, stv, srcv[:, k0:k0 + kk, scol0 + c:scol0 + c + cc])
                eng = engs[i % 5]
                if rowscale is None:
                    groups = [(stv, dst[:, k0:k0 + kk, dcol0 + c:dcol0 + c + cc], None)]
                else:
                    groups = [(stv[:, q, :], dst[:, k0 + q, dcol0 + c:dcol0 + c + cc], rowscale[:, k0 + q:k0 + q + 1]) for q in range(kk)]
                for (sv, dv, rs) in groups:
                    rd = [st] if rs is None else [st, rowscale]
                    if eng == "scalar":
                        if rs is None:
                            A(lambda e, dv=dv, sv=sv: e.activation(out=dv, in_=sv, func=AF.Copy), r=rd, w=[])
                        else:
                            A(lambda e, dv=dv, sv=sv, rs=rs: e.activation(out=dv, in_=sv, func=AF.Copy, scale=rs), r=rd, w=[])
                    else:
                        fnE = V if eng == "vector" else G
                        if rs is None:
                            fnE(lambda e, dv=dv, sv=sv: e.tensor_copy(out=dv, in_=sv), r=rd, w=[])
                        else:
                            fnE(lambda e, dv=dv, sv=sv, rs=rs: e.tensor_scalar(out=dv, in0=sv, scalar1=rs, scalar2=None, op0=ALU.mult), r=rd, w=[])
                i += 1
                k0 += kk
            c += cc

    ssr = [S.tile([128, 8], F32, f"ssr{i}") for i in range(4)]
    ssr_i = {"i": 0}
    junk = S.tile([128, 2048], BF16, "junk")

    def rmsnorm(xt, xap, g_t, gap, ut, uap, Dn):
        ssr_i["i"] += 1
        ss = ssr[ssr_i["i"] % 4]
        V(lambda e: e.memset(ss[:, 0:2], 0.0), w=[ss])
        A(lambda e: e.activation(out=junk[:, 0:Dn], in_=xap, func=AF.Square, accum_out=ss[:, 0:1]),
          r=[xt, ss], w=[junk, ss])
        V(lambda e: e.tensor_scalar(out=ss[:, 1:2], in0=ss[:, 0:1], scalar1=1.0 / Dn, scalar2=EPS,
                                    op0=ALU.mult, op1=ALU.add), r=[ss], w=[ss])
        A(lambda e: e.activation(out=ss[:, 1:2], in_=ss[:, 1:2], func=AF.Ln), r=[ss], w=[ss])
        A(lambda e: e.activation(out=ss[:, 1:2], in_=ss[:, 1:2], func=AF.Exp, scale=-0.5), r=[ss], w=[ss])
        if g_t is None:
            V(lambda e: e.tensor_scalar(out=uap, in0=xap, scalar1=ss[:, 1:2], scalar2=None, op0=ALU.mult), r=[xt, ss], w=[ut])
        else:
            V(lambda e: e.scalar_tensor_tensor(out=uap, in0=xap, scalar=ss[:, 1:2], in1=gap,
                                               op0=ALU.mult, op1=ALU.mult), r=[xt, ss, g_t], w=[ut])

    def transpose_to(src_t, src_ap_fn, ntiles, dst_t, dst_ap_fn):
        i = 0
        while i < ntiles:
            n = min(8, ntiles - i)
            pb = npsb()
            for j in range(n):
                P(lambda e, pb=pb, j=j, i=i: e.transpose(pb[:, j * 128:(j + 1) * 128], src_ap_fn(i + j), identb[:]),
                  r=[src_t, identb], w=[pb])
            V(lambda e, pb=pb, n=n, i=i: e.tensor_copy(out=dst_ap_fn(i, n), in_=pb[:, 0:n * 128].rearrange("p (a b) -> p a b", a=n)), r=[pb], w=[dst_t])
            i += n

    class Locks:
        def __init__(self):
            self.front = False
            self.back = False

        def release_front(self):
            self.front = False

        def back_free(self):
            return not self.back

        def acquire_back(self):
            self.back = True

        def release_back(self):
            self.back = False

    def run_pipeline(items, body, depth=2):
        L = Locks()
        active = []
        idx = 0
        while active or idx < len(items):
            if idx < len(items) and not L.front and len(active) < depth:
                L.front = True
                active.append(body(items[idx], L))
                idx += 1
            for g in list(active):
                try:
                    next(g)
                except StopIteration:
                    active.remove(g)

    def bcast_load(t, ap_row, n):
        LD(t, t[:, 0:n], ap_row.partition_broadcast(128))

    def phase_mlp(layer, Xsrc, Xdst, final):
        m0 = S.mark()
        Wup = S.tile([128, 8, DFF], BF16, "Wup")
        Wdn = S.tile([128, 32, DM], BF16, "Wdn")
        gcm = S.tile([128, 8], F32, "gcm")
        LD(gcm, gcm[:], norm_mlp_gc[layer])
        if final:
            gf_bc = S.tile([128, DM], F32, "gf_bc")
            bcast_load(gf_bc, norm_final_g[0:1, :], DM)
        m1 = S.mark()
        stg = [S.tile([128, 2048], F32, f"stg{i}") for i in range(3)]
        load_w(Wup, 0, mlp_w_up[layer], 0, DFF, 8, stg, rowscale=gcm)
        load_w(Wdn, 0, mlp_w_down[layer], 0, DM, 32, stg)
        S.barrier()
        S.release(m1)
        TT = 256
        xts = [S.tile([128, 2, DM], F32, f"mx{i}") for i in range(2)]
        xos = [S.tile([128, 2, DM], F32, "mo0")] * 2
        hTs = [S.tile([128, 32, TT], BF16, f"mhT{i}") for i in range(2)]
        u = S.tile([128, DM], BF16, "mu")
        uT = S.tile([128, 8, TT], BF16, "muT")
        rl = [S.tile([128, 512], F32, f"mrl{i}") for i in range(2)]

        def body(t, L):
            xt, xo, hT = xts[t % 2], xos[t % 2], hTs[t % 2]
            LD(xt, xt[:], Xsrc[t * TT:(t + 1) * TT, :].rearrange("(j p) d -> p j d", p=128))
            yield
            for j in range(2):
                rmsnorm(xt, xt[:, j, :], None, None, u, u[:], DM)
                yield
                yield
                transpose_to(u, lambda i: u[:, i * 128:(i + 1) * 128], 8, uT,
                             lambda i0, n, j=j: uT[:, i0:i0 + n, j * 128:(j + 1) * 128])
                yield
            for fp in range(16):
                ps = nps()
                for f2 in range(2):
                    f = fp * 2 + f2
                    for k in range(8):
                        P(lambda e, ps=ps, f=f, f2=f2, k=k: e.matmul(ps[:, f2 * TT:(f2 + 1) * TT],
                                                                     lhsT=Wup[:, k, f * 128:(f + 1) * 128],
                                                                     rhs=uT[:, k, :], start=(k == 0), stop=(k == 7)),
                          r=[Wup, uT], w=[ps])
                r_ = rl[fp % 2]
                A(lambda e, ps=ps, r_=r_: e.activation(out=r_[:], in_=ps[:], func=AF.Relu), r=[ps], w=[r_])
                V(lambda e, r_=r_, fp=fp, hT=hT: e.tensor_tensor(out=hT[:, fp * 2:fp * 2 + 2, :],
                                                                 in0=r_[:].rearrange("p (a b) -> p a b", a=2),
                                                                 in1=r_[:].rearrange("p (a b) -> p a b", a=2), op=ALU.mult),
                  r=[r_], w=[hT])
                yield
            L.release_front()
            while not L.back_free():
                yield
            L.acquire_back()
            for j in range(2):
                for cb in range(2):
                    ps = nps()
                    for f in range(32):
                        P(lambda e, ps=ps, f=f, j=j, cb=cb, hT=hT: e.matmul(ps[:], lhsT=hT[:, f, j * 128:(j + 1) * 128],
                                                                            rhs=Wdn[:, f, cb * 512:(cb + 1) * 512],
                                                                            start=(f == 0), stop=(f == 31)),
                          r=[hT, Wdn], w=[ps])
                        if f % 8 == 7 and f != 31:
                            yield
                    V(lambda e, ps=ps, j=j, cb=cb, xo=xo, xt=xt: e.tensor_tensor(
                        out=xo[:, j, cb * 512:(cb + 1) * 512], in0=ps[:], in1=xt[:, j, cb * 512:(cb + 1) * 512],
                        op=ALU.add), r=[ps, xt], w=[xo])
                    yield
                if final:
                    rmsnorm(xo, xo[:, j, :], gf_bc, gf_bc[:], xo, xo[:, j, :], DM)
            ST(xo, Xdst[t * TT:(t + 1) * TT, :].rearrange("(j p) d -> p j d", p=128), xo[:])
            L.release_back()

        run_pipeline(list(range(NT // TT)), body)
        S.barrier()
        S.release(m0)

    def ssd_small_consts():
        hp = S.tile([128, 160], F32, "hp")
        bcast_load(hp, ssd_hp[0:1, :], 160)
        A(lambda e: e.activation(out=hp[:, 64:128], in_=hp[:, 64:128], func=AF.Exp), r=[hp], w=[hp])
        V(lambda e: e.tensor_scalar(out=hp[:, 64:128], in0=hp[:, 64:128], scalar1=-1.0, scalar2=None, op0=ALU.mult),
          r=[hp], w=[hp])
        return hp

    def ssd_phase_a():
        m0 = S.mark()
        Wx = S.tile([128, 8, 4160], BF16, "Wx")
        diag = S.tile([128, 32, 7, 128], BF16, "diag")
        cb = S.tile([128, 32], F32, "cb")
        gcx = S.tile([128, 8], F32, "gcx")
        hp = ssd_small_consts()
        LD(cb, cb[:], ssd_cb)
        LD(gcx, gcx[:], norm_mix_gc[0])
        m1 = S.mark()
        cw = S.tile([128, 32, 7], F32, "cw")
        LD(cw, cw[:], ssd_cw)
        stg = [S.tile([128, 2048], F32, f"stg{i}") for i in range(2)]
        load_w(Wx, 0, ssd_w_in, DI, 4096, 8, stg, rowscale=gcx)
        load_w(Wx, 4096, ssd_w_in, DI + 4096, 64, 8, stg, rowscale=gcx)
        di = 0
        for m in range(32):
            for j in range(7):
                di += 1
                if di % 5 in (0, 2):
                    V(lambda e, m=m, j=j: e.tensor_scalar(out=diag[:, m, j, :], in0=identf[:], scalar1=cw[:, m, j:j + 1],
                                                          scalar2=None, op0=ALU.mult), r=[identf, cw], w=[])
                elif di % 5 in (1, 3):
                    A(lambda e, m=m, j=j: e.activation(out=diag[:, m, j, :], in_=identf[:], func=AF.Copy, scale=cw[:, m, j:j + 1]),
                      r=[identf, cw], w=[])
                else:
                    G(lambda e, m=m, j=j: e.tensor_scalar(out=diag[:, m, j, :], in0=identf[:], scalar1=cw[:, m, j:j + 1],
                                                          scalar2=None, op0=ALU.mult), r=[identf, cw], w=[])
        S.barrier()
        S.release(m1)
        xt = S.tile([128, DM], F32, "ax")
        u = S.tile([128, DM], BF16, "au")
        uT = S.tile([128, 8, 256], BF16, "auT")
        pre = [S.tile([128, 32, 262], BF16, f"pre{i}") for i in range(2)]
        lh = S.tile([128, 32, 3], BF16, "lh")
        xbcT = S.tile([128, 16, 256], BF16, "xbcT")
        xtok = S.tile([128, DI], BF16, "xtok")
        btoks = [S.tile([128, 1024], BF16, f"btok{i}") for i in range(2)]
        sms = [S.tile([128, 192], F32, f"sm{i}") for i in range(6)]
        ex = S.tile([128, 64], F32, "ex")
        wf = S.tile([128, 32], F32, "wf")
        Sf = S.tile([128, DI], F32, "Sf")
        prevb = S.tile([128, DI], BF16, "prevb")
        acs = S.tile([128, 64], F32, "acs")
        acshl = S.tile([128, 128], BF16, "acshl")
        acsT = S.tile([128, 128], BF16, "acsT")

        for (cs, C) in seq_chunks:
            NSC = C // 2
            V(lambda e: e.memset(Sf[:], 0.0), w=[Sf])

            def stage1_a1(sc, jj):
                gc = cs + 2 * sc + jj
                LD(xt, xt[:], x_in[gc * 128:(gc + 1) * 128, :])
                rmsnorm(xt, xt[:], None, None, u, u[:], DM)

            def stage1(sc, hoisted):
                sl = pre[sc % 2]
                for jj in range(2):
                    gc = cs + 2 * sc + jj
                    sm = sms[gc % 6]
                    if not (jj == 0 and hoisted):
                        stage1_a1(sc, jj)
                    transpose_to(u, lambda i: u[:, i * 128:(i + 1) * 128], 8, uT,
                                 lambda i0, n, jj=jj: uT[:, i0:i0 + n, jj * 128:(jj + 1) * 128])
                    ps = nps()
                    for k in range(8):
                        P(lambda e, ps=ps, k=k, jj=jj: e.matmul(ps[:, 0:64], lhsT=uT[:, k, jj * 128:(jj + 1) * 128], rhs=Wx[:, k, 4096:4160],
                                                                start=(k == 0), stop=(k == 7)), r=[Wx, uT], w=[ps])
                    V(lambda e, ps=ps, sm=sm: e.tensor_tensor(out=sm[:, 0:64], in0=ps[:, 0:64], in1=hp[:, 0:64], op=ALU.add),
                      r=[ps, hp], w=[sm])
                    A(lambda e, sm=sm: e.activation(out=sm[:, 0:64], in_=sm[:, 0:64], func=AF.Exp), r=[sm], w=[sm])
                    A(lambda e, sm=sm: e.activation(out=sm[:, 0:64], in_=sm[:, 0:64], func=AF.Ln, bias=1.0), r=[sm], w=[sm])
                    A(lambda e, sm=sm: e.activation(out=sm[:, 64:128], in_=sm[:, 0:64], func=AF.Ln), r=[sm], w=[sm])
                    V(lambda e, sm=sm: e.tensor_tensor(out=sm[:, 128:192], in0=sm[:, 0:64], in1=hp[:, 64:128], op=ALU.mult),
                      r=[sm, hp], w=[sm])
                    yield
                if sc == 0:
                    G(lambda e, sl=sl: e.memset(sl[:, :, 0:3], 0.0), w=[sl])
                else:
                    G(lambda e, sl=sl: e.tensor_copy(out=sl[:, :, 0:3], in_=lh[:]), r=[lh], w=[sl])
                for mb in range(16):
                    ps = nps()
                    for m2 in range(2):
                        m = mb * 2 + m2
                        for k in range(8):
                            P(lambda e, ps=ps, m=m, m2=m2, k=k: e.matmul(ps[:, m2 * 256:(m2 + 1) * 256],
                                                                         lhsT=Wx[:, k, m * 128:(m + 1) * 128],
                                                                         rhs=uT[:, k, :], start=(k == 0), stop=(k == 7)),
                              r=[Wx, uT], w=[ps])
                    A(lambda e, ps=ps, mb=mb, sl=sl: e.activation(out=sl[:, mb * 2:mb * 2 + 2, 3:259],
                                                                  in_=ps[:].rearrange("p (a b) -> p a b", a=2),
                                                                  func=AF.Copy), r=[ps], w=[sl])
                    yield
                G(lambda e, sl=sl: e.tensor_copy(out=lh[:], in_=sl[:, :, 256:259]), r=[sl], w=[lh])
                if sc > 0:
                    slp = pre[(sc - 1) % 2]
                    G(lambda e, sl=sl, slp=slp: e.tensor_copy(out=slp[:, :, 259:262], in_=sl[:, :, 3:6]), r=[sl], w=[slp])
                if sc == NSC - 1:
                    G(lambda e, sl=sl: e.memset(sl[:, :, 259:262], 0.0), w=[sl])

            def conv_half(sc, half):
                sl = pre[sc % 2]
                for mb in range(8):
                    ps = nps()
                    for m2 in range(2):
                        m = half * 16 + mb * 2 + m2
                        for j in range(7):
                            P(lambda e, ps=ps, m=m, m2=m2, j=j, sl=sl: e.matmul(ps[:, m2 * 256:(m2 + 1) * 256],
                                                                                lhsT=diag[:, m, j, :], rhs=sl[:, m, j:j + 256],
                                                                                start=(j == 0), stop=(j == 6)),
                              r=[diag, sl], w=[ps])
                    for m2 in range(2):
                        m = half * 16 + mb * 2 + m2
                        A(lambda e, ps=ps, m=m, m2=m2, mb=mb: e.activation(out=xbcT[:, mb * 2 + m2, :], in_=ps[:, m2 * 256:(m2 + 1) * 256],
                                                                           func=AF.Silu, bias=cb[:, m:m + 1]),
                          r=[ps, cb], w=[xbcT])

            def conv_part(sc):
                conv_half(sc, 1)
                for jj in range(2):
                    gc = cs + 2 * sc + jj
                    btok = btoks[jj]
                    transpose_to(xbcT, lambda i, jj=jj: xbcT[:, i, jj * 128:(jj + 1) * 128], 8, btok,
                                 lambda i0, n, btok=btok: btok[:, i0 * 128:(i0 + n) * 128].rearrange("p (a b) -> p a b", a=n))
                    ST(btok, s_btok[gc * 128:(gc + 1) * 128, :], btok[:])
                    ST(xbcT, s_bct[gc].rearrange("p (a b) -> p a b", a=16), xbcT[:, :, jj * 128:(jj + 1) * 128])
                conv_half(sc, 0)

            def tails(sc):
                for jj in range(2):
                    gc = cs + 2 * sc + jj
                    sm = sms[gc % 6]
                    btok = btoks[jj]
                    transpose_to(xbcT, lambda i, jj=jj: xbcT[:, i, jj * 128:(jj + 1) * 128], 16, xtok,
                                 lambda i0, n: xtok[:, i0 * 128:(i0 + n) * 128].rearrange("p (a b) -> p a b", a=n))
                    ST(xtok, s_xtok[gc * 128:(gc + 1) * 128, :], xtok[:])
                    yield
                    ps = nps()
                    P(lambda e, ps=ps, sm=sm: e.matmul(ps[:, 0:32], lhsT=Mle[:], rhs=sm[:, 128:160], start=True, stop=True), r=[Mle, sm], w=[ps])
                    P(lambda e, ps=ps, sm=sm: e.matmul(ps[:, 32:64], lhsT=Mge[:], rhs=sm[:, 160:192], start=True, stop=True), r=[Mge, sm], w=[ps])
                    V(lambda e, ps=ps: e.tensor_copy(out=acs[:], in_=ps[:, 0:64]), r=[ps], w=[acs])
                    V(lambda e, sm=sm: e.tensor_tensor(out=sm[:, 64:128], in0=sm[:, 64:128], in1=acs[:], op=ALU.subtract), r=[sm, acs], w=[sm])
                    ST(sm, s_small[gc * 128:(gc + 1) * 128, :], sm[:])
                    yield
                    V(lambda e: e.tensor_copy(out=acshl[:, 0:64], in_=acs[:]), r=[acs], w=[acshl])
                    V(lambda e: e.tensor_tensor(out=acs[:], in0=acs[:], in1=acshl[:, 0:64], op=ALU.subtract), r=[acs, acshl], w=[acs])
                    V(lambda e: e.tensor_copy(out=acshl[:, 64:128], in_=acs[:]), r=[acs], w=[acshl])
                    pb = npsb()
                    P(lambda e, pb=pb: e.transpose(pb[:, 0:128], acshl[:], identb[:]), r=[acshl, identb], w=[pb])
                    V(lambda e, pb=pb: e.tensor_copy(out=acsT[:], in_=pb[:, 0:128]), r=[pb], w=[acsT])
                    ST(acsT, s_acs[gc], acsT[:])
                    yield
                    ps = nps()
                    P(lambda e, ps=ps, sm=sm: e.matmul(ps[:, 0:32], lhsT=Mgt[:], rhs=sm[:, 128:160], start=True, stop=True),
                      r=[Mgt, sm], w=[ps])
                    P(lambda e, ps=ps, sm=sm: e.matmul(ps[:, 32:64], lhsT=onesf[:], rhs=sm[:, 128:160], start=True, stop=True),
                      r=[onesf, sm], w=[ps])
                    A(lambda e, ps=ps: e.activation(out=ex[:], in_=ps[:, 0:64], func=AF.Exp), r=[ps], w=[ex])
                    V(lambda e, sm=sm: e.tensor_tensor(out=wf[:], in0=sm[:, 0:32], in1=ex[:, 0:32], op=ALU.mult), r=[sm, ex], w=[wf])
                    V(lambda e: e.tensor_tensor(out=xtok[:].rearrange("p (h d) -> p h d", h=32),
                                                in0=xtok[:].rearrange("p (h d) -> p h d", h=32),
                                                in1=wf[:].unsqueeze(2).to_broadcast([128, 32, 64]), op=ALU.mult),
                      r=[xtok, wf], w=[xtok])
                    yield
                    V(lambda e: e.tensor_copy(out=prevb[:], in_=Sf[:]), r=[Sf], w=[prevb])
                    ST(prevb, s_prev[gc], prevb[:])
                    V(lambda e: e.tensor_tensor(out=Sf[:].rearrange("p (h d) -> p h d", h=32),
                                                in0=Sf[:].rearrange("p (h d) -> p h d", h=32),
                                                in1=ex[:, 32:64].unsqueeze(2).to_broadcast([128, 32, 64]), op=ALU.mult),
                      r=[Sf, ex], w=[Sf])
                    for g2 in range(4):
                        ps = nps()
                        for gg in range(2):
                            g = g2 * 2 + gg
                            P(lambda e, ps=ps, g=g, gg=gg, btok=btok: e.matmul(ps[:, gg * 256:(gg + 1) * 256],
                                                                              lhsT=btok[:, g * 128:(g + 1) * 128],
                                                                              rhs=xtok[:, g * 256:(g + 1) * 256], start=True, stop=True),
                              r=[btok, xtok], w=[ps])
                        V(lambda e, ps=ps, g2=g2: e.tensor_tensor(out=Sf[:, g2 * 512:(g2 + 1) * 512],
                                                                  in0=Sf[:, g2 * 512:(g2 + 1) * 512], in1=ps[:], op=ALU.add),
                          r=[Sf, ps], w=[Sf])
                        yield

            def interleave(ga, gb):
                gens = [g for g in (ga, gb) if g is not None]
                while gens:
                    for g in list(gens):
                        try:
                            next(g)
                        except StopIteration:
                            gens.remove(g)

            interleave(stage1(0, False), None)
            if NSC > 1:
                stage1_a1(1, 0)
            for sc in range(NSC):
                ga = stage1(sc + 1, True) if sc + 1 < NSC else None
                gb = tails(sc - 1) if sc >= 1 else None
                interleave(ga, gb)
                if sc + 2 < NSC:
                    stage1_a1(sc + 2, 0)
                conv_part(sc)
            interleave(tails(NSC - 1), None)
        S.barrier()
        S.release(m0)

    def ssd_phase_b(Xdst):
        m0 = S.mark()
        Wz = S.tile([128, 8, DI], BF16, "Wz")
        Wo = S.tile([128, 16, DM], BF16, "Wo")
        gcz = S.tile([128, 8], F32, "gcz")
        gco = S.tile([128, 16], F32, "gco")
        hp = ssd_small_consts()
        LD(gcz, gcz[:], norm_mix_gc[0])
        LD(gco, gco[:], ssd_norm_gc)
        m1 = S.mark()
        stg = [S.tile([128, 2048], F32, f"stg{i}") for i in range(3)]
        load_w(Wz, 0, ssd_w_in, 0, DI, 8, stg, rowscale=gcz)
        load_w(Wo, 0, ssd_w_out, 0, DM, 16, stg, rowscale=gco)
        S.barrier()
        S.release(m1)
        xts = [S.tile([128, DM], F32, f"bx{i}") for i in range(2)]
        xtoks = [S.tile([128, DI], BF16, f"bxtok{i}") for i in range(2)]
        btoks = [S.tile([128, 1024], BF16, f"bbtok{i}") for i in range(2)]
        bcts = [S.tile([128, 16, 128], BF16, f"bbct{i}") for i in range(2)]
        smsb = [S.tile([128, 192], F32, f"bsm{i}") for i in range(2)]
        prevfs = [S.tile([128, DI], BF16, f"bprevf{i}") for i in range(2)]
        zss = [S.tile([128, DI], BF16, f"zs{i}") for i in range(2)]
        evs = [S.tile([128, 128], F32, f"ev{i}") for i in range(2)]
        yaccs = [S.tile([128, DI], F32, f"yacc{i}") for i in range(2)]
        xos = [S.tile([128, DM], F32, f"bxo{i}") for i in range(2)]
        RB = S.tile([128, 3072], BF16, "RB")
        u = S.tile([128, DM], BF16, "bu")
        uT = S.tile([128, 8, 128], BF16, "buT")
        sc = S.tile([128, 8, 128], BF16, "sc")
        Wr = [S.tile([128, 4, 128], BF16, f"Wr{i}") for i in range(8)]
        mneg = [S.tile([128, 4, 128], BF16, f"mneg{i}") for i in range(2)]
        for d, Mm in enumerate([Mle, Mge]):
            V(lambda e, d=d, Mm=Mm: e.tensor_scalar(out=mneg[d][:], in0=Mm[:].unsqueeze(1).to_broadcast([128, 4, 128]), scalar1=-1.0, scalar2=30000.0,
                                                    op0=ALU.add, op1=ALU.mult), r=[Mm], w=[mneg[d]])
        t1r = [S.tile([128, 512], F32, f"t1r{i}") for i in range(2)]
        wb = S.tile([128, 32], F32, "wb")
        xw = S.tile([128, DI], BF16, "bxw")
        Sb = S.tile([128, DI], F32, "Sb")
        prevb = S.tile([128, DI], BF16, "bprevb")
        yn = S.tile([128, DI], BF16, "yn")
        ynT = S.tile([128, 16, 128], BF16, "ynT")
        st = {"ri": 0, "ti": 0}

        def body(item, L):
            gc, first = item
            i2 = gc % 2
            xt, xtok, btok, bct, sm, prevf, zs, ev, yacc, xo = (xts[i2], xtoks[i2], btoks[i2], bcts[i2], smsb[i2], prevfs[i2],
                                                                zss[i2], evs[i2], yaccs[i2], xos[i2])
            LD(xt, xt[:], x_in[gc * 128:(gc + 1) * 128, :])
            LD(sm, sm[:], s_small[gc * 128:(gc + 1) * 128, :])
            LD(bct, bct[:].rearrange("p a b -> p (a b)"), s_bct[gc])
            acv = s_acs[gc].rearrange("(hl u h4) l -> hl u (h4 l)", hl=2, u=16)
            for b in range(3):
                n = len(range(b, 16, 3))
                for hl in range(2):
                    LD(RB, RB[32 * b + hl:32 * b + hl + 1, 0:n * 512].rearrange("p (s x) -> p s x", s=n), acv[hl:hl + 1, b::3, :])
            LD(xtok, xtok[:], s_xtok[gc * 128:(gc + 1) * 128, :])
            LD(btok, btok[:], s_btok[gc * 128:(gc + 1) * 128, :])
            LD(prevf, prevf[:], s_prev[gc])
            yield
            rmsnorm(xt, xt[:], None, None, u, u[:], DM)
            yield
            yield
            yield
            transpose_to(u, lambda i: u[:, i * 128:(i + 1) * 128], 8, uT, lambda i0, n: uT[:, i0:i0 + n, :])
            yield
            for cbk in range(4):
                ps = nps()
                for k in range(8):
                    P(lambda e, ps=ps, k=k, cbk=cbk: e.matmul(ps[:], lhsT=uT[:, k, :], rhs=Wz[:, k, cbk * 512:(cbk + 1) * 512],
                                                              start=(k == 0), stop=(k == 7)), r=[uT, Wz], w=[ps])
                A(lambda e, ps=ps, cbk=cbk, zs=zs: e.activation(out=zs[:, cbk * 512:(cbk + 1) * 512], in_=ps[:], func=AF.Silu),
                  r=[ps], w=[zs])
                yield
            ps = nps()
            P(lambda e, ps=ps, sm=sm: e.matmul(ps[:, 0:32], lhsT=Mle[:], rhs=sm[:, 128:160], start=True, stop=True), r=[Mle, sm], w=[ps])
            P(lambda e, ps=ps, sm=sm: e.matmul(ps[:, 32:64], lhsT=Mge[:], rhs=sm[:, 160:192], start=True, stop=True), r=[Mge, sm], w=[ps])
            P(lambda e, ps=ps, sm=sm: e.matmul(ps[:, 64:96], lhsT=Mlt[:], rhs=sm[:, 160:192], start=True, stop=True), r=[Mlt, sm], w=[ps])
            P(lambda e, ps=ps, sm=sm: e.matmul(ps[:, 96:128], lhsT=onesf[:], rhs=sm[:, 160:192], start=True, stop=True), r=[onesf, sm], w=[ps])
            A(lambda e, ps=ps, ev=ev: e.activation(out=ev[:], in_=ps[:, 0:128], func=AF.Exp), r=[ps], w=[ev])
            for g2 in range(2):
                ps = nps()
                for g4 in range(4):
                    g = g2 * 4 + g4
                    P(lambda e, ps=ps, g=g, g4=g4, bct=bct: e.matmul(ps[:, g4 * 128:(g4 + 1) * 128], lhsT=bct[:, g, :],
                                                                     rhs=bct[:, 8 + g, :], start=True, stop=True), r=[bct], w=[ps])
                A(lambda e, ps=ps, g2=g2: e.activation(out=sc[:, g2 * 4:g2 * 4 + 4, :], in_=ps[:].rearrange("p (a b) -> p a b", a=4),
                                                       func=AF.Copy), r=[ps], w=[sc])
            yield
            Wts = {}

            def P1(g):
                for d in range(2):
                    Wt = Wr[st["ri"] % 8]
                    st["ri"] += 1
                    uu = d * 8 + g
                    b, slot = uu % 3, uu // 3
                    ps = nps()
                    P(lambda e, ps=ps, b=b, slot=slot: e.matmul(ps[:], lhsT=onesb[32 * b:32 * b + 2, :], rhs=RB[32 * b:32 * b + 2, slot * 512:(slot + 1) * 512],
                                                                start=True, stop=False), r=[onesb, RB], w=[ps])
                    P(lambda e, ps=ps, d=d: e.matmul(ps[:], lhsT=identb[:], rhs=mneg[d][:].rearrange("p a b -> p (a b)"),
                                                     start=False, stop=True), r=[identb, mneg[d]], w=[ps])
                    for h4 in range(4):
                        h = g * 4 + h4
                        A(lambda e, ps=ps, Wt=Wt, h4=h4, h=h, d=d, sm=sm: e.activation(
                            out=Wt[:, h4, :], in_=ps[:, h4 * 128:(h4 + 1) * 128], func=AF.Exp,
                            bias=sm[:, 64 + d * 32 + h:64 + d * 32 + h + 1]), r=[ps, sm], w=[Wt])
                    V(lambda e, Wt=Wt, g=g: e.tensor_tensor(out=Wt[:], in0=Wt[:],
                                                            in1=sc[:, g:g + 1, :].to_broadcast([128, 4, 128]), op=ALU.mult),
                      r=[Wt, sc], w=[Wt])
                    Wts[(g, d)] = Wt

            def P2(g):
                psy = nps()
                for h4 in range(4):
                    h = g * 4 + h4
                    for d in range(2):
                        Wt = Wts[(g, d)]
                        P(lambda e, psy=psy, Wt=Wt, h4=h4, h=h, d=d, xtok=xtok: e.matmul(
                            psy[:, h4 * 64:(h4 + 1) * 64], lhsT=Wt[:, h4, :], rhs=xtok[:, h * 64:(h + 1) * 64],
                            start=(d == 0), stop=(d == 1)), r=[Wt, xtok], w=[psy])
                V(lambda e, psy=psy, g=g, yacc=yacc: e.tensor_copy(out=yacc[:, g * 256:(g + 1) * 256], in_=psy[:, 0:256]), r=[psy], w=[yacc])

            P1(0)
            yield
            P1(1)
            yield
            for g in range(8):
                if g + 2 < 8:
                    P1(g + 2)
                P2(g)
                yield
            L.release_front()
            while not L.back_free():
                yield
            L.acquire_back()
            if first:
                V(lambda e: e.memset(Sb[:], 0.0), w=[Sb])
            V(lambda e: e.tensor_copy(out=prevb[:], in_=Sb[:]), r=[Sb], w=[prevb])
            for d in range(2):
                pv = prevf if d == 0 else prevb
                for g2 in range(4):
                    ps = nps()
                    t1 = t1r[st["ti"] % 2]
                    st["ti"] += 1
                    for gg in range(2):
                        g = g2 * 2 + gg
                        P(lambda e, ps=ps, g=g, gg=gg, pv=pv, bct=bct: e.matmul(ps[:, gg * 256:(gg + 1) * 256], lhsT=bct[:, 8 + g, :],
                                                                                rhs=pv[:, g * 256:(g + 1) * 256], start=True, stop=True),
                          r=[bct, pv], w=[ps])
                    V(lambda e, ps=ps, g2=g2, d=d, t1=t1, ev=ev: e.tensor_tensor(
                        out=t1[:].rearrange("p (h x) -> p h x", h=8), in0=ps[:].rearrange("p (h x) -> p h x", h=8),
                        in1=ev[:, d * 32 + g2 * 8:d * 32 + g2 * 8 + 8].unsqueeze(2).to_broadcast([128, 8, 64]), op=ALU.mult),
                      r=[ps, ev], w=[t1])
                    G(lambda e, g2=g2, t1=t1, yacc=yacc: e.tensor_tensor(out=yacc[:, g2 * 512:(g2 + 1) * 512], in0=yacc[:, g2 * 512:(g2 + 1) * 512],
                                                                         in1=t1[:], op=ALU.add), r=[yacc, t1], w=[yacc])
                    yield
            for g2 in range(4):
                t1 = t1r[st["ti"] % 2]
                st["ti"] += 1
                G(lambda e, xtok=xtok, t1=t1, g2=g2: e.tensor_tensor(out=t1[:].rearrange("p (h x) -> p h x", h=8),
                                                                     in0=xtok[:, g2 * 512:(g2 + 1) * 512].rearrange("p (h x) -> p h x", h=8),
                                                                     in1=hp[:, 128 + g2 * 8:128 + g2 * 8 + 8].unsqueeze(2).to_broadcast([128, 8, 64]), op=ALU.mult),
                  r=[xtok, hp], w=[t1])
                V(lambda e, t1=t1, g2=g2, yacc=yacc: e.tensor_tensor(out=yacc[:, g2 * 512:(g2 + 1) * 512], in0=yacc[:, g2 * 512:(g2 + 1) * 512],
                                                                     in1=t1[:], op=ALU.add), r=[yacc, t1], w=[yacc])
            yield
            V(lambda e, sm=sm, ev=ev: e.tensor_tensor(out=wb[:], in0=sm[:, 32:64], in1=ev[:, 64:96], op=ALU.mult), r=[sm, ev], w=[wb])
            V(lambda e, xtok=xtok: e.tensor_tensor(out=xw[:].rearrange("p (h d) -> p h d", h=32),
                                                   in0=xtok[:].rearrange("p (h d) -> p h d", h=32),
                                                   in1=wb[:].unsqueeze(2).to_broadcast([128, 32, 64]), op=ALU.mult),
              r=[xtok, wb], w=[xw])
            G(lambda e, ev=ev: e.tensor_tensor(out=Sb[:].rearrange("p (h d) -> p h d", h=32),
                                               in0=Sb[:].rearrange("p (h d) -> p h d", h=32),
                                               in1=ev[:, 96:128].unsqueeze(2).to_broadcast([128, 32, 64]), op=ALU.mult),
              r=[Sb, ev], w=[Sb])
            for g2 in range(4):
                ps = nps()
                for gg in range(2):
                    g = g2 * 2 + gg
                    P(lambda e, ps=ps, g=g, gg=gg, btok=btok: e.matmul(ps[:, gg * 256:(gg + 1) * 256], lhsT=btok[:, g * 128:(g + 1) * 128],
                                                                      rhs=xw[:, g * 256:(g + 1) * 256], start=True, stop=True),
                      r=[btok, xw], w=[ps])
                V(lambda e, ps=ps, g2=g2: e.tensor_tensor(out=Sb[:, g2 * 512:(g2 + 1) * 512], in0=Sb[:, g2 * 512:(g2 + 1) * 512],
                                                          in1=ps[:], op=ALU.add), r=[Sb, ps], w=[Sb])
            yield
            V(lambda e, yacc=yacc, zs=zs: e.tensor_tensor(out=yacc[:], in0=yacc[:], in1=zs[:], op=ALU.mult), r=[yacc, zs], w=[yacc])
            rmsnorm(yacc, yacc[:], None, None, yn, yn[:], DI)
            yield
            yield
            yield
            yield
            transpose_to(yn, lambda i: yn[:, i * 128:(i + 1) * 128], 16, ynT, lambda i0, n: ynT[:, i0:i0 + n, :])
            yield
            for cbk in range(2):
                ps = nps()
                for k in range(16):
                    P(lambda e, ps=ps, k=k, cbk=cbk: e.matmul(ps[:], lhsT=ynT[:, k, :], rhs=Wo[:, k, cbk * 512:(cbk + 1) * 512],
                                                              start=(k == 0), stop=(k == 15)), r=[ynT, Wo], w=[ps])
                V(lambda e, ps=ps, cbk=cbk, xo=xo, xt=xt: e.tensor_tensor(out=xo[:, cbk * 512:(cbk + 1) * 512], in0=ps[:],
                                                                          in1=xt[:, cbk * 512:(cbk + 1) * 512], op=ALU.add),
                  r=[ps, xt], w=[xo])
                yield
            ST(xo, Xdst[gc * 128:(gc + 1) * 128, :], xo[:])
            L.release_back()

        items = []
        for (cs, C) in seq_chunks:
            for c in range(C - 1, -1, -1):
                items.append((cs + c, c == C - 1))
        run_pipeline(items, body)
        S.barrier()
        S.release(m0)

    def gla_common_tiles():
        k = K()
        k.Win = S.tile([128, 8, GLA_IN], BF16, "gWin")
        k.g_bc = S.tile([128, DM], F32, "gg_bc")
        k.gup = S.tile([16, 1024], F32, "gup")
        k.gupb = S.tile([16, 1024], BF16, "gupb")
        k.gbias = S.tile([128, 1024], F32, "gbias")
        bcast_load(k.g_bc, norm_mix_g[1:2, :], DM)
        bcast_load(k.gbias, gla_gate_bias[0:1, :], 1024)
        LD(k.gup, k.gup[:], gla_gate_up)
        V(lambda e: e.tensor_copy(out=k.gupb[:], in_=k.gup[:]), r=[k.gup], w=[k.gupb])
        return k

    def gla_front(k, xt, need_b, skip_norm=False):
        if not skip_norm:
            rmsnorm(xt, xt[:], k.g_bc, k.g_bc[:], k.u, k.u[:], DM)
        transpose_to(k.u, lambda i: k.u[:, i * 128:(i + 1) * 128], 8, k.uT, lambda i0, n: k.uT[:, i0:i0 + n, :])
        for d in range(2 if need_b else 1):
            ps = nps()
            for kk in range(8):
                P(lambda e, ps=ps, kk=kk, d=d: e.matmul(ps[0:16, 0:128], lhsT=k.Win[:, kk, 3072 + d * 16:3072 + d * 16 + 16],
                                                        rhs=k.uT[:, kk, :], start=(kk == 0), stop=(kk == 7)), r=[k.Win, k.uT], w=[ps])
            V(lambda e, ps=ps: e.tensor_copy(out=k.lrT[:], in_=ps[0:16, 0:128]), r=[ps], w=[k.lrT])
            ps2 = nps()
            P(lambda e, ps2=ps2, d=d: e.matmul(ps2[:], lhsT=k.lrT[:], rhs=k.gupb[:, d * 512:(d + 1) * 512], start=True, stop=True),
              r=[k.lrT, k.gupb], w=[ps2])
            lg = k.lg[d]
            V(lambda e, ps2=ps2, d=d, lg=lg: e.tensor_tensor(out=lg[:], in0=ps2[:], in1=k.gbias[:, d * 512:(d + 1) * 512], op=ALU.add),
              r=[ps2, k.gbias], w=[lg])
            A(lambda e, lg=lg: e.activation(out=lg[:], in_=lg[:], func=AF.Exp, scale=-1.0), r=[lg], w=[lg])
            A(lambda e, lg=lg: e.activation(out=lg[:], in_=lg[:], func=AF.Ln, bias=1.0), r=[lg], w=[lg])
            V(lambda e, lg=lg: e.tensor_scalar(out=lg[:], in0=lg[:], scalar1=-1.0 / 16.0, scalar2=None, op0=ALU.mult), r=[lg], w=[lg])
            lgh, lgl = k.lgh[d], k.lgl[d]
            V(lambda e, lg=lg, lgh=lgh: e.tensor_copy(out=lgh[:], in_=lg[:]), r=[lg], w=[lgh])
            G(lambda e, lg=lg, lgh=lgh: e.tensor_tensor(out=k.lgt[:], in0=lg[:], in1=lgh[:], op=ALU.subtract), r=[lg, lgh], w=[k.lgt])
            G(lambda e, lgl=lgl: e.tensor_copy(out=lgl[:], in_=k.lgt[:]), r=[k.lgt], w=[lgl])

    def gla_state_update(k, d, lT, Sst, ktok_ps_fn):
        lgh, lgl = k.lgh[d], k.lgl[d]
        ps = nps()
        P(lambda e, ps=ps: e.matmul(ps[:], lhsT=lT[:], rhs=lgh[:], start=True, stop=False), r=[lT, lgh], w=[ps])
        P(lambda e, ps=ps: e.matmul(ps[:], lhsT=lT[:], rhs=lgl[:], start=False, stop=True), r=[lT, lgl], w=[ps])
        A(lambda e, ps=ps: e.activation(out=k.kex[:], in_=ps[:], func=AF.Exp), r=[ps], w=[k.kex])
        psk = ktok_ps_fn()
        V(lambda e, psk=psk: e.tensor_tensor(out=k.kend[:], in0=psk[:], in1=k.kex[:], op=ALU.mult), r=[psk, k.kex], w=[k.kend])
        ps = nps()
        for h in range(4):
            P(lambda e, ps=ps, h=h: e.matmul(ps[:, h:h + 1], lhsT=lgh[:, h * 128:(h + 1) * 128], rhs=onesb[:, 0:1], start=True, stop=False),
              r=[lgh, onesb], w=[ps])
            P(lambda e, ps=ps, h=h: e.matmul(ps[:, h:h + 1], lhsT=lgl[:, h * 128:(h + 1) * 128], rhs=onesb[:, 0:1], start=False, stop=True),
              r=[lgl, onesb], w=[ps])
        A(lambda e, ps=ps: e.activation(out=k.cd[:], in_=ps[:, 0:4], func=AF.Exp), r=[ps], w=[k.cd])
        for h2 in range(2):
            ps = nps()
            for hh in range(2):
                h = h2 * 2 + hh
                P(lambda e, ps=ps, h=h, hh=hh: e.matmul(ps[:, hh * 256:(hh + 1) * 256], lhsT=k.kend[:, h * 128:(h + 1) * 128],
                                                        rhs=k.vtok[:, h * 256:(h + 1) * 256], start=True, stop=True),
                  r=[k.kend, k.vtok], w=[ps])
            for hh in range(2):
                h = h2 * 2 + hh
                V(lambda e, ps=ps, h=h, hh=hh: e.scalar_tensor_tensor(out=Sst[:, h * 256:(h + 1) * 256], in0=Sst[:, h * 256:(h + 1) * 256],
                                                                      scalar=k.cd[:, h:h + 1], in1=ps[:, hh * 256:(hh + 1) * 256],
                                                                      op0=ALU.mult, op1=ALU.add), r=[Sst, k.cd, ps], w=[Sst])

    def gla_state_front(k, d, lT, ktok_ps_fn, kend, cd):
        lgh, lgl = k.lgh[d], k.lgl[d]
        ps = nps()
        P(lambda e, ps=ps: e.matmul(ps[:], lhsT=lT[:], rhs=lgh[:], start=True, stop=False), r=[lT, lgh], w=[ps])
        P(lambda e, ps=ps: e.matmul(ps[:], lhsT=lT[:], rhs=lgl[:], start=False, stop=True), r=[lT, lgl], w=[ps])
        A(lambda e, ps=ps: e.activation(out=k.kex[:], in_=ps[:], func=AF.Exp), r=[ps], w=[k.kex])
        psk = ktok_ps_fn()
        V(lambda e, psk=psk: e.tensor_tensor(out=kend[:], in0=psk[:], in1=k.kex[:], op=ALU.mult), r=[psk, k.kex], w=[kend])
        ps = nps()
        for h in range(4):
            P(lambda e, ps=ps, h=h: e.matmul(ps[:, h:h + 1], lhsT=lgh[:, h * 128:(h + 1) * 128], rhs=onesb[:, 0:1], start=True, stop=False),
              r=[lgh, onesb], w=[ps])
            P(lambda e, ps=ps, h=h: e.matmul(ps[:, h:h + 1], lhsT=lgl[:, h * 128:(h + 1) * 128], rhs=onesb[:, 0:1], start=False, stop=True),
              r=[lgl, onesb], w=[ps])
        A(lambda e, ps=ps: e.activation(out=cd[:], in_=ps[:, 0:4], func=AF.Exp), r=[ps], w=[cd])

    def gla_state_back(kend, vtok, cd, Sst):
        for h2 in range(2):
            ps = nps()
            for hh in range(2):
                h = h2 * 2 + hh
                P(lambda e, ps=ps, h=h, hh=hh: e.matmul(ps[:, hh * 256:(hh + 1) * 256], lhsT=kend[:, h * 128:(h + 1) * 128],
                                                        rhs=vtok[:, h * 256:(h + 1) * 256], start=True, stop=True),
                  r=[kend, vtok], w=[ps])
            for hh in range(2):
                h = h2 * 2 + hh
                V(lambda e, ps=ps, h=h, hh=hh: e.scalar_tensor_tensor(out=Sst[:, h * 256:(h + 1) * 256], in0=Sst[:, h * 256:(h + 1) * 256],
                                                                      scalar=cd[:, h:h + 1], in1=ps[:, hh * 256:(hh + 1) * 256],
                                                                      op0=ALU.mult, op1=ALU.add), r=[Sst, cd, ps], w=[Sst])

    def gla_ktok(k):
        ps = nps()
        for kk in range(8):
            P(lambda e, ps=ps, kk=kk: e.matmul(ps[:], lhsT=k.uT[:, kk, :], rhs=k.Win[:, kk, 512:1024], start=(kk == 0), stop=(kk == 7)),
              r=[k.uT, k.Win], w=[ps])
        return ps

    def gla_vtok(k, vtok=None):
        if vtok is None:
            vtok = k.vtok
        for cbk in range(2):
            ps = nps()
            for kk in range(8):
                P(lambda e, ps=ps, kk=kk, cbk=cbk: e.matmul(ps[:], lhsT=k.uT[:, kk, :], rhs=k.Win[:, kk, 1024 + cbk * 512:1024 + (cbk + 1) * 512],
                                                            start=(kk == 0), stop=(kk == 7)), r=[k.uT, k.Win], w=[ps])
            A(lambda e, ps=ps, cbk=cbk, vtok=vtok: e.activation(out=vtok[:, cbk * 512:(cbk + 1) * 512], in_=ps[:], func=AF.Copy), r=[ps], w=[vtok])

    def gla_alloc_work(k):
        k.u = S.tile([128, DM], BF16, "gu")
        k.uT = S.tile([128, 8, 128], BF16, "guT")
        k.lrT = S.tile([16, 128], BF16, "glrT")
        k.lg = [S.tile([128, 512], F32, f"glg{i}") for i in range(2)]
        k.lgh = [S.tile([128, 512], BF16, f"glgh{i}") for i in range(2)]
        k.lgl = [S.tile([128, 512], BF16, f"glgl{i}") for i in range(2)]
        k.lgt = S.tile([128, 512], F32, "glgt")
        k.kex = S.tile([128, 512], F32, "gkex")
        k.kend = S.tile([128, 512], BF16, "gkend")
        k.cd = S.tile([128, 4], F32, "gcd")
        k.vtok = S.tile([128, 1024], BF16, "gvtok")

    def gla_phase_a(Xsrc):
        m0 = S.mark()
        k = gla_common_tiles()
        m1 = S.mark()
        stg = [S.tile([128, 2048], F32, f"stg{i}") for i in range(3)]
        load_w(k.Win, 0, gla_w_in, 0, GLA_IN, 8, stg)
        S.barrier()
        S.release(m1)
        ks = []
        NF = 3
        for i in range(NF):
            kq = K()
            kq.Win, kq.g_bc, kq.gupb, kq.gbias = k.Win, k.g_bc, k.gupb, k.gbias
            gla_alloc_work(kq)
            ks.append(kq)
        xts = [S.tile([128, DM], F32, f"gax{i}") for i in range(NF + 1)]
        Sf = S.tile([128, 1024], F32, "gSf")
        prevbs = [S.tile([128, 1024], BF16, f"gprevb{i}") for i in range(NF + 1)]
        vtoks = [S.tile([128, 1024], BF16, f"gavt{i}") for i in range(NF + 1)]
        kends = [S.tile([128, 512], BF16, f"gake{i}") for i in range(NF + 1)]
        cds = [S.tile([128, 4], F32, f"gacd{i}") for i in range(NF + 1)]
        LK = {"front": 0, "back": False, "next_back": 0, "front_done": set()}

        def body(item, my):
            gc, first, last = item
            i3 = my % (NF + 1)
            kq = ks[my % NF]
            xt, prevb, vtok, kend, cd = xts[i3], prevbs[i3], vtoks[i3], kends[i3], cds[i3]
            if not last:
                LD(xt, xt[:], Xsrc[gc * 128:(gc + 1) * 128, :])
                yield
                rmsnorm(xt, xt[:], kq.g_bc, kq.g_bc[:], kq.u, kq.u[:], DM)
                yield
                yield
                gla_front(kq, xt, False, skip_norm=True)
                yield
                gla_vtok(kq, vtok)
                yield
                gla_state_front(kq, 0, MgtB, lambda: gla_ktok(kq), kend, cd)
                yield
            LK["front"] -= 1
            LK["front_done"].add(my)
            while LK["back"] or LK["next_back"] != my:
                yield
            LK["back"] = True
            if first:
                V(lambda e: e.memset(Sf[:], 0.0), w=[Sf])
            V(lambda e, prevb=prevb: e.tensor_copy(out=prevb[:], in_=Sf[:]), r=[Sf], w=[prevb])
            ST(prevb, g_prev[gc], prevb[:])
            yield
            if not last:
                gla_state_back(kend, vtok, cd, Sf)
            LK["back"] = False
            LK["next_back"] += 1

        items = []
        for (cs, C) in seq_chunks:
            for c in range(C):
                items.append((cs + c, c == 0, c == C - 1))
        active = []
        idx = 0
        while active or idx < len(items):
            if (idx < len(items) and LK["front"] < NF and len(active) < NF + 1
                    and (idx < NF or (idx - NF) in LK["front_done"])):
                LK["front"] += 1
                active.append(body(items[idx], idx))
                idx += 1
            for g in list(active):
                try:
                    next(g)
                except StopIteration:
                    active.remove(g)
        S.barrier()
        S.release(m0)

    def gla_phase_b(Xsrc, Xdst):
        m0 = S.mark()
        k = gla_common_tiles()
        Wo = S.tile([128, 8, DM], BF16, "gWo")
        ng = S.tile([128, 256], F32, "gng")
        bcast_load(ng, gla_norm_g[0:1, :], 256)
        TF = S.tile([128, 128], F32, "TF")
        TB = S.tile([128, 128], F32, "TB")
        V(lambda e: e.tensor_tensor(out=TF[:], in0=Mle[:], in1=Mle[:, 64:65].to_broadcast([128, 128]), op=ALU.subtract), r=[Mle], w=[TF])
        V(lambda e: e.tensor_tensor(out=TB[:], in0=Mge[:], in1=Mge[:, 64:65].to_broadcast([128, 128]), op=ALU.subtract), r=[Mge], w=[TB])
        TFb = S.tile([128, 128], BF16, "TFb")
        TBb = S.tile([128, 128], BF16, "TBb")
        V(lambda e: e.tensor_copy(out=TFb[:], in_=TF[:]), r=[TF], w=[TFb])
        V(lambda e: e.tensor_copy(out=TBb[:], in_=TB[:]), r=[TB], w=[TBb])
        m1 = S.mark()
        stg = [S.tile([128, 2048], F32, f"stg{i}") for i in range(3)]
        load_w(k.Win, 0, gla_w_in, 0, GLA_IN, 8, stg)
        load_w(Wo, 0, gla_w_out, 0, DM, 8, stg)
        S.barrier()
        S.release(m1)
        gla_alloc_work(k)
        xts = [S.tile([128, DM], F32, f"gbx{i}") for i in range(2)]
        prevfs = [S.tile([128, 1024], BF16, f"gprevf{i}") for i in range(2)]
        qgs = [[S.tile([128, 4, 128], BF16, f"gqg{i}{d}") for d in range(2)] for i in range(2)]
        ATs = [[S.tile([128, 4, 128], BF16, f"gAT{i}{d}") for d in range(2)] for i in range(2)]
        vtoks = [S.tile([128, 1024], BF16, f"gvt{i}") for i in range(2)]
        gss = [S.tile([128, 1024], BF16, f"ggs{i}") for i in range(2)]
        kends = [S.tile([128, 512], BF16, f"gke{i}") for i in range(2)]
        cds = [S.tile([128, 4], F32, f"gcd{i}") for i in range(2)]
        xos = [S.tile([128, DM], F32, f"gxo{i}") for i in range(2)]
        qk = S.tile([128, 8, 128], F32, "gqk")
        EX = [S.tile([128, 4, 128], F32, f"gEX{i}") for i in range(2)]
        opT = [S.tile([128, 4, 128], BF16, f"gopT{i}") for i in range(4)]
        Sb = S.tile([128, 1024], F32, "gSb")
        prevb = S.tile([128, 1024], BF16, "gprevb")
        o = S.tile([128, 1024], F32, "go")
        osq = S.tile([128, 1024], F32, "gosq")
        rs = S.tile([128, 8], F32, "grs")
        on = S.tile([128, 1024], BF16, "gon")
        onT = S.tile([128, 8, 128], BF16, "gonT")

        def body(item, L):
            gc, first, last = item
            i2 = gc % 2
            xt, prevf, xo, qg, AT, vtok, gs, kend, cd = xts[i2], prevfs[i2], xos[i2], qgs[i2], ATs[i2], vtoks[i2], gss[i2], kends[i2], cds[i2]
            LD(xt, xt[:], Xsrc[gc * 128:(gc + 1) * 128, :])
            LD(prevf, prevf[:], g_prev[gc])
            yield
            rmsnorm(xt, xt[:], k.g_bc, k.g_bc[:], k.u, k.u[:], DM)
            yield
            yield
            yield
            gla_front(k, xt, True, skip_norm=True)
            yield
            gla_vtok(k, vtok)
            yield
            for cbk in range(2):
                ps = nps()
                for kk in range(8):
                    P(lambda e, ps=ps, kk=kk, cbk=cbk: e.matmul(ps[:], lhsT=k.uT[:, kk, :], rhs=k.Win[:, kk, 2048 + cbk * 512:2048 + (cbk + 1) * 512],
                                                                start=(kk == 0), stop=(kk == 7)), r=[k.uT, k.Win], w=[ps])
                A(lambda e, ps=ps, cbk=cbk, gs=gs: e.activation(out=gs[:, cbk * 512:(cbk + 1) * 512], in_=ps[:], func=AF.Silu), r=[ps], w=[gs])
                yield
            for qkk in range(2):
                ps = nps()
                for h in range(4):
                    m = qkk * 4 + h
                    for kk in range(8):
                        P(lambda e, ps=ps, h=h, m=m, kk=kk: e.matmul(ps[:, h * 128:(h + 1) * 128], lhsT=k.Win[:, kk, m * 128:(m + 1) * 128],
                                                                     rhs=k.uT[:, kk, :], start=(kk == 0), stop=(kk == 7)), r=[k.Win, k.uT], w=[ps])
                A(lambda e, ps=ps, qkk=qkk: e.activation(out=qk[:, qkk * 4:qkk * 4 + 4, :], in_=ps[:].rearrange("p (a b) -> p a b", a=4),
                                                         func=AF.Copy, scale=(128.0 ** -0.5 if qkk == 0 else 1.0)), r=[ps], w=[qk])
                yield
            outs = [opT[0], opT[1], opT[2], opT[3], qg[0], qg[1]]
            specs = [(0, TFb, 1.0, 0, 0), (0, TFb, -1.0, 1, 1), (1, TBb, 1.0, 0, 2), (1, TBb, -1.0, 1, 3),
                     (0, MleB, 1.0, 0, 4), (1, MgeB, 1.0, 0, 5)]
            last_key = None
            pse = None
            for si, (d, Tm, sgn, qi, oi) in enumerate(specs):
                if last_key != (d, id(Tm)):
                    ps = nps()
                    for h in range(4):
                        P(lambda e, ps=ps, h=h, d=d, Tm=Tm: e.matmul(ps[:, h * 128:(h + 1) * 128], lhsT=k.lgh[d][:, h * 128:(h + 1) * 128],
                                                                     rhs=Tm[:], start=True, stop=False), r=[k.lgh[d], Tm], w=[ps])
                        P(lambda e, ps=ps, h=h, d=d, Tm=Tm: e.matmul(ps[:, h * 128:(h + 1) * 128], lhsT=k.lgl[d][:, h * 128:(h + 1) * 128],
                                                                     rhs=Tm[:], start=False, stop=True), r=[k.lgl[d], Tm], w=[ps])
                    last_key = (d, id(Tm))
                    pse = ps
                ex = EX[si % 2]
                A(lambda e, pse=pse, ex=ex, sgn=sgn: e.activation(out=ex[:], in_=pse[:].rearrange("p (a b) -> p a b", a=4), func=AF.Exp, scale=sgn),
                  r=[pse], w=[ex])
                ot = outs[oi]
                V(lambda e, ex=ex, ot=ot, qi=qi: e.tensor_tensor(out=ot[:], in0=qk[:, qi * 4:qi * 4 + 4, :], in1=ex[:], op=ALU.mult),
                  r=[qk, ex], w=[ot])
                yield
            for d in range(2):
                ps = nps()
                for h in range(4):
                    P(lambda e, ps=ps, h=h, d=d: e.matmul(ps[:, h * 128:(h + 1) * 128], lhsT=opT[2 * d + 1][:, h, :], rhs=opT[2 * d][:, h, :],
                                                          start=True, stop=True), r=[opT[2 * d + 1], opT[2 * d]], w=[ps])
                msk = Mle if d == 0 else Mge
                V(lambda e, ps=ps, d=d, msk=msk, AT=AT: e.tensor_tensor(out=AT[d][:], in0=ps[:].rearrange("p (a b) -> p a b", a=4),
                                                                       in1=msk[:].unsqueeze(1).to_broadcast([128, 4, 128]), op=ALU.mult),
                  r=[ps, msk], w=[AT[d]])
                yield
            if not last:
                gla_state_front(k, 1, MltB, lambda: gla_ktok(k), kend, cd)
            yield
            L.release_front()
            while not L.back_free():
                yield
            L.acquire_back()
            if first:
                V(lambda e: e.memset(Sb[:], 0.0), w=[Sb])
            V(lambda e: e.tensor_copy(out=prevb[:], in_=Sb[:]), r=[Sb], w=[prevb])
            for h2 in range(2):
                ps = nps()
                for hh in range(2):
                    h = h2 * 2 + hh
                    oslc = ps[:, hh * 256:(hh + 1) * 256]
                    P(lambda e, oslc=oslc, h=h, AT=AT, vtok=vtok: e.matmul(oslc, lhsT=AT[0][:, h, :], rhs=vtok[:, h * 256:(h + 1) * 256], start=True, stop=False),
                      r=[AT[0], vtok], w=[ps])
                    P(lambda e, oslc=oslc, h=h, AT=AT, vtok=vtok: e.matmul(oslc, lhsT=AT[1][:, h, :], rhs=vtok[:, h * 256:(h + 1) * 256], start=False, stop=False),
                      r=[AT[1], vtok], w=[ps])
                    P(lambda e, oslc=oslc, h=h, prevf=prevf, qg=qg: e.matmul(oslc, lhsT=qg[0][:, h, :], rhs=prevf[:, h * 256:(h + 1) * 256], start=False, stop=False),
                      r=[qg[0], prevf], w=[ps])
                    P(lambda e, oslc=oslc, h=h, qg=qg: e.matmul(oslc, lhsT=qg[1][:, h, :], rhs=prevb[:, h * 256:(h + 1) * 256], start=False, stop=True),
                      r=[qg[1], prevb], w=[ps])
                V(lambda e, ps=ps, h2=h2: e.tensor_copy(out=o[:, h2 * 512:(h2 + 1) * 512], in_=ps[:]), r=[ps], w=[o])
                yield
            if not last:
                gla_state_back(kend, vtok, cd, Sb)
            yield
            G(lambda e: e.tensor_tensor(out=osq[:], in0=o[:], in1=o[:], op=ALU.mult), r=[o], w=[osq])
            V(lambda e: e.tensor_reduce(out=rs[:, 0:4], in_=osq[:].rearrange("p (h x) -> p h x", h=4), axis=AX.X, op=ALU.add), r=[osq], w=[rs])
            V(lambda e: e.tensor_scalar(out=rs[:, 4:8], in0=rs[:, 0:4], scalar1=1.0 / 256, scalar2=EPS, op0=ALU.mult, op1=ALU.add), r=[rs], w=[rs])
            A(lambda e: e.activation(out=rs[:, 4:8], in_=rs[:, 4:8], func=AF.Ln), r=[rs], w=[rs])
            A(lambda e: e.activation(out=rs[:, 4:8], in_=rs[:, 4:8], func=AF.Exp, scale=-0.5), r=[rs], w=[rs])
            V(lambda e: e.tensor_tensor(out=o[:].rearrange("p (h x) -> p h x", h=4), in0=o[:].rearrange("p (h x) -> p h x", h=4),
                                        in1=rs[:, 4:8].unsqueeze(2).to_broadcast([128, 4, 256]), op=ALU.mult), r=[o, rs], w=[o])
            G(lambda e: e.tensor_tensor(out=o[:].rearrange("p (h x) -> p h x", h=4), in0=o[:].rearrange("p (h x) -> p h x", h=4),
                                        in1=ng[:].unsqueeze(1).to_broadcast([128, 4, 256]), op=ALU.mult), r=[o, ng], w=[o])
            V(lambda e, gs=gs: e.tensor_tensor(out=on[:], in0=o[:], in1=gs[:], op=ALU.mult), r=[o, gs], w=[on])
            yield
            yield
            yield
            yield
            transpose_to(on, lambda i: on[:, i * 128:(i + 1) * 128], 8, onT, lambda i0, n: onT[:, i0:i0 + n, :])
            yield
            for cbk in range(2):
                ps = nps()
                for kk in range(8):
                    P(lambda e, ps=ps, kk=kk, cbk=cbk: e.matmul(ps[:], lhsT=onT[:, kk, :], rhs=Wo[:, kk, cbk * 512:(cbk + 1) * 512],
                                                                start=(kk == 0), stop=(kk == 7)), r=[onT, Wo], w=[ps])
                V(lambda e, ps=ps, cbk=cbk, xo=xo, xt=xt: e.tensor_tensor(out=xo[:, cbk * 512:(cbk + 1) * 512], in0=ps[:],
                                                                          in1=xt[:, cbk * 512:(cbk + 1) * 512], op=ALU.add), r=[ps, xt], w=[xo])
                yield
            ST(xo, Xdst[gc * 128:(gc + 1) * 128, :], xo[:])
            L.release_back()

        items = []
        for (cs, C) in seq_chunks:
            for c in range(C - 1, -1, -1):
                items.append((cs + c, c == C - 1, c == 0))
        run_pipeline(items, body)
        S.barrier()
        S.release(m0)

    S.barrier()
    phases = [
        lambda: ssd_phase_a(),
        lambda: ssd_phase_b(X1 if stop_after > 2 else y_out),
        lambda: phase_mlp(0, X1, X2 if stop_after > 3 else y_out, False),
        lambda: gla_phase_a(X2),
        lambda: gla_phase_b(X2, X3 if stop_after > 5 else y_out),
        lambda: phase_mlp(1, X3, y_out, True),
    ]
    for i, ph in enumerate(phases):
        if i + 1 > stop_after:
            break
        ph()
    S.barrier()
    S.emit()
    S.close()
    S.dbg_names = dbg_names
    return nc, S


def shard_inputs(inp, seq_per_core):
    f = lambda a: np.ascontiguousarray(np.asarray(a, dtype=np.float32))
    cw = f(inp["ssd_conv_w"])[0]
    cwl = np.ascontiguousarray(cw.T.reshape(32, 128, 7).transpose(1, 0, 2))
    cbl = np.ascontiguousarray(f(inp["ssd_conv_b"])[0].reshape(32, 128).T)
    hp = np.concatenate([f(inp["ssd_dt_bias_f"])[0], f(inp["ssd_dt_bias_b"])[0], f(inp["ssd_a_log_f"])[0],
                         f(inp["ssd_a_log_b"])[0], f(inp["ssd_d"])[0]])[None, :]
    common = {
        "norm_mix_g": f(inp["norm_mix_g"]), "norm_mlp_g": f(inp["norm_mlp_g"]),
        "norm_final_g": f(inp["norm_final_g"])[None, :],
        "norm_mix_gc": np.ascontiguousarray(f(inp["norm_mix_g"]).reshape(2, 8, 128).transpose(0, 2, 1)),
        "norm_mlp_gc": np.ascontiguousarray(f(inp["norm_mlp_g"]).reshape(2, 8, 128).transpose(0, 2, 1)),
        "ssd_norm_gc": np.ascontiguousarray(f(inp["ssd_norm_g"])[0].reshape(16, 128).T),
        "ssd_w_in": f(inp["ssd_w_in"])[0], "ssd_cw": cwl, "ssd_cb": cbl, "ssd_hp": np.ascontiguousarray(hp),
        "ssd_norm_g": f(inp["ssd_norm_g"]), "ssd_w_out": f(inp["ssd_w_out"])[0],
        "gla_w_in": f(inp["gla_w_in"])[0],
        "gla_gate_up": np.ascontiguousarray(np.concatenate([f(inp["gla_gate_up_f"])[0], f(inp["gla_gate_up_b"])[0]], axis=1)),
        "gla_gate_bias": np.ascontiguousarray(np.concatenate([f(inp["gla_gate_bias_f"])[0], f(inp["gla_gate_bias_b"])[0]])[None, :]),
        "gla_norm_g": f(inp["gla_norm_g"]), "gla_w_out": f(inp["gla_w_out"])[0],
        "mlp_w_up": f(inp["mlp_w_up"]), "mlp_w_down": f(inp["mlp_w_down"]),
    }
    maps = []
    for seqs in seq_per_core:
        m = dict(common)
        m["x_in"] = np.ascontiguousarray(np.concatenate([s.reshape(-1, DM) for s in seqs], axis=0))
        maps.append(m)
    return maps


def kernel(**inp):
    xp = np.asarray(inp["x_prompt"], dtype=np.float32)
    xs = np.asarray(inp["x_sample"], dtype=np.float32)
    n = 8
    seq_per_core = [[xp[2 * c], xp[2 * c + 1], xs[c]] for c in range(n)]
    maps = shard_inputs(inp, seq_per_core)
    nc, _ = build(SEQ_LENS)
    res = run_bass_kernel_spmd(nc, maps, core_ids=list(range(n)))
    yp = np.empty_like(xp)
    ys = np.empty_like(xs)
    for c in range(n):
        y = res.results[c]["y_out"]
        yp[2 * c] = y[0:2048]
        yp[2 * c + 1] = y[2048:4096]
        ys[c] = y[4096:12288]
    return (yp, ys)
```

```python
import numpy as np
import concourse.bass as bass
import concourse.mybir as mybir
from concourse.bass_utils import run_bass_kernel_spmd

F32 = mybir.dt.float32
BF16 = mybir.dt.bfloat16
AF = mybir.ActivationFunctionType
ALU = mybir.AluOpType
AX = mybir.AxisListType

DM = 1024
DI = 2048
NH = 32
NGRP = 8
SSD_IN = 6208
GLA_IN = 3104
DFF = 4096
EPS = 1e-5
SEQ_LENS = (2048, 2048, 8192)


class T:
    def __init__(self, h, name):
        self.h = h
        self.name = name
        self.w = None
        self.r = []

    def __getitem__(self, k):
        return self.h[k]


class Sched:
    ENG = ["sync", "tensor", "vector", "scalar", "gpsimd"]

    def __init__(self, nc):
        self.nc = nc
        self.prog = {e: [] for e in self.ENG}
        self.cnt = {}
        self.sems = {}
        self.waited = {e: {} for e in self.ENG}
        self.sem_stack = []
        self.tile_stack = []
        self.ninstr = 0
        self.uid = 0

    def sem(self, key):
        if key not in self.sems:
            cm = self.nc.semaphore("s_" + key)
            self.sems[key] = cm.__enter__()
            self.sem_stack.append(cm)
            self.cnt[key] = 0
        return self.sems[key]

    def tile(self, shape, dt, name, psum=False):
        self.uid += 1
        nm = f"{name}_{self.uid}"
        if psum:
            cm = self.nc.psum_tensor(nm, shape, dt)
        else:
            cm = self.nc.sbuf_tensor(nm, shape, dt)
        h = cm.__enter__()
        self.tile_stack.append(cm)
        return T(h, name)

    def mark(self):
        return len(self.tile_stack)

    def release(self, mark):
        while len(self.tile_stack) > mark:
            self.tile_stack.pop().__exit__(None, None, None)

    def op(self, eng, fn, reads=(), writes=(), dma=False, semkey=None):
        waits = {}

        def need(dep):
            if dep is None:
                return
            k, v = dep
            if k == "tensor" and eng == "tensor":
                return
            if self.waited[eng].get(k, 0) >= v:
                return
            waits[k] = max(waits.get(k, 0), v)

        for t in reads:
            need(t.w)
        for t in writes:
            need(t.w)
            for d in t.r:
                need(d)
        for k, v in waits.items():
            self.waited[eng][k] = v
        if dma:
            key, inc = semkey, 16
        else:
            key, inc = eng, 1
        self.sem(key)
        self.cnt[key] += inc
        me = (key, self.cnt[key])
        for t in reads:
            t.r.append(me)
            if len(t.r) > 64:
                mx = {}
                for k, v in t.r:
                    mx[k] = max(mx.get(k, 0), v)
                t.r = list(mx.items())
        for t in writes:
            t.w = me
            t.r = []
        self.prog[eng].append((fn, list(waits.items()), key, inc))
        self.ninstr += 1
        return me

    def wait_all(self, eng, keys=None):
        waits = []
        for k in (keys if keys is not None else list(self.cnt.keys())):
            v = self.cnt.get(k, 0)
            if v > 0 and self.waited[eng].get(k, 0) < v:
                waits.append((k, v))
                self.waited[eng][k] = v
        self.prog[eng].append((None, waits, None, 0))

    def barrier(self):
        for e in self.ENG:
            self.wait_all(e)

    def emit(self):
        nc = self.nc
        with nc.Block() as block:
            def run(eng_name):
                def body(e):
                    for fn, waits, key, inc in self.prog[eng_name]:
                        for k, v in waits:
                            e.wait_ge(self.sems[k], v)
                        if fn is not None:
                            fn(e).then_inc(self.sems[key], inc)
                return body
            block.sync(run("sync"))
            block.tensor(run("tensor"))
            block.vector(run("vector"))
            block.scalar(run("scalar"))
            block.gpsimd(run("gpsimd"))

    def close(self):
        self.release(0)
        while self.sem_stack:
            self.sem_stack.pop().__exit__(None, None, None)


class K:
    pass


def build(seq_lens=SEQ_LENS, stop_after=99, dbg=False):
    nc = bass.Bass("TRN2", target_bir_lowering=False)
    NT = sum(seq_lens)
    NCH = NT // 128
    seq_chunks = []
    c0 = 0
    for L in seq_lens:
        seq_chunks.append((c0, L // 128))
        c0 += L // 128

    def din(name, shape, dt=F32):
        return nc.dram_tensor(name, list(shape), dt, kind="ExternalInput").ap()

    x_in = din("x_in", [NT, DM])
    norm_mix_g = din("norm_mix_g", [2, DM])
    norm_mlp_g = din("norm_mlp_g", [2, DM])
    norm_final_g = din("norm_final_g", [1, DM])
    ssd_w_in = din("ssd_w_in", [DM, SSD_IN])
    ssd_cw = din("ssd_cw", [128, 32, 7])
    ssd_cb = din("ssd_cb", [128, 32])
    ssd_hp = din("ssd_hp", [1, 5 * 32])
    ssd_norm_g = din("ssd_norm_g", [1, DI])
    ssd_w_out = din("ssd_w_out", [DI, DM])
    gla_w_in = din("gla_w_in", [DM, GLA_IN])
    gla_gate_up = din("gla_gate_up", [16, 1024])
    gla_gate_bias = din("gla_gate_bias", [1, 1024])
    gla_norm_g = din("gla_norm_g", [1, 256])
    gla_w_out = din("gla_w_out", [DM, DM])
    mlp_w_up = din("mlp_w_up", [2, DM, DFF])
    mlp_w_down = din("mlp_w_down", [2, DFF, DM])
    norm_mix_gc = din("norm_mix_gc", [2, 128, 8])
    norm_mlp_gc = din("norm_mlp_gc", [2, 128, 8])
    ssd_norm_gc = din("ssd_norm_gc", [128, 16])
    y_out = nc.dram_tensor("y_out", [NT, DM], F32, kind="ExternalOutput").ap()
    ndbg = 24
    if dbg:
        dbg32 = nc.dram_tensor("dbg32", [ndbg, 128, 2048], F32, kind="ExternalOutput").ap()
        dbg16 = nc.dram_tensor("dbg16", [ndbg, 128, 2048], BF16, kind="ExternalOutput").ap()

    def dscr(name, shape, dt):
        return nc.dram_tensor(name, list(shape), dt).ap()

    X1 = dscr("X1", [NT, DM], F32)
    X2 = dscr("X2", [NT, DM], F32)
    X3 = dscr("X3", [NT, DM], F32)
    s_xtok = dscr("s_xtok", [NT, DI], BF16)
    s_btok = dscr("s_btok", [NT, 1024], BF16)
    s_bct = dscr("s_bct", [NCH, 128, 2048], BF16)
    s_small = dscr("s_small", [NT, 192], F32)
    s_prev = dscr("s_prev", [NCH, 128, 2048], BF16)
    g_prev = dscr("g_prev", [NCH, 128, 1024], BF16)
    s_acs = dscr("s_acs", [NCH, 128, 128], BF16)

    S = Sched(nc)
    for e in Sched.ENG:
        S.sem(e)

    def V(fn, r=(), w=()):
        S.op("vector", fn, reads=r, writes=w)

    def A(fn, r=(), w=()):
        S.op("scalar", fn, reads=r, writes=w)

    def G(fn, r=(), w=()):
        S.op("gpsimd", fn, reads=r, writes=w)

    def P(fn, r=(), w=()):
        S.op("tensor", fn, reads=r, writes=w)

    def LD(out_t, out_ap, in_ap, eng="sync"):
        S.op(eng, lambda e: e.dma_start(out=out_ap, in_=in_ap), writes=[out_t], dma=True,
             semkey="ld_" + out_t.name)

    def ST(in_t, out_ap, in_ap, eng="gpsimd"):
        S.op(eng, lambda e: e.dma_start(out=out_ap, in_=in_ap), reads=[in_t], dma=True,
             semkey="st_" + in_t.name)

    PS = [S.tile([128, 512], F32, f"ps{i}", psum=True) for i in range(6)]
    PSB = [S.tile([128, 1024], BF16, f"psb{i}", psum=True) for i in range(2)]
    st_ps = {"i": 0, "b": 0}

    def nps():
        st_ps["i"] += 1
        return PS[st_ps["i"] % 6]

    def npsb():
        st_ps["b"] += 1
        return PSB[st_ps["b"] % 2]

    identf = S.tile([128, 128], F32, "identf")
    identb = S.tile([128, 128], BF16, "identb")
    Mle = S.tile([128, 128], F32, "Mle")
    Mge = S.tile([128, 128], F32, "Mge")
    Mgt = S.tile([128, 128], F32, "Mgt")
    Mlt = S.tile([128, 128], F32, "Mlt")
    onesf = S.tile([128, 128], F32, "onesf")
    MleB = S.tile([128, 128], BF16, "MleB")
    MgeB = S.tile([128, 128], BF16, "MgeB")

    def tri(t, pat, cm, cmp):
        G(lambda e: e.memset(t[:], 1.0), w=[t])
        G(lambda e: e.affine_select(out=t[:], in_=t[:], pattern=[[pat, 128]], compare_op=cmp, fill=0.0,
                                    base=0, channel_multiplier=cm), r=[t], w=[t])

    tri(identf, -1, 1, ALU.is_equal)
    tri(Mle, 1, -1, ALU.is_ge)
    tri(Mge, -1, 1, ALU.is_ge)
    tri(Mgt, -1, 1, ALU.is_gt)
    tri(Mlt, 1, -1, ALU.is_gt)
    G(lambda e: e.memset(onesf[:], 1.0), w=[onesf])
    V(lambda e: e.tensor_copy(out=identb[:], in_=identf[:]), r=[identf], w=[identb])
    V(lambda e: e.tensor_copy(out=MleB[:], in_=Mle[:]), r=[Mle], w=[MleB])
    V(lambda e: e.tensor_copy(out=MgeB[:], in_=Mge[:]), r=[Mge], w=[MgeB])
    MgtB = S.tile([128, 128], BF16, "MgtB")
    MltB = S.tile([128, 128], BF16, "MltB")
    onesb = S.tile([128, 128], BF16, "onesb")
    V(lambda e: e.tensor_copy(out=MgtB[:], in_=Mgt[:]), r=[Mgt], w=[MgtB])
    V(lambda e: e.tensor_copy(out=MltB[:], in_=Mlt[:]), r=[Mlt], w=[MltB])
    V(lambda e: e.tensor_copy(out=onesb[:], in_=onesf[:]), r=[onesf], w=[onesb])

    dbg_i = {"i": 0}
    dbg_names = {}

    def DBG(name, t, ap, ncols, parts=128, dt=F32):
        if not dbg:
            return
        i = dbg_i["i"]
        dbg_i["i"] += 1
        assert i < ndbg
        dbg_names[name] = (i, dt == F32)
        dst = dbg32 if dt == F32 else dbg16
        ST(t, dst[i][0:parts, 0:ncols], ap)

    def load_w(dst, dcol0, src, scol0, ncols, K, stg, rowscale=None):
        srcv = src.rearrange("(k p) c -> p k c", p=128)
        engs = ["vector", "scalar", "gpsimd", "vector", "scalar"]
        i = 0
        c = 0
        while c < ncols:
            cc = min(512, ncols - c)
            kk_max = max(1, 2048 // cc)
            k0 = 0
            while k0 < K:
                kk = min(kk_max, K - k0)
                st = stg[i % len(stg)]
                stv = st[:, 0:kk * cc].rearrange("p (k c) -> p k c", k=kk)
                LD(st, stv, srcv[:, k0:k0 + kk, scol0 + c:scol0 + c + cc])
                eng = engs[i % 5]
                if rowscale is None:
                    groups = [(stv, dst[:, k0:k0 + kk, dcol0 + c:dcol0 + c + cc], None)]
                else:
                    groups = [(stv[:, q, :], dst[:, k0 + q, dcol0 + c:dcol0 + c + cc], rowscale[:, k0 + q:k0 + q + 1]) for q in range(kk)]
                for (sv, dv, rs) in groups:
                    rd = [st] if rs is None else [st, rowscale]
                    if eng == "scalar":
                        if rs is None:
                            A(lambda e, dv=dv, sv=sv: e.activation(out=dv, in_=sv, func=AF.Copy), r=rd, w=[])
                        else:
                            A(lambda e, dv=dv, sv=sv, rs=rs: e.activation(out=dv, in_=sv, func=AF.Copy, scale=rs), r=rd, w=[])
                    else:
                        fnE = V if eng == "vector" else G
                        if rs is None:
                            fnE(lambda e, dv=dv, sv=sv: e.tensor_copy(out=dv, in_=sv), r=rd, w=[])
                        else:
                            fnE(lambda e, dv=dv, sv=sv, rs=rs: e.tensor_scalar(out=dv, in0=sv, scalar1=rs, scalar2=None, op0=ALU.mult), r=rd, w=[])
                i += 1
                k0 += kk
            c += cc

    ssr = [S.tile([128, 8], F32, f"ssr{i}") for i in range(4)]
    ssr_i = {"i": 0}
    junk = S.tile([128, 2048], BF16, "junk")

    def rmsnorm(xt, xap, g_t, gap, ut, uap, Dn):
        ssr_i["i"] += 1
        ss = ssr[ssr_i["i"] % 4]
        V(lambda e: e.memset(ss[:, 0:2], 0.0), w=[ss])
        A(lambda e: e.activation(out=junk[:, 0:Dn], in_=xap, func=AF.Square, accum_out=ss[:, 0:1]),
          r=[xt, ss], w=[junk, ss])
        V(lambda e: e.tensor_scalar(out=ss[:, 1:2], in0=ss[:, 0:1], scalar1=1.0 / Dn, scalar2=EPS,
                                    op0=ALU.mult, op1=ALU.add), r=[ss], w=[ss])
        A(lambda e: e.activation(out=ss[:, 1:2], in_=ss[:, 1:2], func=AF.Ln), r=[ss], w=[ss])
        A(lambda e: e.activation(out=ss[:, 1:2], in_=ss[:, 1:2], func=AF.Exp, scale=-0.5), r=[ss], w=[ss])
        if g_t is None:
            V(lambda e: e.tensor_scalar(out=uap, in0=xap, scalar1=ss[:, 1:2], scalar2=None, op0=ALU.mult), r=[xt, ss], w=[ut])
        else:
            V(lambda e: e.scalar_tensor_tensor(out=uap, in0=xap, scalar=ss[:, 1:2], in1=gap,
                                               op0=ALU.mult, op1=ALU.mult), r=[xt, ss, g_t], w=[ut])

    def transpose_to(src_t, src_ap_fn, ntiles, dst_t, dst_ap_fn):
        i = 0
        while i < ntiles:
            n = min(8, ntiles - i)
            pb = npsb()
            for j in range(n):
                P(lambda e, pb=pb, j=j, i=i: e.transpose(pb[:, j * 128:(j + 1) * 128], src_ap_fn(i + j), identb[:]),
                  r=[src_t, identb], w=[pb])
            V(lambda e, pb=pb, n=n, i=i: e.tensor_copy(out=dst_ap_fn(i, n), in_=pb[:, 0:n * 128].rearrange("p (a b) -> p a b", a=n)), r=[pb], w=[dst_t])
            i += n

    class Locks:
        def __init__(self):
            self.front = False
            self.back = False

        def release_front(self):
            self.front = False

        def back_free(self):
            return not self.back

        def acquire_back(self):
            self.back = True

        def release_back(self):
            self.back = False

    def run_pipeline(items, body, depth=2):
        L = Locks()
        active = []
        idx = 0
        while active or idx < len(items):
            if idx < len(items) and not L.front and len(active) < depth:
                L.front = True
                active.append(body(items[idx], L))
                idx += 1
            for g in list(active):
                try:
                    next(g)
                except StopIteration:
                    active.remove(g)

    def bcast_load(t, ap_row, n):
        LD(t, t[:, 0:n], ap_row.partition_broadcast(128))

    def phase_mlp(layer, Xsrc, Xdst, final):
        m0 = S.mark()
        Wup = S.tile([128, 8, DFF], BF16, "Wup")
        Wdn = S.tile([128, 32, DM], BF16, "Wdn")
        gcm = S.tile([128, 8], F32, "gcm")
        LD(gcm, gcm[:], norm_mlp_gc[layer])
        if final:
            gf_bc = S.tile([128, DM], F32, "gf_bc")
            bcast_load(gf_bc, norm_final_g[0:1, :], DM)
        m1 = S.mark()
        stg = [S.tile([128, 2048], F32, f"stg{i}") for i in range(3)]
        load_w(Wup, 0, mlp_w_up[layer], 0, DFF, 8, stg, rowscale=gcm)
        load_w(Wdn, 0, mlp_w_down[layer], 0, DM, 32, stg)
        S.barrier()
        S.release(m1)
        TT = 256
        xts = [S.tile([128, 2, DM], F32, f"mx{i}") for i in range(2)]
        xos = [S.tile([128, 2, DM], F32, "mo0")] * 2
        hTs = [S.tile([128, 32, TT], BF16, f"mhT{i}") for i in range(2)]
        u = S.tile([128, DM], BF16, "mu")
        uT = S.tile([128, 8, TT], BF16, "muT")
        rl = [S.tile([128, 512], F32, f"mrl{i}") for i in range(2)]

        def body(t, L):
            xt, xo, hT = xts[t % 2], xos[t % 2], hTs[t % 2]
            LD(xt, xt[:], Xsrc[t * TT:(t + 1) * TT, :].rearrange("(j p) d -> p j d", p=128))
            yield
            for j in range(2):
                rmsnorm(xt, xt[:, j, :], None, None, u, u[:], DM)
                yield
                yield
                transpose_to(u, lambda i: u[:, i * 128:(i + 1) * 128], 8, uT,
                             lambda i0, n, j=j: uT[:, i0:i0 + n, j * 128:(j + 1) * 128])
                yield
            for fp in range(16):
                ps = nps()
                for f2 in range(2):
                    f = fp * 2 + f2
                    for k in range(8):
                        P(lambda e, ps=ps, f=f, f2=f2, k=k: e.matmul(ps[:, f2 * TT:(f2 + 1) * TT],
                                                                     lhsT=Wup[:, k, f * 128:(f + 1) * 128],
                                                                     rhs=uT[:, k, :], start=(k == 0), stop=(k == 7)),
                          r=[Wup, uT], w=[ps])
                r_ = rl[fp % 2]
                A(lambda e, ps=ps, r_=r_: e.activation(out=r_[:], in_=ps[:], func=AF.Relu), r=[ps], w=[r_])
                V(lambda e, r_=r_, fp=fp, hT=hT: e.tensor_tensor(out=hT[:, fp * 2:fp * 2 + 2, :],
                                                                 in0=r_[:].rearrange("p (a b) -> p a b", a=2),
                                                                 in1=r_[:].rearrange("p (a b) -> p a b", a=2), op=ALU.mult),
                  r=[r_], w=[hT])
                yield
            L.release_front()
            while not L.back_free():
                yield
            L.acquire_back()
            for j in range(2):
                for cb in range(2):
                    ps = nps()
                    for f in range(32):
                        P(lambda e, ps=ps, f=f, j=j, cb=cb, hT=hT: e.matmul(ps[:], lhsT=hT[:, f, j * 128:(j + 1) * 128],
                                                                            rhs=Wdn[:, f, cb * 512:(cb + 1) * 512],
                                                                            start=(f == 0), stop=(f == 31)),
                          r=[hT, Wdn], w=[ps])
                        if f % 8 == 7 and f != 31:
                            yield
                    V(lambda e, ps=ps, j=j, cb=cb, xo=xo, xt=xt: e.tensor_tensor(
                        out=xo[:, j, cb * 512:(cb + 1) * 512], in0=ps[:], in1=xt[:, j, cb * 512:(cb + 1) * 512],
                        op=ALU.add), r=[ps, xt], w=[xo])
                    yield
                if final:
                    rmsnorm(xo, xo[:, j, :], gf_bc, gf_bc[:], xo, xo[:, j, :], DM)
            ST(xo, Xdst[t * TT:(t + 1) * TT, :].rearrange("(j p) d -> p j d", p=128), xo[:])
            L.release_back()

        run_pipeline(list(range(NT // TT)), body)
        S.barrier()
        S.release(m0)

    def ssd_small_consts():
        hp = S.tile([128, 160], F32, "hp")
        bcast_load(hp, ssd_hp[0:1, :], 160)
        A(lambda e: e.activation(out=hp[:, 64:128], in_=hp[:, 64:128], func=AF.Exp), r=[hp], w=[hp])
        V(lambda e: e.tensor_scalar(out=hp[:, 64:128], in0=hp[:, 64:128], scalar1=-1.0, scalar2=None, op0=ALU.mult),
          r=[hp], w=[hp])
        return hp

    def ssd_phase_a():
        m0 = S.mark()
        Wx = S.tile([128, 8, 4160], BF16, "Wx")
        diag = S.tile([128, 32, 7, 128], BF16, "diag")
        cb = S.tile([128, 32], F32, "cb")
        gcx = S.tile([128, 8], F32, "gcx")
        hp = ssd_small_consts()
        LD(cb, cb[:], ssd_cb)
        LD(gcx, gcx[:], norm_mix_gc[0])
        m1 = S.mark()
        cw = S.tile([128, 32, 7], F32, "cw")
        LD(cw, cw[:], ssd_cw)
        stg = [S.tile([128, 2048], F32, f"stg{i}") for i in range(2)]
        load_w(Wx, 0, ssd_w_in, DI, 4096, 8, stg, rowscale=gcx)
        load_w(Wx, 4096, ssd_w_in, DI + 4096, 64, 8, stg, rowscale=gcx)
        di = 0
        for m in range(32):
            for j in range(7):
                di += 1
                if di % 5 in (0, 2):
                    V(lambda e, m=m, j=j: e.tensor_scalar(out=diag[:, m, j, :], in0=identf[:], scalar1=cw[:, m, j:j + 1],
                                                          scalar2=None, op0=ALU.mult), r=[identf, cw], w=[])
                elif di % 5 in (1, 3):
                    A(lambda e, m=m, j=j: e.activation(out=diag[:, m, j, :], in_=identf[:], func=AF.Copy, scale=cw[:, m, j:j + 1]),
                      r=[identf, cw], w=[])
                else:
                    G(lambda e, m=m, j=j: e.tensor_scalar(out=diag[:, m, j, :], in0=identf[:], scalar1=cw[:, m, j:j + 1],
                                                          scalar2=None, op0=ALU.mult), r=[identf, cw], w=[])
        S.barrier()
        S.release(m1)
        xt = S.tile([128, DM], F32, "ax")
        u = S.tile([128, DM], BF16, "au")
        uT = S.tile([128, 8, 256], BF16, "auT")
        pre = [S.tile([128, 32, 262], BF16, f"pre{i}") for i in range(2)]
        lh = S.tile([128, 32, 3], BF16, "lh")
        xbcT = S.tile([128, 16, 256], BF16, "xbcT")
        xtok = S.tile([128, DI], BF16, "xtok")
        btoks = [S.tile([128, 1024], BF16, f"btok{i}") for i in range(2)]
        sms = [S.tile([128, 192], F32, f"sm{i}") for i in range(6)]
        ex = S.tile([128, 64], F32, "ex")
        wf = S.tile([128, 32], F32, "wf")
        Sf = S.tile([128, DI], F32, "Sf")
        prevb = S.tile([128, DI], BF16, "prevb")
        acs = S.tile([128, 64], F32, "acs")
        acshl = S.tile([128, 128], BF16, "acshl")
        acsT = S.tile([128, 128], BF16, "acsT")

        for (cs, C) in seq_chunks:
            NSC = C // 2
            V(lambda e: e.memset(Sf[:], 0.0), w=[Sf])

            def stage1_a1(sc, jj):
                gc = cs + 2 * sc + jj
                LD(xt, xt[:], x_in[gc * 128:(gc + 1) * 128, :])
                rmsnorm(xt, xt[:], None, None, u, u[:], DM)

            def stage1(sc, hoisted):
                sl = pre[sc % 2]
                for jj in range(2):
                    gc = cs + 2 * sc + jj
                    sm = sms[gc % 6]
                    if not (jj == 0 and hoisted):
                        stage1_a1(sc, jj)
                    transpose_to(u, lambda i: u[:, i * 128:(i + 1) * 128], 8, uT,
                                 lambda i0, n, jj=jj: uT[:, i0:i0 + n, jj * 128:(jj + 1) * 128])
                    ps = nps()
                    for k in range(8):
                        P(lambda e, ps=ps, k=k, jj=jj: e.matmul(ps[:, 0:64], lhsT=uT[:, k, jj * 128:(jj + 1) * 128], rhs=Wx[:, k, 4096:4160],
                                                                start=(k == 0), stop=(k == 7)), r=[Wx, uT], w=[ps])
                    V(lambda e, ps=ps, sm=sm: e.tensor_tensor(out=sm[:, 0:64], in0=ps[:, 0:64], in1=hp[:, 0:64], op=ALU.add),
                      r=[ps, hp], w=[sm])
                    A(lambda e, sm=sm: e.activation(out=sm[:, 0:64], in_=sm[:, 0:64], func=AF.Exp), r=[sm], w=[sm])
                    A(lambda e, sm=sm: e.activation(out=sm[:, 0:64], in_=sm[:, 0:64], func=AF.Ln, bias=1.0), r=[sm], w=[sm])
                    A(lambda e, sm=sm: e.activation(out=sm[:, 64:128], in_=sm[:, 0:64], func=AF.Ln), r=[sm], w=[sm])
                    V(lambda e, sm=sm: e.tensor_tensor(out=sm[:, 128:192], in0=sm[:, 0:64], in1=hp[:, 64:128], op=ALU.mult),
                      r=[sm, hp], w=[sm])
                    yield
                if sc == 0:
                    G(lambda e, sl=sl: e.memset(sl[:, :, 0:3], 0.0), w=[sl])
                else:
                    G(lambda e, sl=sl: e.tensor_copy(out=sl[:, :, 0:3], in_=lh[:]), r=[lh], w=[sl])
                for mb in range(16):
                    ps = nps()
                    for m2 in range(2):
                        m = mb * 2 + m2
                        for k in range(8):
                            P(lambda e, ps=ps, m=m, m2=m2, k=k: e.matmul(ps[:, m2 * 256:(m2 + 1) * 256],
                                                                         lhsT=Wx[:, k, m * 128:(m + 1) * 128],
                                                                         rhs=uT[:, k, :], start=(k == 0), stop=(k == 7)),
                              r=[Wx, uT], w=[ps])
                    A(lambda e, ps=ps, mb=mb, sl=sl: e.activation(out=sl[:, mb * 2:mb * 2 + 2, 3:259],
                                                                  in_=ps[:].rearrange("p (a b) -> p a b", a=2),
                                                                  func=AF.Copy), r=[ps], w=[sl])
                    yield
                G(lambda e, sl=sl: e.tensor_copy(out=lh[:], in_=sl[:, :, 256:259]), r=[sl], w=[lh])
                if sc > 0:
                    slp = pre[(sc - 1) % 2]
                    G(lambda e, sl=sl, slp=slp: e.tensor_copy(out=slp[:, :, 259:262], in_=sl[:, :, 3:6]), r=[sl], w=[slp])
                if sc == NSC - 1:
                    G(lambda e, sl=sl: e.memset(sl[:, :, 259:262], 0.0), w=[sl])

            def conv_half(sc, half):
                sl = pre[sc % 2]
                for mb in range(8):
                    ps = nps()
                    for m2 in range(2):
                        m = half * 16 + mb * 2 + m2
                        for j in range(7):
                            P(lambda e, ps=ps, m=m, m2=m2, j=j, sl=sl: e.matmul(ps[:, m2 * 256:(m2 + 1) * 256],
                                                                                lhsT=diag[:, m, j, :], rhs=sl[:, m, j:j + 256],
                                                                                start=(j == 0), stop=(j == 6)),
                              r=[diag, sl], w=[ps])
                    for m2 in range(2):
                        m = half * 16 + mb * 2 + m2
                        A(lambda e, ps=ps, m=m, m2=m2, mb=mb: e.activation(out=xbcT[:, mb * 2 + m2, :], in_=ps[:, m2 * 256:(m2 + 1) * 256],
                                                                           func=AF.Silu, bias=cb[:, m:m + 1]),
                          r=[ps, cb], w=[xbcT])

            def conv_part(sc):
                conv_half(sc, 1)
                for jj in range(2):
                    gc = cs + 2 * sc + jj
                    btok = btoks[jj]
                    transpose_to(xbcT, lambda i, jj=jj: xbcT[:, i, jj * 128:(jj + 1) * 128], 8, btok,
                                 lambda i0, n, btok=btok: btok[:, i0 * 128:(i0 + n) * 128].rearrange("p (a b) -> p a b", a=n))
                    ST(btok, s_btok[gc * 128:(gc + 1) * 128, :], btok[:])
                    ST(xbcT, s_bct[gc].rearrange("p (a b) -> p a b", a=16), xbcT[:, :, jj * 128:(jj + 1) * 128])
                conv_half(sc, 0)

            def tails(sc):
                for jj in range(2):
                    gc = cs + 2 * sc + jj
                    sm = sms[gc % 6]
                    btok = btoks[jj]
                    transpose_to(xbcT, lambda i, jj=jj: xbcT[:, i, jj * 128:(jj + 1) * 128], 16, xtok,
                                 lambda i0, n: xtok[:, i0 * 128:(i0 + n) * 128].rearrange("p (a b) -> p a b", a=n))
                    ST(xtok, s_xtok[gc * 128:(gc + 1) * 128, :], xtok[:])
                    yield
                    ps = nps()
                    P(lambda e, ps=ps, sm=sm: e.matmul(ps[:, 0:32], lhsT=Mle[:], rhs=sm[:, 128:160], start=True, stop=True), r=[Mle, sm], w=[ps])
                    P(lambda e, ps=ps, sm=sm: e.matmul(ps[:, 32:64], lhsT=Mge[:], rhs=sm[:, 160:192], start=True, stop=True), r=[Mge, sm], w=[ps])
                    V(lambda e, ps=ps: e.tensor_copy(out=acs[:], in_=ps[:, 0:64]), r=[ps], w=[acs])
                    V(lambda e, sm=sm: e.tensor_tensor(out=sm[:, 64:128], in0=sm[:, 64:128], in1=acs[:], op=ALU.subtract), r=[sm, acs], w=[sm])
                    ST(sm, s_small[gc * 128:(gc + 1) * 128, :], sm[:])
                    yield
                    V(lambda e: e.tensor_copy(out=acshl[:, 0:64], in_=acs[:]), r=[acs], w=[acshl])
                    V(lambda e: e.tensor_tensor(out=acs[:], in0=acs[:], in1=acshl[:, 0:64], op=ALU.subtract), r=[acs, acshl], w=[acs])
                    V(lambda e: e.tensor_copy(out=acshl[:, 64:128], in_=acs[:]), r=[acs], w=[acshl])
                    pb = npsb()
                    P(lambda e, pb=pb: e.transpose(pb[:, 0:128], acshl[:], identb[:]), r=[acshl, identb], w=[pb])
                    V(lambda e, pb=pb: e.tensor_copy(out=acsT[:], in_=pb[:, 0:128]), r=[pb], w=[acsT])
                    ST(acsT, s_acs[gc], acsT[:])
                    yield
                    ps = nps()
                    P(lambda e, ps=ps, sm=sm: e.matmul(ps[:, 0:32], lhsT=Mgt[:], rhs=sm[:, 128:160], start=True, stop=True),
                      r=[Mgt, sm], w=[ps])
                    P(lambda e, ps=ps, sm=sm: e.matmul(ps[:, 32:64], lhsT=onesf[:], rhs=sm[:, 128:160], start=True, stop=True),
                      r=[onesf, sm], w=[ps])
                    A(lambda e, ps=ps: e.activation(out=ex[:], in_=ps[:, 0:64], func=AF.Exp), r=[ps], w=[ex])
                    V(lambda e, sm=sm: e.tensor_tensor(out=wf[:], in0=sm[:, 0:32], in1=ex[:, 0:32], op=ALU.mult), r=[sm, ex], w=[wf])
                    V(lambda e: e.tensor_tensor(out=xtok[:].rearrange("p (h d) -> p h d", h=32),
                                                in0=xtok[:].rearrange("p (h d) -> p h d", h=32),
                                                in1=wf[:].unsqueeze(2).to_broadcast([128, 32, 64]), op=ALU.mult),
                      r=[xtok, wf], w=[xtok])
                    yield
                    V(lambda e: e.tensor_copy(out=prevb[:], in_=Sf[:]), r=[Sf], w=[prevb])
                    ST(prevb, s_prev[gc], prevb[:])
                    V(lambda e: e.tensor_tensor(out=Sf[:].rearrange("p (h d) -> p h d", h=32),
                                                in0=Sf[:].rearrange("p (h d) -> p h d", h=32),
                                                in1=ex[:, 32:64].unsqueeze(2).to_broadcast([128, 32, 64]), op=ALU.mult),
                      r=[Sf, ex], w=[Sf])
                    for g2 in range(4):
                        ps = nps()
                        for gg in range(2):
                            g = g2 * 2 + gg
                            P(lambda e, ps=ps, g=g, gg=gg, btok=btok: e.matmul(ps[:, gg * 256:(gg + 1) * 256],
                                                                              lhsT=btok[:, g * 128:(g + 1) * 128],
                                                                              rhs=xtok[:, g * 256:(g + 1) * 256], start=True, stop=True),
                              r=[btok, xtok], w=[ps])
                        V(lambda e, ps=ps, g2=g2: e.tensor_tensor(out=Sf[:, g2 * 512:(g2 + 1) * 512],
                                                                  in0=Sf[:, g2 * 512:(g2 + 1) * 512], in1=ps[:], op=ALU.add),
                          r=[Sf, ps], w=[Sf])
                        yield

            def interleave(ga, gb):
                gens = [g for g in (ga, gb) if g is not None]
                while gens:
                    for g in list(gens):
                        try:
                            next(g)
                        except StopIteration:
                            gens.remove(g)

            interleave(stage1(0, False), None)
            if NSC > 1:
                stage1_a1(1, 0)
            for sc in range(NSC):
                ga = stage1(sc + 1, True) if sc + 1 < NSC else None
                gb = tails(sc - 1) if sc >= 1 else None
                interleave(ga, gb)
                if sc + 2 < NSC:
                    stage1_a1(sc + 2, 0)
                conv_part(sc)
            interleave(tails(NSC - 1), None)
        S.barrier()
        S.release(m0)

    def ssd_phase_b(Xdst):
        m0 = S.mark()
        Wz = S.tile([128, 8, DI], BF16, "Wz")
        Wo = S.tile([128, 16, DM], BF16, "Wo")
        gcz = S.tile([128, 8], F32, "gcz")
        gco = S.tile([128, 16], F32, "gco")
        hp = ssd_small_consts()
        LD(gcz, gcz[:], norm_mix_gc[0])
        LD(gco, gco[:], ssd_norm_gc)
        m1 = S.mark()
        stg = [S.tile([128, 2048], F32, f"stg{i}") for i in range(3)]
        load_w(Wz, 0, ssd_w_in, 0, DI, 8, stg, rowscale=gcz)
        load_w(Wo, 0, ssd_w_out, 0, DM, 16, stg, rowscale=gco)
        S.barrier()
        S.release(m1)
        xts = [S.tile([128, DM], F32, f"bx{i}") for i in range(2)]
        xtoks = [S.tile([128, DI], BF16, f"bxtok{i}") for i in range(2)]
        btoks = [S.tile([128, 1024], BF16, f"bbtok{i}") for i in range(2)]
        bcts = [S.tile([128, 16, 128], BF16, f"bbct{i}") for i in range(2)]
        smsb = [S.tile([128, 192], F32, f"bsm{i}") for i in range(2)]
        prevfs = [S.tile([128, DI], BF16, f"bprevf{i}") for i in range(2)]
        zss = [S.tile([128, DI], BF16, f"zs{i}") for i in range(2)]
        evs = [S.tile([128, 128], F32, f"ev{i}") for i in range(2)]
        yaccs = [S.tile([128, DI], F32, f"yacc{i}") for i in range(2)]
        xos = [S.tile([128, DM], F32, f"bxo{i}") for i in range(2)]
        RB = S.tile([128, 3072], BF16, "RB")
        u = S.tile([128, DM], BF16, "bu")
        uT = S.tile([128, 8, 128], BF16, "buT")
        sc = S.tile([128, 8, 128], BF16, "sc")
        Wr = [S.tile([128, 4, 128], BF16, f"Wr{i}") for i in range(8)]
        mneg = [S.tile([128, 4, 128], BF16, f"mneg{i}") for i in range(2)]
        for d, Mm in enumerate([Mle, Mge]):
            V(lambda e, d=d, Mm=Mm: e.tensor_scalar(out=mneg[d][:], in0=Mm[:].unsqueeze(1).to_broadcast([128, 4, 128]), scalar1=-1.0, scalar2=30000.0,
                                                    op0=ALU.add, op1=ALU.mult), r=[Mm], w=[mneg[d]])
        t1r = [S.tile([128, 512], F32, f"t1r{i}") for i in range(2)]
        wb = S.tile([128, 32], F32, "wb")
        xw = S.tile([128, DI], BF16, "bxw")
        Sb = S.tile([128, DI], F32, "Sb")
        prevb = S.tile([128, DI], BF16, "bprevb")
        yn = S.tile([128, DI], BF16, "yn")
        ynT = S.tile([128, 16, 128], BF16, "ynT")
        st = {"ri": 0, "ti": 0}

        def body(item, L):
            gc, first = item
            i2 = gc % 2
            xt, xtok, btok, bct, sm, prevf, zs, ev, yacc, xo = (xts[i2], xtoks[i2], btoks[i2], bcts[i2], smsb[i2], prevfs[i2],
                                                                zss[i2], evs[i2], yaccs[i2], xos[i2])
            LD(xt, xt[:], x_in[gc * 128:(gc + 1) * 128, :])
            LD(sm, sm[:], s_small[gc * 128:(gc + 1) * 128, :])
            LD(bct, bct[:].rearrange("p a b -> p (a b)"), s_bct[gc])
            acv = s_acs[gc].rearrange("(hl u h4) l -> hl u (h4 l)", hl=2, u=16)
            for b in range(3):
                n = len(range(b, 16, 3))
                for hl in range(2):
                    LD(RB, RB[32 * b + hl:32 * b + hl + 1, 0:n * 512].rearrange("p (s x) -> p s x", s=n), acv[hl:hl + 1, b::3, :])
            LD(xtok, xtok[:], s_xtok[gc * 128:(gc + 1) * 128, :])
            LD(btok, btok[:], s_btok[gc * 128:(gc + 1) * 128, :])
            LD(prevf, prevf[:], s_prev[gc])
            yield
            rmsnorm(xt, xt[:], None, None, u, u[:], DM)
            yield
            yield
            yield
            transpose_to(u, lambda i: u[:, i * 128:(i + 1) * 128], 8, uT, lambda i0, n: uT[:, i0:i0 + n, :])
            yield
            for cbk in range(4):
                ps = nps()
                for k in range(8):
                    P(lambda e, ps=ps, k=k, cbk=cbk: e.matmul(ps[:], lhsT=uT[:, k, :], rhs=Wz[:, k, cbk * 512:(cbk + 1) * 512],
                                                              start=(k == 0), stop=(k == 7)), r=[uT, Wz], w=[ps])
                A(lambda e, ps=ps, cbk=cbk, zs=zs: e.activation(out=zs[:, cbk * 512:(cbk + 1) * 512], in_=ps[:], func=AF.Silu),
                  r=[ps], w=[zs])
                yield
            ps = nps()
            P(lambda e, ps=ps, sm=sm: e.matmul(ps[:, 0:32], lhsT=Mle[:], rhs=sm[:, 128:160], start=True, stop=True), r=[Mle, sm], w=[ps])
            P(lambda e, ps=ps, sm=sm: e.matmul(ps[:, 32:64], lhsT=Mge[:], rhs=sm[:, 160:192], start=True, stop=True), r=[Mge, sm], w=[ps])
            P(lambda e, ps=ps, sm=sm: e.matmul(ps[:, 64:96], lhsT=Mlt[:], rhs=sm[:, 160:192], start=True, stop=True), r=[Mlt, sm], w=[ps])
            P(lambda e, ps=ps, sm=sm: e.matmul(ps[:, 96:128], lhsT=onesf[:], rhs=sm[:, 160:192], start=True, stop=True), r=[onesf, sm], w=[ps])
            A(lambda e, ps=ps, ev=ev: e.activation(out=ev[:], in_=ps[:, 0:128], func=AF.Exp), r=[ps], w=[ev])
            for g2 in range(2):
                ps = nps()
                for g4 in range(4):
                    g = g2 * 4 + g4
                    P(lambda e, ps=ps, g=g, g4=g4, bct=bct: e.matmul(ps[:, g4 * 128:(g4 + 1) * 128], lhsT=bct[:, g, :],
                                                                     rhs=bct[:, 8 + g, :], start=True, stop=True), r=[bct], w=[ps])
                A(lambda e, ps=ps, g2=g2: e.activation(out=sc[:, g2 * 4:g2 * 4 + 4, :], in_=ps[:].rearrange("p (a b) -> p a b", a=4),
                                                       func=AF.Copy), r=[ps], w=[sc])
            yield
            Wts = {}

            def P1(g):
                for d in range(2):
                    Wt = Wr[st["ri"] % 8]
                    st["ri"] += 1
                    uu = d * 8 + g
                    b, slot = uu % 3, uu // 3
                    ps = nps()
                    P(lambda e, ps=ps, b=b, slot=slot: e.matmul(ps[:], lhsT=onesb[32 * b:32 * b + 2, :], rhs=RB[32 * b:32 * b + 2, slot * 512:(slot + 1) * 512],
                                                                start=True, stop=False), r=[onesb, RB], w=[ps])
                    P(lambda e, ps=ps, d=d: e.matmul(ps[:], lhsT=identb[:], rhs=mneg[d][:].rearrange("p a b -> p (a b)"),
                                                     start=False, stop=True), r=[identb, mneg[d]], w=[ps])
                    for h4 in range(4):
                        h = g * 4 + h4
                        A(lambda e, ps=ps, Wt=Wt, h4=h4, h=h, d=d, sm=sm: e.activation(
                            out=Wt[:, h4, :], in_=ps[:, h4 * 128:(h4 + 1) * 128], func=AF.Exp,
                            bias=sm[:, 64 + d * 32 + h:64 + d * 32 + h + 1]), r=[ps, sm], w=[Wt])
                    V(lambda e, Wt=Wt, g=g: e.tensor_tensor(out=Wt[:], in0=Wt[:],
                                                            in1=sc[:, g:g + 1, :].to_broadcast([128, 4, 128]), op=ALU.mult),
                      r=[Wt, sc], w=[Wt])
                    Wts[(g, d)] = Wt

            def P2(g):
                psy = nps()
                for h4 in range(4):
                    h = g * 4 + h4
                    for d in range(2):
                        Wt = Wts[(g, d)]
                        P(lambda e, psy=psy, Wt=Wt, h4=h4, h=h, d=d, xtok=xtok: e.matmul(
                            psy[:, h4 * 64:(h4 + 1) * 64], lhsT=Wt[:, h4, :], rhs=xtok[:, h * 64:(h + 1) * 64],
                            start=(d == 0), stop=(d == 1)), r=[Wt, xtok], w=[psy])
                V(lambda e, psy=psy, g=g, yacc=yacc: e.tensor_copy(out=yacc[:, g * 256:(g + 1) * 256], in_=psy[:, 0:256]), r=[psy], w=[yacc])

            P1(0)
            yield
            P1(1)
            yield
            for g in range(8):
                if g + 2 < 8:
                    P1(g + 2)
                P2(g)
                yield
            L.release_front()
            while not L.back_free():
                yield
            L.acquire_back()
            if first:
                V(lambda e: e.memset(Sb[:], 0.0), w=[Sb])
            V(lambda e: e.tensor_copy(out=prevb[:], in_=Sb[:]), r=[Sb], w=[prevb])
            for d in range(2):
                pv = prevf if d == 0 else prevb
                for g2 in range(4):
                    ps = nps()
                    t1 = t1r[st["ti"] % 2]
                    st["ti"] += 1
                    for gg in range(2):
                        g = g2 * 2 + gg
                        P(lambda e, ps=ps, g=g, gg=gg, pv=pv, bct=bct: e.matmul(ps[:, gg * 256:(gg + 1) * 256], lhsT=bct[:, 8 + g, :],
                                                                                rhs=pv[:, g * 256:(g + 1) * 256], start=True, stop=True),
                          r=[bct, pv], w=[ps])
                    V(lambda e, ps=ps, g2=g2, d=d, t1=t1, ev=ev: e.tensor_tensor(
                        out=t1[:].rearrange("p (h x) -> p h x", h=8), in0=ps[:].rearrange("p (h x) -> p h x", h=8),
                        in1=ev[:, d * 32 + g2 * 8:d * 32 + g2 * 8 + 8].unsqueeze(2).to_broadcast([128, 8, 64]), op=ALU.mult),
                      r=[ps, ev], w=[t1])
                    G(lambda e, g2=g2, t1=t1, yacc=yacc: e.tensor_tensor(out=yacc[:, g2 * 512:(g2 + 1) * 512], in0=yacc[:, g2 * 512:(g2 + 1) * 512],
                                                                         in1=t1[:], op=ALU.add), r=[yacc, t1], w=[yacc])
                    yield
            for g2 in range(4):
                t1 = t1r[st["ti"] % 2]
                st["ti"] += 1
                G(lambda e, xtok=xtok, t1=t1, g2=g2: e.tensor_tensor(out=t1[:].rearrange("p (h x) -> p h x", h=8),
                                                                     in0=xtok[:, g2 * 512:(g2 + 1) * 512].rearrange("p (h x) -> p h x", h=8),
                                                                     in1=hp[:, 128 + g2 * 8:128 + g2 * 8 + 8].unsqueeze(2).to_broadcast([128, 8, 64]), op=ALU.mult),
                  r=[xtok, hp], w=[t1])
                V(lambda e, t1=t1, g2=g2, yacc=yacc: e.tensor_tensor(out=yacc[:, g2 * 512:(g2 + 1) * 512], in0=yacc[:, g2 * 512:(g2 + 1) * 512],
                                                                     in1=t1[:], op=ALU.add), r=[yacc, t1], w=[yacc])
            yield
            V(lambda e, sm=sm, ev=ev: e.tensor_tensor(out=wb[:], in0=sm[:, 32:64], in1=ev[:, 64:96], op=ALU.mult), r=[sm, ev], w=[wb])
            V(lambda e, xtok=xtok: e.tensor_tensor(out=xw[:].rearrange("p (h d) -> p h d", h=32),
                                                   in0=xtok[:].rearrange("p (h d) -> p h d", h=32),
                                                   in1=wb[:].unsqueeze(2).to_broadcast([128, 32, 64]), op=ALU.mult),
              r=[xtok, wb], w=[xw])
            G(lambda e, ev=ev: e.tensor_tensor(out=Sb[:].rearrange("p (h d) -> p h d", h=32),
                                               in0=Sb[:].rearrange("p (h d) -> p h d", h=32),
                                               in1=ev[:, 96:128].unsqueeze(2).to_broadcast([128, 32, 64]), op=ALU.mult),
              r=[Sb, ev], w=[Sb])
            for g2 in range(4):
                ps = nps()
                for gg in range(2):
                    g = g2 * 2 + gg
                    P(lambda e, ps=ps, g=g, gg=gg, btok=btok: e.matmul(ps[:, gg * 256:(gg + 1) * 256], lhsT=btok[:, g * 128:(g + 1) * 128],
                                                                      rhs=xw[:, g * 256:(g + 1) * 256], start=True, stop=True),
                      r=[btok, xw], w=[ps])
                V(lambda e, ps=ps, g2=g2: e.tensor_tensor(out=Sb[:, g2 * 512:(g2 + 1) * 512], in0=Sb[:, g2 * 512:(g2 + 1) * 512],
                                                          in1=ps[:], op=ALU.add), r=[Sb, ps], w=[Sb])
            yield
            V(lambda e, yacc=yacc, zs=zs: e.tensor_tensor(out=yacc[:], in0=yacc[:], in1=zs[:], op=ALU.mult), r=[yacc, zs], w=[yacc])
            rmsnorm(yacc, yacc[:], None, None, yn, yn[:], DI)
            yield
            yield
            yield
            yield
            transpose_to(yn, lambda i: yn[:, i * 128:(i + 1) * 128], 16, ynT, lambda i0, n: ynT[:, i0:i0 + n, :])
            yield
            for cbk in range(2):
                ps = nps()
                for k in range(16):
                    P(lambda e, ps=ps, k=k, cbk=cbk: e.matmul(ps[:], lhsT=ynT[:, k, :], rhs=Wo[:, k, cbk * 512:(cbk + 1) * 512],
                                                              start=(k == 0), stop=(k == 15)), r=[ynT, Wo], w=[ps])
                V(lambda e, ps=ps, cbk=cbk, xo=xo, xt=xt: e.tensor_tensor(out=xo[:, cbk * 512:(cbk + 1) * 512], in0=ps[:],
                                                                          in1=xt[:, cbk * 512:(cbk + 1) * 512], op=ALU.add),
                  r=[ps, xt], w=[xo])
                yield
            ST(xo, Xdst[gc * 128:(gc + 1) * 128, :], xo[:])
            L.release_back()

        items = []
        for (cs, C) in seq_chunks:
            for c in range(C - 1, -1, -1):
                items.append((cs + c, c == C - 1))
        run_pipeline(items, body)
        S.barrier()
        S.release(m0)

    def gla_common_tiles():
        k = K()
        k.Win = S.tile([128, 8, GLA_IN], BF16, "gWin")
        k.g_bc = S.tile([128, DM], F32, "gg_bc")
        k.gup = S.tile([16, 1024], F32, "gup")
        k.gupb = S.tile([16, 1024], BF16, "gupb")
        k.gbias = S.tile([128, 1024], F32, "gbias")
        bcast_load(k.g_bc, norm_mix_g[1:2, :], DM)
        bcast_load(k.gbias, gla_gate_bias[0:1, :], 1024)
        LD(k.gup, k.gup[:], gla_gate_up)
        V(lambda e: e.tensor_copy(out=k.gupb[:], in_=k.gup[:]), r=[k.gup], w=[k.gupb])
        return k

    def gla_front(k, xt, need_b, skip_norm=False):
        if not skip_norm:
            rmsnorm(xt, xt[:], k.g_bc, k.g_bc[:], k.u, k.u[:], DM)
        transpose_to(k.u, lambda i: k.u[:, i * 128:(i + 1) * 128], 8, k.uT, lambda i0, n: k.uT[:, i0:i0 + n, :])
        for d in range(2 if need_b else 1):
            ps = nps()
            for kk in range(8):
                P(lambda e, ps=ps, kk=kk, d=d: e.matmul(ps[0:16, 0:128], lhsT=k.Win[:, kk, 3072 + d * 16:3072 + d * 16 + 16],
                                                        rhs=k.uT[:, kk, :], start=(kk == 0), stop=(kk == 7)), r=[k.Win, k.uT], w=[ps])
            V(lambda e, ps=ps: e.tensor_copy(out=k.lrT[:], in_=ps[0:16, 0:128]), r=[ps], w=[k.lrT])
            ps2 = nps()
            P(lambda e, ps2=ps2, d=d: e.matmul(ps2[:], lhsT=k.lrT[:], rhs=k.gupb[:, d * 512:(d + 1) * 512], start=True, stop=True),
              r=[k.lrT, k.gupb], w=[ps2])
            lg = k.lg[d]
            V(lambda e, ps2=ps2, d=d, lg=lg: e.tensor_tensor(out=lg[:], in0=ps2[:], in1=k.gbias[:, d * 512:(d + 1) * 512], op=ALU.add),
              r=[ps2, k.gbias], w=[lg])
            A(lambda e, lg=lg: e.activation(out=lg[:], in_=lg[:], func=AF.Exp, scale=-1.0), r=[lg], w=[lg])
            A(lambda e, lg=lg: e.activation(out=lg[:], in_=lg[:], func=AF.Ln, bias=1.0), r=[lg], w=[lg])
            V(lambda e, lg=lg: e.tensor_scalar(out=lg[:], in0=lg[:], scalar1=-1.0 / 16.0, scalar2=None, op0=ALU.mult), r=[lg], w=[lg])
            lgh, lgl = k.lgh[d], k.lgl[d]
            V(lambda e, lg=lg, lgh=lgh: e.tensor_copy(out=lgh[:], in_=lg[:]), r=[lg], w=[lgh])
            G(lambda e, lg=lg, lgh=lgh: e.tensor_tensor(out=k.lgt[:], in0=lg[:], in1=lgh[:], op=ALU.subtract), r=[lg, lgh], w=[k.lgt])
            G(lambda e, lgl=lgl: e.tensor_copy(out=lgl[:], in_=k.lgt[:]), r=[k.lgt], w=[lgl])

    def gla_state_update(k, d, lT, Sst, ktok_ps_fn):
        lgh, lgl = k.lgh[d], k.lgl[d]
        ps = nps()
        P(lambda e, ps=ps: e.matmul(ps[:], lhsT=lT[:], rhs=lgh[:], start=True, stop=False), r=[lT, lgh], w=[ps])
        P(lambda e, ps=ps: e.matmul(ps[:], lhsT=lT[:], rhs=lgl[:], start=False, stop=True), r=[lT, lgl], w=[ps])
        A(lambda e, ps=ps: e.activation(out=k.kex[:], in_=ps[:], func=AF.Exp), r=[ps], w=[k.kex])
        psk = ktok_ps_fn()
        V(lambda e, psk=psk: e.tensor_tensor(out=k.kend[:], in0=psk[:], in1=k.kex[:], op=ALU.mult), r=[psk, k.kex], w=[k.kend])
        ps = nps()
        for h in range(4):
            P(lambda e, ps=ps, h=h: e.matmul(ps[:, h:h + 1], lhsT=lgh[:, h * 128:(h + 1) * 128], rhs=onesb[:, 0:1], start=True, stop=False),
              r=[lgh, onesb], w=[ps])
            P(lambda e, ps=ps, h=h: e.matmul(ps[:, h:h + 1], lhsT=lgl[:, h * 128:(h + 1) * 128], rhs=onesb[:, 0:1], start=False, stop=True),
              r=[lgl, onesb], w=[ps])
        A(lambda e, ps=ps: e.activation(out=k.cd[:], in_=ps[:, 0:4], func=AF.Exp), r=[ps], w=[k.cd])
        for h2 in range(2):
            ps = nps()
            for hh in range(2):
                h = h2 * 2 + hh
                P(lambda e, ps=ps, h=h, hh=hh: e.matmul(ps[:, hh * 256:(hh + 1) * 256], lhsT=k.kend[:, h * 128:(h + 1) * 128],
                                                        rhs=k.vtok[:, h * 256:(h + 1) * 256], start=True, stop=True),
                  r=[k.kend, k.vtok], w=[ps])
            for hh in range(2):
                h = h2 * 2 + hh
                V(lambda e, ps=ps, h=h, hh=hh: e.scalar_tensor_tensor(out=Sst[:, h * 256:(h + 1) * 256], in0=Sst[:, h * 256:(h + 1) * 256],
                                                                      scalar=k.cd[:, h:h + 1], in1=ps[:, hh * 256:(hh + 1) * 256],
                                                                      op0=ALU.mult, op1=ALU.add), r=[Sst, k.cd, ps], w=[Sst])

    def gla_state_front(k, d, lT, ktok_ps_fn, kend, cd):
        lgh, lgl = k.lgh[d], k.lgl[d]
        ps = nps()
        P(lambda e, ps=ps: e.matmul(ps[:], lhsT=lT[:], rhs=lgh[:], start=True, stop=False), r=[lT, lgh], w=[ps])
        P(lambda e, ps=ps: e.matmul(ps[:], lhsT=lT[:], rhs=lgl[:], start=False, stop=True), r=[lT, lgl], w=[ps])
        A(lambda e, ps=ps: e.activation(out=k.kex[:], in_=ps[:], func=AF.Exp), r=[ps], w=[k.kex])
        psk = ktok_ps_fn()
        V(lambda e, psk=psk: e.tensor_tensor(out=kend[:], in0=psk[:], in1=k.kex[:], op=ALU.mult), r=[psk, k.kex], w=[kend])
        ps = nps()
        for h in range(4):
            P(lambda e, ps=ps, h=h: e.matmul(ps[:, h:h + 1], lhsT=lgh[:, h * 128:(h + 1) * 128], rhs=onesb[:, 0:1], start=True, stop=False),
              r=[lgh, onesb], w=[ps])
            P(lambda e, ps=ps, h=h: e.matmul(ps[:, h:h + 1], lhsT=lgl[:, h * 128:(h + 1) * 128], rhs=onesb[:, 0:1], start=False, stop=True),
              r=[lgl, onesb], w=[ps])
        A(lambda e, ps=ps: e.activation(out=cd[:], in_=ps[:, 0:4], func=AF.Exp), r=[ps], w=[cd])

    def gla_state_back(kend, vtok, cd, Sst):
        for h2 in range(2):
            ps = nps()
            for hh in range(2):
                h = h2 * 2 + hh
                P(lambda e, ps=ps, h=h, hh=hh: e.matmul(ps[:, hh * 256:(hh + 1) * 256], lhsT=kend[:, h * 128:(h + 1) * 128],
                                                        rhs=vtok[:, h * 256:(h + 1) * 256], start=True, stop=True),
                  r=[kend, vtok], w=[ps])
            for hh in range(2):
                h = h2 * 2 + hh
                V(lambda e, ps=ps, h=h, hh=hh: e.scalar_tensor_tensor(out=Sst[:, h * 256:(h + 1) * 256], in0=Sst[:, h * 256:(h + 1) * 256],
                                                                      scalar=cd[:, h:h + 1], in1=ps[:, hh * 256:(hh + 1) * 256],
                                                                      op0=ALU.mult, op1=ALU.add), r=[Sst, cd, ps], w=[Sst])

    def gla_ktok(k):
        ps = nps()
        for kk in range(8):
            P(lambda e, ps=ps, kk=kk: e.matmul(ps[:], lhsT=k.uT[:, kk, :], rhs=k.Win[:, kk, 512:1024], start=(kk == 0), stop=(kk == 7)),
              r=[k.uT, k.Win], w=[ps])
        return ps

    def gla_vtok(k, vtok=None):
        if vtok is None:
            vtok = k.vtok
        for cbk in range(2):
            ps = nps()
            for kk in range(8):
                P(lambda e, ps=ps, kk=kk, cbk=cbk: e.matmul(ps[:], lhsT=k.uT[:, kk, :], rhs=k.Win[:, kk, 1024 + cbk * 512:1024 + (cbk + 1) * 512],
                                                            start=(kk == 0), stop=(kk == 7)), r=[k.uT, k.Win], w=[ps])
            A(lambda e, ps=ps, cbk=cbk, vtok=vtok: e.activation(out=vtok[:, cbk * 512:(cbk + 1) * 512], in_=ps[:], func=AF.Copy), r=[ps], w=[vtok])

    def gla_alloc_work(k):
        k.u = S.tile([128, DM], BF16, "gu")
        k.uT = S.tile([128, 8, 128], BF16, "guT")
        k.lrT = S.tile([16, 128], BF16, "glrT")
        k.lg = [S.tile([128, 512], F32, f"glg{i}") for i in range(2)]
        k.lgh = [S.tile([128, 512], BF16, f"glgh{i}") for i in range(2)]
        k.lgl = [S.tile([128, 512], BF16, f"glgl{i}") for i in range(2)]
        k.lgt = S.tile([128, 512], F32, "glgt")
        k.kex = S.tile([128, 512], F32, "gkex")
        k.kend = S.tile([128, 512], BF16, "gkend")
        k.cd = S.tile([128, 4], F32, "gcd")
        k.vtok = S.tile([128, 1024], BF16, "gvtok")

    def gla_phase_a(Xsrc):
        m0 = S.mark()
        k = gla_common_tiles()
        m1 = S.mark()
        stg = [S.tile([128, 2048], F32, f"stg{i}") for i in range(3)]
        load_w(k.Win, 0, gla_w_in, 0, GLA_IN, 8, stg)
        S.barrier()
        S.release(m1)
        ks = []
        NF = 4
        for i in range(NF):
            kq = K()
            kq.Win, kq.g_bc, kq.gupb, kq.gbias = k.Win, k.g_bc, k.gupb, k.gbias
            gla_alloc_work(kq)
            ks.append(kq)
        xts = [S.tile([128, DM], F32, f"gax{i}") for i in range(NF + 1)]
        Sf = S.tile([128, 1024], F32, "gSf")
        prevbs = [S.tile([128, 1024], BF16, f"gprevb{i}") for i in range(NF + 1)]
        vtoks = [S.tile([128, 1024], BF16, f"gavt{i}") for i in range(NF + 1)]
        kends = [S.tile([128, 512], BF16, f"gake{i}") for i in range(NF + 1)]
        cds = [S.tile([128, 4], F32, f"gacd{i}") for i in range(NF + 1)]
        LK = {"front": 0, "back": False, "next_back": 0, "front_done": set()}

        def body(item, my):
            gc, first, last = item
            i3 = my % (NF + 1)
            kq = ks[my % NF]
            xt, prevb, vtok, kend, cd = xts[i3], prevbs[i3], vtoks[i3], kends[i3], cds[i3]
            if not last:
                LD(xt, xt[:], Xsrc[gc * 128:(gc + 1) * 128, :])
                yield
                rmsnorm(xt, xt[:], kq.g_bc, kq.g_bc[:], kq.u, kq.u[:], DM)
                yield
                yield
                gla_front(kq, xt, False, skip_norm=True)
                yield
                gla_vtok(kq, vtok)
                yield
                gla_state_front(kq, 0, MgtB, lambda: gla_ktok(kq), kend, cd)
                yield
            LK["front"] -= 1
            LK["front_done"].add(my)
            while LK["back"] or LK["next_back"] != my:
                yield
            LK["back"] = True
            if first:
                V(lambda e: e.memset(Sf[:], 0.0), w=[Sf])
            V(lambda e, prevb=prevb: e.tensor_copy(out=prevb[:], in_=Sf[:]), r=[Sf], w=[prevb])
            ST(prevb, g_prev[gc], prevb[:])
            yield
            if not last:
                gla_state_back(kend, vtok, cd, Sf)
            LK["back"] = False
            LK["next_back"] += 1

        items = []
        for (cs, C) in seq_chunks:
            for c in range(C):
                items.append((cs + c, c == 0, c == C - 1))
        active = []
        idx = 0
        while active or idx < len(items):
            if (idx < len(items) and LK["front"] < NF and len(active) < NF + 1
                    and (idx < NF or (idx - NF) in LK["front_done"])):
                LK["front"] += 1
                active.append(body(items[idx], idx))
                idx += 1
            for g in list(active):
                try:
                    next(g)
                except StopIteration:
                    active.remove(g)
        S.barrier()
        S.release(m0)

    def gla_phase_b(Xsrc, Xdst):
        m0 = S.mark()
        k = gla_common_tiles()
        Wo = S.tile([128, 8, DM], BF16, "gWo")
        ng = S.tile([128, 256], F32, "gng")
        bcast_load(ng, gla_norm_g[0:1, :], 256)
        TF = S.tile([128, 128], F32, "TF")
        TB = S.tile([128, 128], F32, "TB")
        V(lambda e: e.tensor_tensor(out=TF[:], in0=Mle[:], in1=Mle[:, 64:65].to_broadcast([128, 128]), op=ALU.subtract), r=[Mle], w=[TF])
        V(lambda e: e.tensor_tensor(out=TB[:], in0=Mge[:], in1=Mge[:, 64:65].to_broadcast([128, 128]), op=ALU.subtract), r=[Mge], w=[TB])
        TFb = S.tile([128, 128], BF16, "TFb")
        TBb = S.tile([128, 128], BF16, "TBb")
        V(lambda e: e.tensor_copy(out=TFb[:], in_=TF[:]), r=[TF], w=[TFb])
        V(lambda e: e.tensor_copy(out=TBb[:], in_=TB[:]), r=[TB], w=[TBb])
        m1 = S.mark()
        stg = [S.tile([128, 2048], F32, f"stg{i}") for i in range(3)]
        load_w(k.Win, 0, gla_w_in, 0, GLA_IN, 8, stg)
        load_w(Wo, 0, gla_w_out, 0, DM, 8, stg)
        S.barrier()
        S.release(m1)
        gla_alloc_work(k)
        xts = [S.tile([128, DM], F32, f"gbx{i}") for i in range(2)]
        prevfs = [S.tile([128, 1024], BF16, f"gprevf{i}") for i in range(2)]
        qgs = [[S.tile([128, 4, 128], BF16, f"gqg{i}{d}") for d in range(2)] for i in range(2)]
        ATs = [[S.tile([128, 4, 128], BF16, f"gAT{i}{d}") for d in range(2)] for i in range(2)]
        vtoks = [S.tile([128, 1024], BF16, f"gvt{i}") for i in range(2)]
        gss = [S.tile([128, 1024], BF16, f"ggs{i}") for i in range(2)]
        kends = [S.tile([128, 512], BF16, f"gke{i}") for i in range(2)]
        cds = [S.tile([128, 4], F32, f"gcd{i}") for i in range(2)]
        xos = [S.tile([128, DM], F32, f"gxo{i}") for i in range(2)]
        qk = S.tile([128, 8, 128], F32, "gqk")
        EX = [S.tile([128, 4, 128], F32, f"gEX{i}") for i in range(2)]
        opT = [S.tile([128, 4, 128], BF16, f"gopT{i}") for i in range(4)]
        Sb = S.tile([128, 1024], F32, "gSb")
        prevb = S.tile([128, 1024], BF16, "gprevb")
        o = S.tile([128, 1024], F32, "go")
        osq = S.tile([128, 1024], F32, "gosq")
        rs = S.tile([128, 8], F32, "grs")
        on = S.tile([128, 1024], BF16, "gon")
        onT = S.tile([128, 8, 128], BF16, "gonT")

        def body(item, L):
            gc, first, last = item
            i2 = gc % 2
            xt, prevf, xo, qg, AT, vtok, gs, kend, cd = xts[i2], prevfs[i2], xos[i2], qgs[i2], ATs[i2], vtoks[i2], gss[i2], kends[i2], cds[i2]
            LD(xt, xt[:], Xsrc[gc * 128:(gc + 1) * 128, :])
            LD(prevf, prevf[:], g_prev[gc])
            yield
            rmsnorm(xt, xt[:], k.g_bc, k.g_bc[:], k.u, k.u[:], DM)
            yield
            yield
            yield
            gla_front(k, xt, True, skip_norm=True)
            yield
            gla_vtok(k, vtok)
            yield
            for cbk in range(2):
                ps = nps()
                for kk in range(8):
                    P(lambda e, ps=ps, kk=kk, cbk=cbk: e.matmul(ps[:], lhsT=k.uT[:, kk, :], rhs=k.Win[:, kk, 2048 + cbk * 512:2048 + (cbk + 1) * 512],
                                                                start=(kk == 0), stop=(kk == 7)), r=[k.uT, k.Win], w=[ps])
                A(lambda e, ps=ps, cbk=cbk, gs=gs: e.activation(out=gs[:, cbk * 512:(cbk + 1) * 512], in_=ps[:], func=AF.Silu), r=[ps], w=[gs])
                yield
            for qkk in range(2):
                ps = nps()
                for h in range(4):
                    m = qkk * 4 + h
                    for kk in range(8):
                        P(lambda e, ps=ps, h=h, m=m, kk=kk: e.matmul(ps[:, h * 128:(h + 1) * 128], lhsT=k.Win[:, kk, m * 128:(m + 1) * 128],
                                                                     rhs=k.uT[:, kk, :], start=(kk == 0), stop=(kk == 7)), r=[k.Win, k.uT], w=[ps])
                A(lambda e, ps=ps, qkk=qkk: e.activation(out=qk[:, qkk * 4:qkk * 4 + 4, :], in_=ps[:].rearrange("p (a b) -> p a b", a=4),
                                                         func=AF.Copy, scale=(128.0 ** -0.5 if qkk == 0 else 1.0)), r=[ps], w=[qk])
                yield
            outs = [opT[0], opT[1], opT[2], opT[3], qg[0], qg[1]]
            specs = [(0, TFb, 1.0, 0, 0), (0, TFb, -1.0, 1, 1), (1, TBb, 1.0, 0, 2), (1, TBb, -1.0, 1, 3),
                     (0, MleB, 1.0, 0, 4), (1, MgeB, 1.0, 0, 5)]
            last_key = None
            pse = None
            for si, (d, Tm, sgn, qi, oi) in enumerate(specs):
                if last_key != (d, id(Tm)):
                    ps = nps()
                    for h in range(4):
                        P(lambda e, ps=ps, h=h, d=d, Tm=Tm: e.matmul(ps[:, h * 128:(h + 1) * 128], lhsT=k.lgh[d][:, h * 128:(h + 1) * 128],
                                                                     rhs=Tm[:], start=True, stop=False), r=[k.lgh[d], Tm], w=[ps])
                        P(lambda e, ps=ps, h=h, d=d, Tm=Tm: e.matmul(ps[:, h * 128:(h + 1) * 128], lhsT=k.lgl[d][:, h * 128:(h + 1) * 128],
                                                                     rhs=Tm[:], start=False, stop=True), r=[k.lgl[d], Tm], w=[ps])
                    last_key = (d, id(Tm))
                    pse = ps
                ex = EX[si % 2]
                A(lambda e, pse=pse, ex=ex, sgn=sgn: e.activation(out=ex[:], in_=pse[:].rearrange("p (a b) -> p a b", a=4), func=AF.Exp, scale=sgn),
                  r=[pse], w=[ex])
                ot = outs[oi]
                V(lambda e, ex=ex, ot=ot, qi=qi: e.tensor_tensor(out=ot[:], in0=qk[:, qi * 4:qi * 4 + 4, :], in1=ex[:], op=ALU.mult),
                  r=[qk, ex], w=[ot])
                yield
            for d in range(2):
                ps = nps()
                for h in range(4):
                    P(lambda e, ps=ps, h=h, d=d: e.matmul(ps[:, h * 128:(h + 1) * 128], lhsT=opT[2 * d + 1][:, h, :], rhs=opT[2 * d][:, h, :],
                                                          start=True, stop=True), r=[opT[2 * d + 1], opT[2 * d]], w=[ps])
                msk = Mle if d == 0 else Mge
                V(lambda e, ps=ps, d=d, msk=msk, AT=AT: e.tensor_tensor(out=AT[d][:], in0=ps[:].rearrange("p (a b) -> p a b", a=4),
                                                                       in1=msk[:].unsqueeze(1).to_broadcast([128, 4, 128]), op=ALU.mult),
                  r=[ps, msk], w=[AT[d]])
                yield
            if not last:
                gla_state_front(k, 1, MltB, lambda: gla_ktok(k), kend, cd)
            yield
            L.release_front()
            while not L.back_free():
                yield
            L.acquire_back()
            if first:
                V(lambda e: e.memset(Sb[:], 0.0), w=[Sb])
            V(lambda e: e.tensor_copy(out=prevb[:], in_=Sb[:]), r=[Sb], w=[prevb])
            for h2 in range(2):
                ps = nps()
                for hh in range(2):
                    h = h2 * 2 + hh
                    oslc = ps[:, hh * 256:(hh + 1) * 256]
                    P(lambda e, oslc=oslc, h=h, AT=AT, vtok=vtok: e.matmul(oslc, lhsT=AT[0][:, h, :], rhs=vtok[:, h * 256:(h + 1) * 256], start=True, stop=False),
                      r=[AT[0], vtok], w=[ps])
                    P(lambda e, oslc=oslc, h=h, AT=AT, vtok=vtok: e.matmul(oslc, lhsT=AT[1][:, h, :], rhs=vtok[:, h * 256:(h + 1) * 256], start=False, stop=False),
                      r=[AT[1], vtok], w=[ps])
                    P(lambda e, oslc=oslc, h=h, prevf=prevf, qg=qg: e.matmul(oslc, lhsT=qg[0][:, h, :], rhs=prevf[:, h * 256:(h + 1) * 256], start=False, stop=False),
                      r=[qg[0], prevf], w=[ps])
                    P(lambda e, oslc=oslc, h=h, qg=qg: e.matmul(oslc, lhsT=qg[1][:, h, :], rhs=prevb[:, h * 256:(h + 1) * 256], start=False, stop=True),
                      r=[qg[1], prevb], w=[ps])
                V(lambda e, ps=ps, h2=h2: e.tensor_copy(out=o[:, h2 * 512:(h2 + 1) * 512], in_=ps[:]), r=[ps], w=[o])
                yield
            if not last:
                gla_state_back(kend, vtok, cd, Sb)
            yield
            G(lambda e: e.tensor_tensor(out=osq[:], in0=o[:], in1=o[:], op=ALU.mult), r=[o], w=[osq])
            V(lambda e: e.tensor_reduce(out=rs[:, 0:4], in_=osq[:].rearrange("p (h x) -> p h x", h=4), axis=AX.X, op=ALU.add), r=[osq], w=[rs])
            V(lambda e: e.tensor_scalar(out=rs[:, 4:8], in0=rs[:, 0:4], scalar1=1.0 / 256, scalar2=EPS, op0=ALU.mult, op1=ALU.add), r=[rs], w=[rs])
            A(lambda e: e.activation(out=rs[:, 4:8], in_=rs[:, 4:8], func=AF.Ln), r=[rs], w=[rs])
            A(lambda e: e.activation(out=rs[:, 4:8], in_=rs[:, 4:8], func=AF.Exp, scale=-0.5), r=[rs], w=[rs])
            V(lambda e: e.tensor_tensor(out=o[:].rearrange("p (h x) -> p h x", h=4), in0=o[:].rearrange("p (h x) -> p h x", h=4),
                                        in1=rs[:, 4:8].unsqueeze(2).to_broadcast([128, 4, 256]), op=ALU.mult), r=[o, rs], w=[o])
            G(lambda e: e.tensor_tensor(out=o[:].rearrange("p (h x) -> p h x", h=4), in0=o[:].rearrange("p (h x) -> p h x", h=4),
                                        in1=ng[:].unsqueeze(1).to_broadcast([128, 4, 256]), op=ALU.mult), r=[o, ng], w=[o])
            V(lambda e, gs=gs: e.tensor_tensor(out=on[:], in0=o[:], in1=gs[:], op=ALU.mult), r=[o, gs], w=[on])
            yield
            yield
            yield
            yield
            transpose_to(on, lambda i: on[:, i * 128:(i + 1) * 128], 8, onT, lambda i0, n: onT[:, i0:i0 + n, :])
            yield
            for cbk in range(2):
                ps = nps()
                for kk in range(8):
                    P(lambda e, ps=ps, kk=kk, cbk=cbk: e.matmul(ps[:], lhsT=onT[:, kk, :], rhs=Wo[:, kk, cbk * 512:(cbk + 1) * 512],
                                                                start=(kk == 0), stop=(kk == 7)), r=[onT, Wo], w=[ps])
                V(lambda e, ps=ps, cbk=cbk, xo=xo, xt=xt: e.tensor_tensor(out=xo[:, cbk * 512:(cbk + 1) * 512], in0=ps[:],
                                                                          in1=xt[:, cbk * 512:(cbk + 1) * 512], op=ALU.add), r=[ps, xt], w=[xo])
                yield
            ST(xo, Xdst[gc * 128:(gc + 1) * 128, :], xo[:])
            L.release_back()

        items = []
        for (cs, C) in seq_chunks:
            for c in range(C - 1, -1, -1):
                items.append((cs + c, c == C - 1, c == 0))
        run_pipeline(items, body)
        S.barrier()
        S.release(m0)

    S.barrier()
    phases = [
        lambda: ssd_phase_a(),
        lambda: ssd_phase_b(X1 if stop_after > 2 else y_out),
        lambda: phase_mlp(0, X1, X2 if stop_after > 3 else y_out, False),
        lambda: gla_phase_a(X2),
        lambda: gla_phase_b(X2, X3 if stop_after > 5 else y_out),
        lambda: phase_mlp(1, X3, y_out, True),
    ]
    for i, ph in enumerate(phases):
        if i + 1 > stop_after:
            break
        ph()
    S.barrier()
    S.emit()
    S.close()
    S.dbg_names = dbg_names
    return nc, S


def shard_inputs(inp, seq_per_core):
    f = lambda a: np.ascontiguousarray(np.asarray(a, dtype=np.float32))
    cw = f(inp["ssd_conv_w"])[0]
    cwl = np.ascontiguousarray(cw.T.reshape(32, 128, 7).transpose(1, 0, 2))
    cbl = np.ascontiguousarray(f(inp["ssd_conv_b"])[0].reshape(32, 128).T)
    hp = np.concatenate([f(inp["ssd_dt_bias_f"])[0], f(inp["ssd_dt_bias_b"])[0], f(inp["ssd_a_log_f"])[0],
                         f(inp["ssd_a_log_b"])[0], f(inp["ssd_d"])[0]])[None, :]
    common = {
        "norm_mix_g": f(inp["norm_mix_g"]), "norm_mlp_g": f(inp["norm_mlp_g"]),
        "norm_final_g": f(inp["norm_final_g"])[None, :],
        "norm_mix_gc": np.ascontiguousarray(f(inp["norm_mix_g"]).reshape(2, 8, 128).transpose(0, 2, 1)),
        "norm_mlp_gc": np.ascontiguousarray(f(inp["norm_mlp_g"]).reshape(2, 8, 128).transpose(0, 2, 1)),
        "ssd_norm_gc": np.ascontiguousarray(f(inp["ssd_norm_g"])[0].reshape(16, 128).T),
        "ssd_w_in": f(inp["ssd_w_in"])[0], "ssd_cw": cwl, "ssd_cb": cbl, "ssd_hp": np.ascontiguousarray(hp),
        "ssd_norm_g": f(inp["ssd_norm_g"]), "ssd_w_out": f(inp["ssd_w_out"])[0],
        "gla_w_in": f(inp["gla_w_in"])[0],
        "gla_gate_up": np.ascontiguousarray(np.concatenate([f(inp["gla_gate_up_f"])[0], f(inp["gla_gate_up_b"])[0]], axis=1)),
        "gla_gate_bias": np.ascontiguousarray(np.concatenate([f(inp["gla_gate_bias_f"])[0], f(inp["gla_gate_bias_b"])[0]])[None, :]),
        "gla_norm_g": f(inp["gla_norm_g"]), "gla_w_out": f(inp["gla_w_out"])[0],
        "mlp_w_up": f(inp["mlp_w_up"]), "mlp_w_down": f(inp["mlp_w_down"]),
    }
    maps = []
    for seqs in seq_per_core:
        m = dict(common)
        m["x_in"] = np.ascontiguousarray(np.concatenate([s.reshape(-1, DM) for s in seqs], axis=0))
        maps.append(m)
    return maps


def kernel(**inp):
    xp = np.asarray(inp["x_prompt"], dtype=np.float32)
    xs = np.asarray(inp["x_sample"], dtype=np.float32)
    n = 8
    seq_per_core = [[xp[2 * c], xp[2 * c + 1], xs[c]] for c in range(n)]
    maps = shard_inputs(inp, seq_per_core)
    nc, _ = build(SEQ_LENS)
    res = run_bass_kernel_spmd(nc, maps, core_ids=list(range(n)))
    yp = np.empty_like(xp)
    ys = np.empty_like(xs)
    for c in range(n):
        y = res.results[c]["y_out"]
        yp[2 * c] = y[0:2048]
        yp[2 * c + 1] = y[2048:4096]
        ys[c] = y[4096:12288]
    return (yp, ys)
```
